# Optimizing a Trainium2 kernel written in Bass

```python
import jax, jax.numpy as jnp
from jax import lax
import numpy as np

D_MODEL = 1024
BATCH = 8
SEQ = 4096
DEPTH = 4

N_MIXERS = 3
N_A = (DEPTH + 2) // 3
N_B = (DEPTH + 1) // 3
N_C = DEPTH // 3

ROPE_THETA = 10000.0
RMS_EPS = 1e-6
NEG_INF = -1e30
ADA_INIT_STD = 0.02

MLA_HEADS = 8
MLA_NOPE = 128
MLA_ROPE = 64
MLA_V = 128
MLA_Q_LORA = 256
MLA_KV_LORA = 128
MLA_BLOCK_Q = 128
MLA_WIDTH = MLA_HEADS * MLA_V
MLA_IN = MLA_Q_LORA + MLA_KV_LORA + MLA_ROPE + MLA_WIDTH

SWA_HEADS = 16
SWA_KV_HEADS = 2
SWA_HEAD_DIM = 64
SWA_WINDOW = 128
SWA_BLOCK_Q = 128
SWA_WIDTH = SWA_HEADS * SWA_HEAD_DIM
SWA_KV_W = SWA_KV_HEADS * SWA_HEAD_DIM
SWA_IN = SWA_WIDTH + 2 * SWA_KV_W + SWA_WIDTH

NSA_HEADS = 16
NSA_KV_HEADS = 4
NSA_HEAD_DIM = 64
NSA_CMP_BLOCK = 32
NSA_CMP_STRIDE = 16
NSA_CMP_HIDDEN = 128
NSA_SEL_BLOCK = 64
NSA_N_SELECT = 16
NSA_WINDOW = 512
NSA_BLOCK_Q = 64
NSA_FORCE_BONUS = 1e4
NSA_WIDTH = NSA_HEADS * NSA_HEAD_DIM
NSA_KV_W = NSA_KV_HEADS * NSA_HEAD_DIM
NSA_IN = NSA_WIDTH + 6 * NSA_KV_W + 3 * NSA_HEADS + NSA_WIDTH

kernel_name = 'hybrid_mla_swa_nsa_interleaved'


def rmsnorm(x, g):
    xf = x.astype(jnp.float32)
    y = xf * lax.rsqrt(jnp.mean(xf * xf, axis=-1, keepdims=True) + RMS_EPS)
    return (y * g.astype(jnp.float32)).astype(x.dtype)


def rope(x, positions):
    half = x.shape[-1] // 2
    inv_freq = ROPE_THETA ** (-jnp.arange(half, dtype=jnp.float32) / half)
    ang = positions.astype(jnp.float32)[..., None] * inv_freq
    cos = jnp.cos(ang)[:, :, None, :]
    sin = jnp.sin(ang)[:, :, None, :]
    xf = x.astype(jnp.float32)
    x1, x2 = xf[..., :half], xf[..., half:]
    return jnp.concatenate([x1 * cos - x2 * sin, x2 * cos + x1 * sin], axis=-1).astype(x.dtype)


def dense_causal_attention(q, k, v, block_q):
    B, S, H, dqk = q.shape
    nb = S // block_q
    scale = dqk ** -0.5
    q_blocks = jnp.moveaxis(q.reshape(B, nb, block_q, H, dqk), 1, 0)
    starts = jnp.arange(nb, dtype=jnp.int32) * block_q
    key_pos = jnp.arange(S, dtype=jnp.int32)

    def one_block(args):
        qb, start = args
        s = jnp.einsum('bqhd,bkhd->bhqk', qb, k, preferred_element_type=jnp.float32) * scale
        q_pos = start + jnp.arange(block_q, dtype=jnp.int32)
        s = jnp.where(q_pos[:, None] >= key_pos[None, :], s, NEG_INF)
        p = jax.nn.softmax(s, axis=-1).astype(v.dtype)
        return jnp.einsum('bhqk,bkhd->bqhd', p, v)

    out = lax.map(one_block, (q_blocks, starts))
    return jnp.moveaxis(out, 0, 1).reshape(B, S, H, v.shape[-1])


def banded_attention(q, k, v, window, block_q, sinks=None):
    B, S, KV, G, d = q.shape
    nb = S // block_q
    span = window + block_q
    scale = d ** -0.5
    pad = ((0, 0), (window, 0), (0, 0), (0, 0))
    k_pad = jnp.pad(k, pad)
    v_pad = jnp.pad(v, pad)
    q_blocks = jnp.moveaxis(q.reshape(B, nb, block_q, KV, G, d), 1, 0)
    starts = jnp.arange(nb, dtype=jnp.int32) * block_q

    def one_block(args):
        qb, start = args
        kb = lax.dynamic_slice_in_dim(k_pad, start, span, axis=1)
        vb = lax.dynamic_slice_in_dim(v_pad, start, span, axis=1)
        s = jnp.einsum('bqkgd,bskd->bkgqs', qb, kb, preferred_element_type=jnp.float32) * scale
        q_pos = start + jnp.arange(block_q, dtype=jnp.int32)
        k_pos = start - window + jnp.arange(span, dtype=jnp.int32)
        diff = q_pos[:, None] - k_pos[None, :]
        mask = (diff >= 0) & (diff < window) & (k_pos[None, :] >= 0)
        s = jnp.where(mask, s, NEG_INF)
        if sinks is not None:
            sink = jnp.broadcast_to(sinks.astype(jnp.float32).reshape(KV, G)[None, :, :, None, None],
                                    s.shape[:-1] + (1,))
            p = jax.nn.softmax(jnp.concatenate([s, sink], axis=-1), axis=-1)[..., :-1]
        else:
            p = jax.nn.softmax(s, axis=-1)
        return jnp.einsum('bkgqs,bskd->bqkgd', p.astype(v.dtype), vb)

    out = lax.map(one_block, (q_blocks, starts))
    return jnp.moveaxis(out, 0, 1).reshape(B, S, KV, G, d)


def mla_mixer(h, positions, w_in, q_norm_g, kv_norm_g, w_q_b, w_kv_b, w_out):
    B, S, _ = h.shape
    o1 = MLA_Q_LORA
    o2 = o1 + MLA_KV_LORA
    o3 = o2 + MLA_ROPE
    q_a, kv_a, k_pe, gate = jnp.split(h @ w_in, [o1, o2, o3], axis=-1)
    q = (rmsnorm(q_a, q_norm_g) @ w_q_b).reshape(B, S, MLA_HEADS, MLA_NOPE + MLA_ROPE)
    kv = (rmsnorm(kv_a, kv_norm_g) @ w_kv_b).reshape(B, S, MLA_HEADS, MLA_NOPE + MLA_V)
    q = jnp.concatenate([q[..., :MLA_NOPE], rope(q[..., MLA_NOPE:], positions)], axis=-1)
    k_pe = rope(k_pe[:, :, None, :], positions)
    k = jnp.concatenate([kv[..., :MLA_NOPE],
                         jnp.broadcast_to(k_pe, (B, S, MLA_HEADS, MLA_ROPE))], axis=-1)
    v = kv[..., MLA_NOPE:]
    o = dense_causal_attention(q, k, v, MLA_BLOCK_Q).reshape(B, S, MLA_WIDTH)
    return (o * jax.nn.silu(gate)) @ w_out


def swa_mixer(h, positions, w_in, sinks, w_out):
    B, S, _ = h.shape
    G = SWA_HEADS // SWA_KV_HEADS
    q, k, v, gate = jnp.split(h @ w_in, [SWA_WIDTH, SWA_WIDTH + SWA_KV_W, SWA_WIDTH + 2 * SWA_KV_W], axis=-1)
    q = rope(q.reshape(B, S, SWA_HEADS, SWA_HEAD_DIM), positions).reshape(B, S, SWA_KV_HEADS, G, SWA_HEAD_DIM)
    k = rope(k.reshape(B, S, SWA_KV_HEADS, SWA_HEAD_DIM), positions)
    v = v.reshape(B, S, SWA_KV_HEADS, SWA_HEAD_DIM)
    o = banded_attention(q, k, v, SWA_WINDOW, SWA_BLOCK_Q, sinks).reshape(B, S, SWA_WIDTH)
    return (o * jax.nn.silu(gate)) @ w_out


def compress_blocks(x, pos_emb, w1, w2):
    B, S, KV, d = x.shape
    r = NSA_CMP_BLOCK // NSA_CMP_STRIDE
    n_chunks = S // NSA_CMP_STRIDE
    nc = n_chunks - r + 1
    chunks = x.reshape(B, n_chunks, NSA_CMP_STRIDE, KV, d)
    blocks = jnp.concatenate([chunks[:, i:i + nc] for i in range(r)], axis=2)
    blocks = blocks + pos_emb[None, None, :, None, :]
    flat = jnp.moveaxis(blocks, 3, 2).reshape(B, nc, KV, NSA_CMP_BLOCK * d)
    return jax.nn.silu(flat @ w1) @ w2


def nsa_mixer(h, positions, w_in, cmp_pos, w_cmp_k1, w_cmp_k2, w_cmp_v1, w_cmp_v2, w_out):
    B, S, _ = h.shape
    KV, G, d = NSA_KV_HEADS, NSA_HEADS // NSA_KV_HEADS, NSA_HEAD_DIM
    sizes = [NSA_WIDTH] + [NSA_KV_W] * 6 + [3 * NSA_HEADS]
    splits = [int(s) for s in np.cumsum(sizes)]
    q, k_cmp, v_cmp, k_slc, v_slc, k_win, v_win, g_branch, gate = jnp.split(h @ w_in, splits, axis=-1)
    kv_shape = (B, S, KV, d)
    q = rope(q.reshape(B, S, NSA_HEADS, d), positions).reshape(B, S, KV, G, d)

    k_c = compress_blocks(k_cmp.reshape(kv_shape), cmp_pos, w_cmp_k1, w_cmp_k2)
    v_c = compress_blocks(v_cmp.reshape(kv_shape), cmp_pos, w_cmp_v1, w_cmp_v2)
    nc = k_c.shape[1]
    ns = S // NSA_SEL_BLOCK
    n_top = min(NSA_N_SELECT, ns)
    k_sel = jnp.moveaxis(rope(k_slc.reshape(kv_shape), positions).reshape(B, ns, NSA_SEL_BLOCK, KV, d), 3, 1)
    v_sel = jnp.moveaxis(v_slc.reshape(B, ns, NSA_SEL_BLOCK, KV, d), 3, 1)

    cmp_start = jnp.arange(nc, dtype=jnp.int32) * NSA_CMP_STRIDE
    sel_start = jnp.arange(ns, dtype=jnp.int32) * NSA_SEL_BLOCK
    overlap = jnp.clip(jnp.minimum(cmp_start[:, None] + NSA_CMP_BLOCK, sel_start[None, :] + NSA_SEL_BLOCK)
                       - jnp.maximum(cmp_start[:, None], sel_start[None, :]), 0, None
                       ).astype(jnp.float32) / NSA_CMP_BLOCK
    cmp_end = cmp_start + NSA_CMP_BLOCK - 1
    scale = d ** -0.5
    nb = S // NSA_BLOCK_Q
    q_blocks = jnp.moveaxis(q.reshape(B, nb, NSA_BLOCK_Q, KV, G, d), 1, 0)
    starts = jnp.arange(nb, dtype=jnp.int32) * NSA_BLOCK_Q
    b_idx = jnp.arange(B)[:, None, None, None]
    h_idx = jnp.arange(KV)[None, :, None, None]
    sel_offsets = jnp.arange(NSA_SEL_BLOCK, dtype=jnp.int32)
    blk = jnp.arange(ns, dtype=jnp.int32)

    def one_block(args):
        qb, start = args
        q_pos = start + jnp.arange(NSA_BLOCK_Q, dtype=jnp.int32)
        s_c = jnp.einsum('bqkgd,bnkd->bkgqn', qb, k_c, preferred_element_type=jnp.float32) * scale
        valid_c = cmp_end[None, :] <= q_pos[:, None]
        p_c = jax.nn.softmax(jnp.where(valid_c, s_c, NEG_INF), axis=-1) * valid_c
        o_c = jnp.einsum('bkgqn,bnkd->bqkgd', p_c.astype(v_c.dtype), v_c)
        imp = jnp.einsum('bkgqn,ns->bkqs', p_c, overlap)
        q_blk = q_pos // NSA_SEL_BLOCK
        causal = blk[None, :] <= q_blk[:, None]
        forced = (blk[None, :] == 0) | (blk[None, :] == q_blk[:, None]) | (blk[None, :] == q_blk[:, None] - 1)
        imp = jnp.where(causal, imp + jnp.where(forced, NSA_FORCE_BONUS, 0.0), -1.0)
        _, top_idx = lax.top_k(imp, n_top)
        m = n_top * NSA_SEL_BLOCK
        k_g = k_sel[b_idx, h_idx, top_idx].reshape(B, KV, NSA_BLOCK_Q, m, d)
        v_g = v_sel[b_idx, h_idx, top_idx].reshape(B, KV, NSA_BLOCK_Q, m, d)
        tok = (top_idx[..., None] * NSA_SEL_BLOCK + sel_offsets).reshape(B, KV, NSA_BLOCK_Q, m)
        valid_s = tok <= q_pos[None, None, :, None]
        s_s = jnp.einsum('bqkgd,bkqmd->bkgqm', qb, k_g, preferred_element_type=jnp.float32) * scale
        p_s = jax.nn.softmax(jnp.where(valid_s[:, :, None], s_s, NEG_INF), axis=-1)
        o_s = jnp.einsum('bkgqm,bkqmd->bqkgd', p_s.astype(v_g.dtype), v_g)
        return o_c, o_s

    o_c, o_s = lax.map(one_block, (q_blocks, starts))
    o_c = jnp.moveaxis(o_c, 0, 1).reshape(B, S, KV, G, d)
    o_s = jnp.moveaxis(o_s, 0, 1).reshape(B, S, KV, G, d)
    o_w = banded_attention(q, rope(k_win.reshape(kv_shape), positions), v_win.reshape(kv_shape),
                           NSA_WINDOW, NSA_BLOCK_Q)
    g = jax.nn.sigmoid(g_branch.astype(jnp.float32)).reshape(B, S, KV, G, 3).astype(h.dtype)
    o = g[..., 0:1] * o_c + g[..., 1:2] * o_s + g[..., 2:3] * o_w
    return (o.reshape(B, S, NSA_WIDTH) * jax.nn.silu(gate)) @ w_out


def setup_inputs(seed: int = 0) -> dict:
    key = jax.random.key(seed)
    ks = iter(jax.random.split(key, 32))

    def dense(shape, fan_in):
        return jax.random.normal(next(ks), shape, jnp.float32) * fan_in ** -0.5

    def gain(shape):
        return 1.0 + 0.02 * jax.random.normal(next(ks), shape, jnp.float32)

    x = jax.random.normal(next(ks), (BATCH, SEQ, D_MODEL), jnp.float32)
    c = jax.random.normal(next(ks), (BATCH, D_MODEL), jnp.float32)
    offsets = jax.random.randint(next(ks), (BATCH, 1), 0, 512, dtype=jnp.int32)
    positions = offsets + jnp.arange(SEQ, dtype=jnp.int32)[None, :]
    return {
        'x': x,
        'c': c,
        'positions': positions,
        'norm_g': gain((DEPTH, D_MODEL)),
        'ada_w': ADA_INIT_STD * jax.random.normal(next(ks), (DEPTH, D_MODEL, 3 * D_MODEL), jnp.float32),
        'ada_b': 0.02 * jax.random.normal(next(ks), (DEPTH, 3 * D_MODEL), jnp.float32),
        'mla_w_in': dense((N_A, D_MODEL, MLA_IN), D_MODEL),
        'mla_q_norm_g': gain((N_A, MLA_Q_LORA)),
        'mla_kv_norm_g': gain((N_A, MLA_KV_LORA)),
        'mla_w_q_b': dense((N_A, MLA_Q_LORA, MLA_HEADS * (MLA_NOPE + MLA_ROPE)), MLA_Q_LORA),
        'mla_w_kv_b': dense((N_A, MLA_KV_LORA, MLA_HEADS * (MLA_NOPE + MLA_V)), MLA_KV_LORA),
        'mla_w_out': dense((N_A, MLA_WIDTH, D_MODEL), MLA_WIDTH),
        'swa_w_in': dense((N_B, D_MODEL, SWA_IN), D_MODEL),
        'swa_sinks': jax.random.normal(next(ks), (N_B, SWA_HEADS), jnp.float32),
        'swa_w_out': dense((N_B, SWA_WIDTH, D_MODEL), SWA_WIDTH),
        'nsa_w_in': dense((N_C, D_MODEL, NSA_IN), D_MODEL),
        'nsa_cmp_pos': 0.1 * jax.random.normal(next(ks), (N_C, NSA_CMP_BLOCK, NSA_HEAD_DIM), jnp.float32),
        'nsa_w_cmp_k1': dense((N_C, NSA_CMP_BLOCK * NSA_HEAD_DIM, NSA_CMP_HIDDEN), NSA_CMP_BLOCK * NSA_HEAD_DIM),
        'nsa_w_cmp_k2': dense((N_C, NSA_CMP_HIDDEN, NSA_HEAD_DIM), NSA_CMP_HIDDEN),
        'nsa_w_cmp_v1': dense((N_C, NSA_CMP_BLOCK * NSA_HEAD_DIM, NSA_CMP_HIDDEN), NSA_CMP_BLOCK * NSA_HEAD_DIM),
        'nsa_w_cmp_v2': dense((N_C, NSA_CMP_HIDDEN, NSA_HEAD_DIM), NSA_CMP_HIDDEN),
        'nsa_w_out': dense((N_C, NSA_WIDTH, D_MODEL), NSA_WIDTH),
        'final_norm_g': gain((D_MODEL,)),
    }


def reference(x, c, positions, norm_g, ada_w, ada_b,
              mla_w_in, mla_q_norm_g, mla_kv_norm_g, mla_w_q_b, mla_w_kv_b, mla_w_out,
              swa_w_in, swa_sinks, swa_w_out,
              nsa_w_in, nsa_cmp_pos, nsa_w_cmp_k1, nsa_w_cmp_k2, nsa_w_cmp_v1, nsa_w_cmp_v2, nsa_w_out,
              final_norm_g):
    cond = jax.nn.silu(c)
    for i in range(DEPTH):
        shift, scale, gate = jnp.split(cond @ ada_w[i] + ada_b[i], 3, axis=-1)
        h = rmsnorm(x, norm_g[i]) * (1 + scale[:, None, :]) + shift[:, None, :]
        kind, j = i % N_MIXERS, i // N_MIXERS
        if kind == 0:
            y = mla_mixer(h, positions, mla_w_in[j], mla_q_norm_g[j], mla_kv_norm_g[j],
                          mla_w_q_b[j], mla_w_kv_b[j], mla_w_out[j])
        elif kind == 1:
            y = swa_mixer(h, positions, swa_w_in[j], swa_sinks[j], swa_w_out[j])
        else:
            y = nsa_mixer(h, positions, nsa_w_in[j], nsa_cmp_pos[j], nsa_w_cmp_k1[j], nsa_w_cmp_k2[j],
                          nsa_w_cmp_v1[j], nsa_w_cmp_v2[j], nsa_w_out[j])
        x = x + gate[:, None, :] * y
    return rmsnorm(x, final_norm_g)
```

```python
import numpy as np
import ml_dtypes
import concourse.bass as bass
import concourse.mybir as mybir
from concourse.bass_utils import run_bass_kernel_spmd

F32 = mybir.dt.float32
BF16 = mybir.dt.bfloat16
I32 = mybir.dt.int32
U8 = mybir.dt.uint8
AF = mybir.ActivationFunctionType
ALU = mybir.AluOpType

S = 4096
D = 1024
NT = S // 128
NST = S // 512
DEPTH = 4
NEG = -30000.0
EPS = 1e-6


class Buf:
    __slots__ = ("w", "r")

    def __init__(self):
        self.w = {}
        self.r = {}


class Prog:
    CE = ("pe", "act", "dve", "pool")
    ALLQ = ("pe", "act", "dve", "pool", "sp")

    def __init__(self, nc, n_dma_sems=32):
        self.nc = nc
        self.ops = {e: [] for e in self.ALLQ}
        self.count = {e: 0 for e in self.CE}
        self.sems = {}
        self.known = {e: {} for e in self.ALLQ}
        self.n_dma = n_dma_sems
        self.dma_cnt = [0] * n_dma_sems
        self.dma_rr = 0
        self.dma_rr_pool = 0
        self._stack = []
        for e in self.CE:
            self.sems[e] = self._sem("s_" + e)
        for i in range(n_dma_sems):
            self.sems[("dma", i)] = self._sem("s_dma%d" % i)
        self.n_ops = 0

    def _sem(self, name):
        cm = self.nc.semaphore(name)
        h = cm.__enter__()
        self._stack.append(cm)
        return h

    def _gather(self, eng, R, W, WP):
        need = {}
        for b in R:
            for k, v in b.w.items():
                if need.get(k, 0) < v:
                    need[k] = v
        for b in list(W) + list(WP):
            for k, v in b.w.items():
                if need.get(k, 0) < v:
                    need[k] = v
            for k, v in b.r.items():
                if need.get(k, 0) < v:
                    need[k] = v
        out = []
        kn = self.known[eng]
        for k, v in need.items():
            if eng == "pe" and k == "pe":
                continue
            if kn.get(k, 0) >= v:
                continue
            kn[k] = v
            out.append((k, v))
        return out

    def _record(self, tok, R, W, WP):
        k, v = tok
        for b in W:
            b.w = {k: v}
            b.r = {}
        for b in WP:
            if b.w.get(k, 0) < v:
                b.w[k] = v
        for b in R:
            if b.r.get(k, 0) < v:
                b.r[k] = v

    def op(self, eng, fn, R=(), W=(), WP=(), inc=True):
        waits = self._gather(eng, R, W, WP)
        if inc:
            self.count[eng] += 1
            ms = self.count[eng]
        else:
            ms = self.count[eng] + 1
        self.ops[eng].append((fn, waits, (eng, 1) if inc else None))
        self._record((eng, ms), R, W, WP)
        self.n_ops += 1

    def dma(self, q, out, in_, R=(), W=(), WP=()):
        half = self.n_dma // 2
        if q == "pool":
            s = half + self.dma_rr_pool
            self.dma_rr_pool = (self.dma_rr_pool + 1) % half
        else:
            s = self.dma_rr
            self.dma_rr = (self.dma_rr + 1) % half
        key = ("dma", s)
        waits = self._gather(q, R, W, WP)
        prev = 16 * self.dma_cnt[s]
        if prev > 0 and self.known[q].get(key, 0) < prev:
            self.known[q][key] = prev
            waits.append((key, prev))
        self.dma_cnt[s] += 1
        val = 16 * self.dma_cnt[s]

        def fn(e, out=out, in_=in_):
            return e.dma_start(out=out, in_=in_)
        self.ops[q].append((fn, waits, (key, 16)))
        self._record((key, val), R, W, WP)
        self.n_ops += 1

    def barrier(self):
        for e in self.ALLQ:
            waits = []
            kn = self.known[e]
            for c in self.CE:
                if c == e:
                    continue
                v = self.count[c]
                if v > 0 and kn.get(c, 0) < v:
                    kn[c] = v
                    waits.append((c, v))
            for i in range(self.n_dma):
                v = 16 * self.dma_cnt[i]
                k = ("dma", i)
                if v > 0 and kn.get(k, 0) < v:
                    kn[k] = v
                    waits.append((k, v))
            if waits:
                self.ops[e].append((None, waits, None))

    def emit(self):
        nc = self.nc
        sems = self.sems
        ops = self.ops

        def run(e, lst):
            for fn, waits, inc in lst:
                for k, v in waits:
                    e.wait_ge(sems[k], v)
                if fn is None:
                    continue
                ins = fn(e)
                if inc is not None:
                    ins.then_inc(sems[inc[0]], inc[1])

        with nc.Block() as blk:
            @blk.sync
            def _(e):
                run(e, ops["sp"])

            @blk.tensor
            def _(e):
                run(e, ops["pe"])

            @blk.scalar
            def _(e):
                run(e, ops["act"])

            @blk.vector
            def _(e):
                run(e, ops["dve"])

            @blk.gpsimd
            def _(e):
                run(e, ops["pool"])


class Arena:
    def __init__(self, nc, nbytes):
        self.t = nc.alloc_sbuf_tensor("arena", [128, nbytes], U8)
        self.n = nbytes
        self.off = 0

    def reset(self, off=0):
        self.off = off

    def alloc(self, free_shape, dtype):
        n = int(np.prod(free_shape))
        nb = n * mybir.dt.size(dtype)
        nb = (nb + 63) // 64 * 64
        assert self.off + nb <= self.n, ("arena overflow", self.off, nb, self.n)
        ap = self.t[:, self.off:self.off + nb].bitcast(dtype)[:, 0:n]
        self.off += nb
        if len(free_shape) == 2:
            ap = ap.rearrange("p (a b) -> p a b", a=free_shape[0])
        elif len(free_shape) == 3:
            ap = ap.rearrange("p (a b c) -> p a b c", a=free_shape[0], b=free_shape[1])
        return ap


def _mask(kk, W):
    kp = np.arange(128)[:, None]
    qf = np.arange(512)[None, :]
    dlt = qf - 128 * kk - kp
    ok = dlt >= 0
    if W is not None:
        ok &= dlt < W
    return np.where(ok, 0.0, NEG).astype(np.float32)


MASK_CAUSAL = {kk: kk for kk in range(4)}
MASK_SWA = {kk: 4 + (kk + 1) for kk in range(-1, 4)}
MASK_WIN = {kk: 9 + (kk + 4) for kk in range(-4, 0)}
for _kk in range(4):
    MASK_WIN[_kk] = MASK_CAUSAL[_kk]
NMASK = 13


def _consts():
    bf = ml_dtypes.bfloat16
    masks = np.zeros((128, NMASK, 512), np.float32)
    for kk in range(4):
        masks[:, MASK_CAUSAL[kk]] = _mask(kk, None)
    for kk in range(-1, 4):
        masks[:, MASK_SWA[kk]] = _mask(kk, 128)
    for kk in range(-4, 0):
        masks[:, MASK_WIN[kk]] = _mask(kk, 512)
    n = np.arange(256)[:, None]
    q = np.arange(S)[None, :]
    cm = np.where(16 * n + 31 <= q, 0.0, NEG).astype(np.float32)
    cmask = np.concatenate([cm[0:128, 0:2560], cm[128:256, 2048:4096]], axis=1)
    ov = np.zeros((256, 64), np.float32)
    for nn in range(255):
        a0, a1 = 16 * nn, 16 * nn + 32
        for s in range(64):
            o = min(a1, 64 * s + 64) - max(a0, 64 * s)
            if o > 0:
                ov[nn, s] = o / 32.0
    ovl = np.stack([ov[0:128], ov[128:256]], axis=1)
    ebig = np.zeros((128, S), np.float32)
    ebig[64 + (np.arange(S) // 64), np.arange(S)] = 1.0
    qq = np.arange(S)
    qb = qq // 64
    s = np.arange(64)[None, :]
    causal = s <= qb[:, None]
    forced = (s == 0) | (s == qb[:, None]) | (s == qb[:, None] - 1)
    cmul = causal.astype(np.float32)
    cadd = np.where(causal, np.where(forced, 1e4, 0.0), -1.0).astype(np.float32)
    cmul = cmul.reshape(32, 128, 64).transpose(1, 0, 2)
    cadd = cadd.reshape(32, 128, 64).transpose(1, 0, 2)
    sel3 = np.zeros((128, 48, 64), np.float32)
    for r in range(48):
        sel3[r, r, :] = 1.0
    half = 32
    inv = (10000.0 ** (-np.arange(half, dtype=np.float32) / half)).astype(np.float32)
    invf = np.tile(inv, 4)[:, None].astype(np.float32)
    sgn = np.where((np.arange(128) % 64) < 32, -1.0, 1.0).astype(np.float32)[:, None]
    return {
        "c_ident": np.eye(128, dtype=np.float32),
        "c_masks": masks.astype(bf),
        "c_cmask": cmask.astype(bf),
        "c_ovl": ovl.astype(bf),
        "c_ebig": ebig.astype(bf),
        "c_cmul": np.ascontiguousarray(cmul).astype(bf),
        "c_cadd": np.ascontiguousarray(cadd).astype(bf),
        "c_sel3": sel3.astype(bf),
        "c_selst": np.where((np.arange(128)[:, None] - 64) <= (np.arange(1024)[None, :] // 64), 0.0, NEG).astype(np.float32).astype(bf),
        "c_invf": invf,
        "c_sgn": sgn,
    }


def _swap64(w):
    k, n = w.shape
    return np.ascontiguousarray(w.reshape(k, n // 64, 2, 32)[:, :, ::-1, :].reshape(k, n))


def _fm(v, nchunk):
    return np.ascontiguousarray(np.asarray(v, np.float32).reshape(nchunk, 128).T)


MLA_COLS = dict(qa=0, kva=256, kpe=384, kpes=448, gate=512, n=1536)
SWA_COLS = dict(q=0, qs=1024, k=2048, ks=2176, v=2304, gate=2432, n=3456)
NSA_COLS = dict(q=0, qs=1024, kslc=2048, kslcs=2304, kwin=2560, kwins=2816, kcmp=3072, vcmp=3328,
                vslc=3584, vwin=3840, gbr=4096, gate=4144, n=5168)


def _prep_weights(inp):
    out = {}
    for j in range(2):
        w = inp["mla_w_in"][j]
        kpe = w[:, 384:448]
        out["mla%d_wa" % j] = np.ascontiguousarray(np.concatenate(
            [w[:, 0:256], w[:, 256:384], kpe, _swap64(kpe), w[:, 448:1472]], axis=1))
        qb = inp["mla_w_q_b"][j].reshape(256, 8, 192)
        out["mla%d_wqn" % j] = np.ascontiguousarray(qb[:, :, 0:128].reshape(256, 1024))
        qr = np.ascontiguousarray(qb[:, :, 128:192].reshape(256, 512))
        out["mla%d_wqr" % j] = qr
        out["mla%d_wqrs" % j] = _swap64(qr)
        kvb = inp["mla_w_kv_b"][j].reshape(128, 8, 256)
        out["mla%d_wkn" % j] = np.ascontiguousarray(kvb[:, :, 0:128].reshape(128, 1024))
        out["mla%d_wv" % j] = np.ascontiguousarray(kvb[:, :, 128:256].reshape(128, 1024))
        out["mla%d_wout" % j] = np.ascontiguousarray(inp["mla_w_out"][j])
        out["mla%d_qg" % j] = _fm(inp["mla_q_norm_g"][j], 2)
        out["mla%d_kvg" % j] = _fm(inp["mla_kv_norm_g"][j], 1)
    w = inp["swa_w_in"][0]
    q, k, v, g = w[:, 0:1024], w[:, 1024:1152], w[:, 1152:1280], w[:, 1280:2304]
    out["swa_wa"] = np.ascontiguousarray(np.concatenate([q, _swap64(q), k, _swap64(k), v, g], axis=1))
    out["swa_wout"] = np.ascontiguousarray(inp["swa_w_out"][0])
    out["swa_sinks"] = np.ascontiguousarray(inp["swa_sinks"][0].reshape(1, 16).astype(np.float32))
    w = inp["nsa_w_in"][0]
    q = w[:, 0:1024]
    kcmp, vcmp, kslc, vslc, kwin, vwin = [w[:, 1024 + 256 * i:1280 + 256 * i] for i in range(6)]
    gbr = w[:, 2560:2608]
    g = w[:, 2608:3632]
    out["nsa_wa"] = np.ascontiguousarray(np.concatenate(
        [q, _swap64(q), kslc, _swap64(kslc), kwin, _swap64(kwin), kcmp, vcmp, vslc, vwin, gbr, g], axis=1))
    out["nsa_wout"] = np.ascontiguousarray(inp["nsa_w_out"][0])
    out["nsa_pe"] = np.ascontiguousarray(inp["nsa_cmp_pos"][0].reshape(16, 128).T.astype(np.float32))
    out["nsa_wk1"] = np.ascontiguousarray(inp["nsa_w_cmp_k1"][0])
    out["nsa_wk2"] = np.ascontiguousarray(inp["nsa_w_cmp_k2"][0])
    out["nsa_wv1"] = np.ascontiguousarray(inp["nsa_w_cmp_v1"][0])
    out["nsa_wv2"] = np.ascontiguousarray(inp["nsa_w_cmp_v2"][0])
    out["ada_w"] = np.ascontiguousarray(inp["ada_w"])
    out["ada_b"] = np.ascontiguousarray(inp["ada_b"].reshape(4, 24, 128).transpose(2, 0, 1))
    out["norm_g"] = np.ascontiguousarray(inp["norm_g"].reshape(4, 8, 128).transpose(2, 0, 1))
    out["final_g"] = np.ascontiguousarray(inp["final_norm_g"].reshape(1, 1024))
    return out


class K:
    pass


def build(n_layers=DEPTH, shapes=None):
    nc = bass.Bass("TRN2", target_bir_lowering=False)
    P = Prog(nc)
    k = K()
    k.nc, k.P = nc, P
    dram_in = {}

    def din(name, shape, dt):
        dram_in[name] = nc.dram_tensor(name, list(shape), dt, kind="ExternalInput").ap()
        return dram_in[name]

    for name, (shape, dt) in shapes.items():
        din(name, shape, dt)
    k.din = dram_in
    out = nc.dram_tensor("out", [S, D], F32, kind="ExternalOutput").ap()
    k.out = out
    k.xr = nc.dram_tensor("xr", [S, D], F32).ap()
    k.QT = nc.dram_tensor("QT", [1536, S], BF16).ap()
    k.KT = nc.dram_tensor("KT", [1280, S], BF16).ap()
    k.VT = nc.dram_tensor("VT", [8, 128, NT, 128], BF16).ap()
    k.VS = nc.dram_tensor("VS", [8, 128, NT, 64], BF16).ap()
    k.GT = nc.dram_tensor("GT", [1024 + 128, S], BF16).ap()
    k.OT = nc.dram_tensor("OT", [1024, S], BF16).ap()
    k.OC = nc.dram_tensor("OC", [1024, S], F32).ap()
    k.modrow = nc.dram_tensor("modrow", [4, 3072], F32).ap()
    k.xr_b = [Buf() for _ in range(NT)]
    k.QT_b, k.KT_b, k.VT_b, k.GT_b, k.OT_b, k.OC_b, k.modrow_b, k.out_b, k.VS_b = [Buf() for _ in range(9)]

    sb = nc.alloc_sbuf_tensor
    k.ident = sb("ident", [128, 128], F32); k.ident_b = Buf()
    k.identb = sb("identb", [128, 128], BF16); k.identb_b = Buf()
    k.onesb = sb("onesb", [128, 128], BF16); k.onesb_b = Buf()
    k.onesf = sb("onesf", [128, 128], F32); k.onesf_b = Buf()
    k.masks = sb("masks", [128, NMASK, 512], BF16); k.masks_b = Buf()
    k.ropeD = nc.dram_tensor("ropeD", [2, 128, S], F32).ap()
    k.rope_b = Buf()
    k.mod = sb("mod", [128, 4, 24], F32); k.mod_b = Buf()
    k.gmod = sb("gmod", [128, 4, 8], F32); k.gmod_b = Buf()
    k.ps = [nc.alloc_psum_tensor("ps%d" % i, [128, 512], F32) for i in range(8)]
    k.ps_b = [Buf() for _ in range(8)]
    k.arena = Arena(nc, 160 * 1024)

    prologue(k)
    import os
    kinds = os.environ.get("K_KINDS", "mla,swa,nsa,mla").split(",")
    for i in range(n_layers):
        kind = kinds[i]
        j = i // 3
        last = (i == n_layers - 1)
        if kind == "mla":
            mla_layer(k, i, j, last)
        elif kind == "swa":
            swa_layer(k, i, last)
        else:
            nsa_layer(k, i, last)
    P.barrier()
    P.emit()
    return nc, P


def prologue(k):
    P, nc, A = k.P, k.nc, k.arena
    din = k.din
    A.reset()
    P.dma("sp", k.ident[:], din["c_ident"], W=[k.ident_b])
    P.dma("sp", k.masks[:], din["c_masks"], W=[k.masks_b])
    P.op("pool", lambda e: e.tensor_copy(out=k.identb[:], in_=k.ident[:]), R=[k.ident_b], W=[k.identb_b])
    P.op("pool", lambda e: e.memset(k.onesb[:], 1.0), W=[k.onesb_b])
    P.op("pool", lambda e: e.memset(k.onesf[:], 1.0), W=[k.onesf_b])
    posi = A.alloc([S], I32); posi_b = Buf()
    ang = A.alloc([S], F32); ang_b = Buf()
    kf = A.alloc([S], F32); kf_b = Buf()
    ki = A.alloc([S], I32); ki_b = Buf()
    rr = A.alloc([S], F32); rr_b = Buf()
    ivf = A.alloc([1], F32); ivf_b = Buf()
    sgn = A.alloc([1], F32); sgn_b = Buf()
    P.dma("sp", posi, din["pos"].partition_broadcast(128), W=[posi_b])
    P.dma("sp", ivf, din["c_invf"], W=[ivf_b])
    P.dma("sp", sgn, din["c_sgn"], W=[sgn_b])
    P.op("dve", lambda e: e.tensor_copy(out=kf, in_=posi), R=[posi_b], W=[kf_b])
    P.op("dve", lambda e: e.tensor_scalar(out=ang, in0=kf, scalar1=ivf[:, 0:1], scalar2=None, op0=ALU.mult),
         R=[kf_b, ivf_b], W=[ang_b])
    TWO_PI = 2 * np.pi
    c1 = float(np.float32(6.28125))
    c2 = float(TWO_PI - 6.28125)
    for dst, shift, use_sgn in ((0, np.pi / 2, False), (1, 0.0, True)):
        P.op("dve", lambda e, shift=shift: e.tensor_scalar(out=kf, in0=ang, scalar1=float(shift), scalar2=float(1.0 / TWO_PI),
                                                          op0=ALU.add, op1=ALU.mult), R=[ang_b], W=[kf_b])
        P.op("dve", lambda e: e.tensor_copy(out=ki, in_=kf), R=[kf_b], W=[ki_b])
        P.op("dve", lambda e: e.tensor_copy(out=kf, in_=ki), R=[ki_b], W=[kf_b])
        P.op("dve", lambda e: e.scalar_tensor_tensor(out=rr, in0=kf, scalar=-c1, in1=ang, op0=ALU.mult, op1=ALU.add),
             R=[kf_b, ang_b], W=[rr_b])
        P.op("dve", lambda e: e.scalar_tensor_tensor(out=rr, in0=kf, scalar=-c2, in1=rr, op0=ALU.mult, op1=ALU.add),
             R=[kf_b, rr_b], W=[rr_b])
        P.op("dve", lambda e, shift=shift: e.tensor_scalar(out=kf, in0=rr, scalar1=float(shift), scalar2=float(np.pi),
                                                          op0=ALU.add, op1=ALU.is_gt), R=[rr_b], W=[kf_b])
        P.op("dve", lambda e, shift=shift: e.tensor_scalar(out=rr, in0=rr, scalar1=float(shift), scalar2=None, op0=ALU.add),
             R=[rr_b], W=[rr_b])
        P.op("dve", lambda e: e.scalar_tensor_tensor(out=rr, in0=kf, scalar=-TWO_PI, in1=rr, op0=ALU.mult, op1=ALU.add),
             R=[kf_b, rr_b], W=[rr_b])
        P.op("dve", lambda e: e.tensor_scalar(out=kf, in0=rr, scalar1=float(-np.pi), scalar2=None, op0=ALU.is_lt),
             R=[rr_b], W=[kf_b])
        P.op("dve", lambda e: e.scalar_tensor_tensor(out=rr, in0=kf, scalar=TWO_PI, in1=rr, op0=ALU.mult, op1=ALU.add),
             R=[kf_b, rr_b], W=[rr_b])
        P.op("dve", lambda e: e.tensor_scalar(out=rr, in0=rr, scalar1=3.141592, scalar2=-3.141592, op0=ALU.min, op1=ALU.max),
             R=[rr_b], W=[rr_b])
        if use_sgn:
            P.op("act", lambda e: e.activation(out=rr, in_=rr, func=AF.Sin, scale=sgn[:, 0:1]),
                 R=[rr_b, sgn_b], W=[rr_b])
        else:
            P.op("act", lambda e: e.activation(out=rr, in_=rr, func=AF.Sin), R=[rr_b], W=[rr_b])
        P.dma("sp", k.ropeD[dst], rr, R=[rr_b], WP=[k.rope_b])
    P.barrier()
    A.reset()
    cfm = A.alloc([8], F32); cfm_b = Buf()
    cond = A.alloc([8], F32); cond_b = Buf()
    adab = A.alloc([4, 24], F32); adab_b = Buf()
    ng = A.alloc([4, 8], F32); ng_b = Buf()
    P.dma("sp", cfm, din["c_fm"], W=[cfm_b])
    P.dma("sp", adab, din["ada_b"], W=[adab_b])
    P.dma("sp", ng, din["norm_g"], W=[ng_b])
    P.op("act", lambda e: e.activation(out=cond, in_=cfm, func=AF.Silu), R=[cfm_b], W=[cond_b])
    wst = [A.alloc([8, 512], F32) for _ in range(2)]
    wst_b = [Buf() for _ in range(2)]
    modT = A.alloc([4, 128], F32); modT_b = Buf()
    n = 0
    for i in range(DEPTH):
        pm = k.ps[i % 2]
        pm_b = k.ps_b[i % 2]
        for mg in range(6):
            b = n % 2
            n += 1
            P.dma("sp", wst[b], din["ada_w"][i, :, mg * 512:(mg + 1) * 512].rearrange("(c p) n -> p c n", p=128), W=[wst_b[b]])
            for m4 in range(4):
                m = mg * 4 + m4
                for c in range(8):
                    P.op("pe", lambda e, b=b, m4=m4, m=m, c=c, pm=pm: e.matmul(
                        pm[:, m:m + 1], lhsT=wst[b][:, c, m4 * 128:(m4 + 1) * 128], rhs=cond[:, c:c + 1],
                        start=(c == 0), stop=(c == 7)),
                        R=[wst_b[b], cond_b], WP=[pm_b], inc=(c == 7))
        P.op("dve", lambda e, i=i, pm=pm: e.tensor_tensor(out=k.mod[:, i, :], in0=pm[:, 0:24], in1=adab[:, i, :], op=ALU.add),
             R=[pm_b, adab_b], WP=[k.mod_b])
        P.op("dve", lambda e, i=i: e.scalar_tensor_tensor(out=k.gmod[:, i, :], in0=k.mod[:, i, 8:16], scalar=1.0, in1=ng[:, i, :],
                                                         op0=ALU.add, op1=ALU.mult), R=[k.mod_b, ng_b], WP=[k.gmod_b])
        pT, pT_b = k.ps[2 + i % 2], k.ps_b[2 + i % 2]
        P.op("pe", lambda e, i=i, pT=pT: e.transpose(out=pT[0:24, 0:128], in_=k.mod[:, i, :], identity=k.ident[:]),
             R=[k.mod_b, k.ident_b], W=[pT_b])
        P.op("act", lambda e, i=i, pT=pT: e.activation(out=modT[0:24, i, :], in_=pT[0:24, 0:128], func=AF.Copy), R=[pT_b], WP=[modT_b])
        P.dma("sp", k.modrow[i].rearrange("(c p) -> c p", p=128), modT[0:24, i, :], R=[modT_b], WP=[k.modrow_b])
    P.barrier()


def load_cast_weights(k, dst, src, ncols, kchunks, stg, stg_b, dst_b, cw=256):
    P = k.P
    n = 0
    for c0 in range(0, ncols, cw):
        w = min(cw, ncols - c0)
        b = k.stg_n % 2
        k.stg_n += 1
        P.dma("sp", stg[b][:, 0:kchunks, 0:w], src[:, c0:c0 + w].rearrange("(c p) n -> p c n", p=128), W=[stg_b[b]])
        if n % 2 == 0:
            P.op("act", lambda e, b=b, c0=c0, w=w: e.activation(out=dst[:, :, c0:c0 + w], in_=stg[b][:, 0:kchunks, 0:w], func=AF.Copy),
                 R=[stg_b[b]], WP=[dst_b])
        else:
            P.op("dve", lambda e, b=b, c0=c0, w=w: e.tensor_copy(out=dst[:, :, c0:c0 + w], in_=stg[b][:, 0:kchunks, 0:w]),
                 R=[stg_b[b]], WP=[dst_b])
        n += 1


class Group:
    def __init__(self, k, a, dst_rows, dst_b, nch, rows_last=128):
        i = a.grp_n % 2
        a.grp_n += 1
        self.k, self.buf, self.b = k, a.gst[i], a.gst_b[i]
        self.dst_rows, self.dst_b, self.nch, self.rows_last = dst_rows, dst_b, nch, rows_last

    def slot(self, j):
        return self.buf[:, j, :]

    def flush(self):
        P = self.k.P
        if self.rows_last == 128:
            P.dma("sp", self.dst_rows.rearrange("(c p) t -> p c t", p=128), self.buf[:, 0:self.nch, :], R=[self.b], WP=[self.dst_b])
        else:
            assert self.nch == 1
            P.dma("sp", self.dst_rows, self.buf[0:self.rows_last, 0, :], R=[self.b], WP=[self.dst_b])


def phase_a_common(k, li, vshape):
    A, P = k.arena, k.P
    a = K()
    a.xt4s = [A.alloc([4, D], F32) for _ in range(2)]
    a.xt4_bss = [[Buf(), Buf()], [Buf(), Buf()]]
    xflat = a.xt4s[1].rearrange("p s d -> p (s d)")
    a.stg = [xflat[:, i * 2048:(i + 1) * 2048].rearrange("p (c n) -> p c n", c=8) for i in range(2)]
    a.stg_b = a.xt4_bss[1]
    a.junk = A.alloc([D], F32); a.junk_b = Buf()
    a.ss = A.alloc([16], F32); a.ss_b = Buf()
    a.hT = [A.alloc([8, 512], BF16) for _ in range(2)]
    a.hT_b = [Buf() for _ in range(2)]
    a.gst = [A.alloc([4, 512], BF16) for _ in range(2)]
    a.gst_b = [Buf() for _ in range(2)]
    a.grp_n = 0
    a.vst = A.alloc([vshape[0], 4, vshape[1]], BF16); a.vst_b = Buf()
    a.rt = [A.alloc([512], F32) for _ in range(2)]
    a.rt_b = [Buf() for _ in range(2)]
    a.rcs = [A.alloc([2, 512], F32) for _ in range(2)]
    a.rcs_b = [Buf() for _ in range(2)]
    k.stg_n = 0
    a.mm_n = 0
    a.tr_n = 0
    return a


def hT_load(k, a, st, xsrc, xsrc_b):
    P = k.P
    hb = st % 2
    P.dma("sp", a.xt4s[hb], xsrc[st * 512:(st + 1) * 512, :].rearrange("(s p) d -> p s d", p=128),
          R=[xsrc_b[st * 4 + s_] for s_ in range(4)], W=list(a.xt4_bss[hb]))


def hT_front(k, a, st):
    P = k.P
    hb = st % 2
    xt4, xt4_bs, ss, ss_b = a.xt4s[hb], a.xt4_bss[hb], a.ss, a.ss_b
    for sub in range(4):
        P.op("act", lambda e, sub=sub: e.activation(out=a.junk, in_=xt4[:, sub, :], func=AF.Square, accum_out=ss[:, sub:sub + 1]),
             R=list(xt4_bs), W=[a.junk_b], WP=[ss_b])
    P.op("dve", lambda e: e.tensor_scalar(out=ss[:, 4:8], in0=ss[:, 0:4], scalar1=1.0 / D, scalar2=EPS, op0=ALU.mult, op1=ALU.add),
         R=[ss_b], WP=[ss_b])
    P.op("act", lambda e: e.activation(out=ss[:, 8:12], in_=ss[:, 4:8], func=AF.Sqrt), R=[ss_b], WP=[ss_b])
    P.op("dve", lambda e: e.reciprocal(out=ss[:, 12:16], in_=ss[:, 8:12]), R=[ss_b], WP=[ss_b])
    for sub in range(4):
        P.op("dve", lambda e, sub=sub: e.tensor_scalar(out=xt4[:, sub, :], in0=xt4[:, sub, :], scalar1=ss[:, 12 + sub:13 + sub], scalar2=None, op0=ALU.mult),
             R=[ss_b], WP=list(xt4_bs))


def hT_back(k, a, li, st):
    P = k.P
    hb = st % 2
    hT, hT_b = a.hT[hb], a.hT_b[hb]
    xt4, xt4_bs = a.xt4s[hb], a.xt4_bss[hb]
    for sub in range(4):
        for half in range(2):
            pi = 4 + a.tr_n % 4
            a.tr_n += 1
            pt, pt_b = k.ps[pi], k.ps_b[pi]
            for c4 in range(4):
                c = half * 4 + c4
                P.op("pe", lambda e, sub=sub, c=c, c4=c4, pt=pt: e.transpose(out=pt[:, c4 * 128:(c4 + 1) * 128], in_=xt4[:, sub, c * 128:(c + 1) * 128], identity=k.ident[:]),
                     R=list(xt4_bs) + [k.ident_b], W=[pt_b], inc=(c4 == 3))
            for c4 in range(4):
                c = half * 4 + c4
                o = hT[:, c, sub * 128:(sub + 1) * 128]
                i_ = pt[:, c4 * 128:(c4 + 1) * 128]
                if c % 2 == 0:
                    P.op("act", lambda e, o=o, i_=i_, c=c: e.activation(out=o, in_=i_, func=AF.Identity, scale=k.gmod[:, li, c:c + 1], bias=k.mod[:, li, c:c + 1]),
                         R=[pt_b, k.gmod_b, k.mod_b], WP=[hT_b])
                else:
                    P.op("dve", lambda e, o=o, i_=i_, c=c: e.tensor_scalar(out=o, in0=i_, scalar1=k.gmod[:, li, c:c + 1], scalar2=k.mod[:, li, c:c + 1], op0=ALU.mult, op1=ALU.add),
                         R=[pt_b, k.gmod_b, k.mod_b], WP=[hT_b])
    return hT, hT_b, a.rcs[hb], a.rcs_b[hb]


def rcs_load(k, a, st):
    hb = st % 2
    k.P.dma("sp", a.rcs[hb], k.ropeD[:, :, st * 512:(st + 1) * 512].rearrange("j p t -> p j t"), R=[k.rope_b], W=[a.rcs_b[hb]])


def hT_prime(k, a, li, xsrc, xsrc_b):
    rcs_load(k, a, 0)
    hT_load(k, a, 0, xsrc, xsrc_b)
    hT_load(k, a, 1, xsrc, xsrc_b)
    hT_front(k, a, 0)
    return hT_back(k, a, li, 0)


def hT_step_begin(k, a, st, xsrc, xsrc_b):
    if st + 1 < NST:
        rcs_load(k, a, st + 1)
        hT_front(k, a, st + 1)


def hT_step_mid(k, a, li, st, xsrc, xsrc_b):
    nxt = None
    if st + 1 < NST:
        nxt = hT_back(k, a, li, st + 1)
    if st + 2 < NST:
        hT_load(k, a, st + 2, xsrc, xsrc_b)
    return nxt


def mm_psum(k, a):
    i = a.mm_n % 4
    a.mm_n += 1
    return k.ps[i], k.ps_b[i]


def fm_group(k, pm, pm_b, w, w_b, kch, col0, ncols, rhs_fn, rhs_b):
    P = k.P
    for c in range(kch):
        P.op("pe", lambda e, c=c: e.matmul(pm[0:ncols, :], lhsT=w[:, c, col0:col0 + ncols], rhs=rhs_fn(c), start=(c == 0), stop=(c == kch - 1)),
             R=[w_b] + list(rhs_b), W=[pm_b], inc=(c == kch - 1))


def job_fm(k, a, w, w_b, kch, col0, ncols, rhs_fn, rhs_b, out, out_b, act=None, eng="act"):
    P = k.P
    pm, pm_b = mm_psum(k, a)
    fm_group(k, pm, pm_b, w, w_b, kch, col0, ncols, rhs_fn, rhs_b)
    o = out[0:ncols, :]
    if act is not None:
        P.op("act", lambda e: e.activation(out=o, in_=pm[0:ncols, :], func=act), R=[pm_b], WP=[out_b])
    elif eng == "act":
        P.op("act", lambda e: e.activation(out=o, in_=pm[0:ncols, :], func=AF.Copy), R=[pm_b], WP=[out_b])
    else:
        P.op("dve", lambda e: e.tensor_copy(out=o, in_=pm[0:ncols, :]), R=[pm_b], WP=[out_b])


def job_rope(k, a, w, w_b, kch, col0, w2, w2_b, cols0, ncols, rhs_fn, rhs_b, rcs, rcs_b, out, out_b):
    P = k.P
    pm, pm_b = mm_psum(k, a)
    fm_group(k, pm, pm_b, w, w_b, kch, col0, ncols, rhs_fn, rhs_b)
    pm2, pm2_b = mm_psum(k, a)
    fm_group(k, pm2, pm2_b, w2, w2_b, kch, cols0, ncols, rhs_fn, rhs_b)
    r0, r0_b, r1, r1_b = a.rt[0], a.rt_b[0], a.rt[1], a.rt_b[1]
    P.op("dve", lambda e: e.tensor_tensor(out=r0[0:ncols, :], in0=pm[0:ncols, :], in1=rcs[0:ncols, 0, :], op=ALU.mult),
         R=[pm_b, rcs_b], W=[r0_b])
    P.op("dve", lambda e: e.tensor_tensor(out=r1[0:ncols, :], in0=pm2[0:ncols, :], in1=rcs[0:ncols, 1, :], op=ALU.mult),
         R=[pm2_b, rcs_b], W=[r1_b])
    P.op("pool", lambda e: e.tensor_tensor(out=out[0:ncols, :], in0=r0[0:ncols, :], in1=r1[0:ncols, :], op=ALU.add),
         R=[r0_b, r1_b], WP=[out_b])


def job_tm(k, a, w, w_b, kch, col0, ncols, lhs_fn, lhs_b, out, out_b):
    P = k.P
    pm, pm_b = mm_psum(k, a)
    for c in range(kch):
        P.op("pe", lambda e, c=c: e.matmul(pm[:, 0:ncols], lhsT=lhs_fn(c), rhs=w[:, c, col0:col0 + ncols], start=(c == 0), stop=(c == kch - 1)),
             R=[w_b] + list(lhs_b), W=[pm_b], inc=(c == kch - 1))
    dv = out.shape[-1]
    P.op("act", lambda e: e.activation(out=out, in_=pm[:, 0:ncols].rearrange("p (h d) -> p h d", d=dv), func=AF.Copy), R=[pm_b], WP=[out_b])


def band_cols(kk, W):
    lo = max(0, 128 * kk)
    hi = 512 if W is None else min(512, 128 * kk + 127 + W)
    return (lo, hi)


def attention_rows(k, rows, pt_bufs, scale):
    P = k.P
    steps = []
    pending = []
    for ri, row in enumerate(rows):
        n = len(row["ksteps"])
        for si, stp in enumerate(row["ksteps"]):
            steps.append((ri, si, n, stp))

    def issue_s(idx):
        ri, si, n, stp = steps[idx]
        sb_i = idx % 3
        ps, ps_b = k.ps[sb_i], k.ps_b[sb_i]
        kp = stp.get("kp", 128)
        c0, c1 = stp.get("cols", (0, 512))
        ns = len(stp["s"])
        for j, (lhsT, rhs, bufs) in enumerate(stp["s"]):
            P.op("pe", lambda e, lhsT=lhsT, rhs=rhs, j=j, ps=ps, kp=kp, c0=c0, c1=c1: e.matmul(ps[0:kp, c0:c1], lhsT=lhsT, rhs=rhs[:, c0:c1], start=(j == 0), stop=(j == ns - 1)),
                 R=list(bufs), W=[ps_b], inc=(j == ns - 1))
        pt, pt_b = pt_bufs[idx % len(pt_bufs)]
        P.op("act", lambda e, pt=pt, ps=ps, kp=kp, c0=c0, c1=c1: e.activation(out=pt[0:kp, c0:c1], in_=ps[0:kp, c0:c1], func=AF.Exp, scale=float(scale)),
             R=[ps_b], W=[pt_b])

    def issue_pv(idx):
        ri, si, n, stp = steps[idx]
        pt, pt_b = pt_bufs[idx % len(pt_bufs)]
        kp = stp.get("kp", 128)
        c0, c1 = stp.get("cols", (0, 512))
        nob = rows[ri].get("o_banks", 2)
        par = ri % nob
        dacc = rows[ri].get("dacc")
        if dacc is not None:
            acc_t, acc_b_ = dacc
            if si == 0:
                P.op("dve", lambda e, pt=pt, kp=kp, c0=c0, c1=c1: e.tensor_copy(out=acc_t[0:kp, c0:c1], in_=pt[0:kp, c0:c1]), R=[pt_b], W=[acc_b_])
            else:
                P.op("dve", lambda e, pt=pt, kp=kp, c0=c0, c1=c1: e.tensor_tensor(out=acc_t[0:kp, c0:c1], in0=acc_t[0:kp, c0:c1], in1=pt[0:kp, c0:c1], op=ALU.add),
                     R=[pt_b], WP=[acc_b_])
        for (which, lhsT, bufs, mrows) in stp["pv"]:
            pi = 7 if which == "A" else (3 + par if which == "O" else 5 + ri % 2)
            po, po_b = k.ps[pi], k.ps_b[pi]
            P.op("pe", lambda e, lhsT=lhsT, po=po, pt=pt, kp=kp, mrows=mrows, si=si, n=n, c0=c0, c1=c1: e.matmul(po[0:mrows, c0:c1], lhsT=lhsT, rhs=pt[0:kp, c0:c1], start=(si == 0), stop=(si == n - 1), skip_group_check=True),
                 R=list(bufs) + [pt_b], W=[po_b], inc=True)
        if si == 0:
            pending.extend(rows[ri].get("pre", ()))
        if si == n - 1:
            pending[:] = [p_ for p_ in pending if p_ is not None]
            while pending:
                pending.pop(0)()
            ret = rows[ri]["epilogue"](k.ps[3 + par], k.ps_b[3 + par], k.ps[5 + ri % 2], k.ps_b[5 + ri % 2])
            if ret:
                pending.extend(ret)
        elif pending:
            p_ = pending.pop(0)
            if p_ is not None:
                p_()

    N = len(steps)
    if N == 0:
        return
    issue_s(0)
    if N > 1:
        issue_s(1)
    for idx in range(N):
        if idx + 2 < N:
            issue_s(idx + 2)
        issue_pv(idx)
    while pending:
        p_ = pending.pop(0)
        if p_ is not None:
            p_()


def phase_c(k, li, wout_name, last):
    P, nc, A, din = k.P, k.nc, k.arena, k.din
    P.barrier()
    A.reset()
    wo = A.alloc([8, D], BF16); wo_b = Buf()
    stg = [A.alloc([8, 256], F32) for _ in range(2)]
    stg_b = [Buf() for _ in range(2)]
    k.stg_n = 0
    load_cast_weights(k, wo, din[wout_name], D, 8, stg, stg_b, wo_b)
    gbc = A.alloc([D], F32); gbc_b = Buf()
    P.dma("sp", gbc, k.modrow[li:li + 1, 2048:3072].partition_broadcast(128), R=[k.modrow_b], W=[gbc_b])
    if last:
        fg = A.alloc([D], F32); fg_b = Buf()
        P.dma("sp", fg, din["final_g"].partition_broadcast(128), W=[fg_b])
    ots = [A.alloc([8, 512], BF16) for _ in range(2)]
    ots_b = [Buf() for _ in range(2)]
    xt = [A.alloc([D], F32) for _ in range(2)]
    xt_b = [Buf() for _ in range(2)]
    xo = [A.alloc([D], F32) for _ in range(2)]
    xo_b = [Buf() for _ in range(2)]
    junk = A.alloc([D], F32); junk_b = Buf()
    ss = [A.alloc([4], F32) for _ in range(2)]
    ss_b = [Buf() for _ in range(2)]
    xsrc = din["x_in"] if li == 0 else k.xr
    n = 0
    for st in range(NST):
        ob = st % 2
        P.dma("sp", ots[ob], k.OT[:, st * 512:(st + 1) * 512].rearrange("(c p) t -> p c t", p=128), R=[k.OT_b], W=[ots_b[ob]])
        for sub in range(4):
            t = st * 4 + sub
            b = t % 2
            P.dma("sp", xt[b], xsrc[t * 128:(t + 1) * 128, :], R=[k.xr_b[t]], W=[xt_b[b]])
            for half in range(2):
                pm, pm_b = k.ps[n % 4], k.ps_b[n % 4]
                n += 1
                for c in range(8):
                    P.op("pe", lambda e, c=c, ob=ob, sub=sub, half=half, pm=pm: e.matmul(pm[:, :], lhsT=ots[ob][:, c, sub * 128:(sub + 1) * 128], rhs=wo[:, c, half * 512:(half + 1) * 512], start=(c == 0), stop=(c == 7)),
                         R=[ots_b[ob], wo_b], W=[pm_b], inc=(c == 7))
                hs = slice(half * 512, (half + 1) * 512)
                P.op("dve", lambda e, b=b, hs=hs, pm=pm: e.tensor_tensor(out=xo[b][:, hs], in0=pm[:, :], in1=gbc[:, hs], op=ALU.mult),
                     R=[pm_b, gbc_b], WP=[xo_b[b]])
                P.op("pool", lambda e, b=b, hs=hs: e.tensor_tensor(out=xo[b][:, hs], in0=xo[b][:, hs], in1=xt[b][:, hs], op=ALU.add),
                     R=[xt_b[b]], WP=[xo_b[b]])
            if not last:
                P.dma("act", k.xr[t * 128:(t + 1) * 128, :], xo[b], R=[xo_b[b]], W=[k.xr_b[t]])
            else:
                s_, s_b = ss[b], ss_b[b]
                P.op("act", lambda e, b=b, s_=s_: e.activation(out=junk, in_=xo[b], func=AF.Square, accum_out=s_[:, 0:1]),
                     R=[xo_b[b]], W=[junk_b], WP=[s_b])
                P.op("dve", lambda e, s_=s_: e.tensor_scalar(out=s_[:, 1:2], in0=s_[:, 0:1], scalar1=1.0 / D, scalar2=EPS, op0=ALU.mult, op1=ALU.add),
                     R=[s_b], WP=[s_b])
                P.op("act", lambda e, s_=s_: e.activation(out=s_[:, 2:3], in_=s_[:, 1:2], func=AF.Sqrt), R=[s_b], WP=[s_b])
                P.op("dve", lambda e, s_=s_: e.reciprocal(out=s_[:, 3:4], in_=s_[:, 2:3]), R=[s_b], WP=[s_b])
                P.op("dve", lambda e, b=b, s_=s_: e.scalar_tensor_tensor(out=xo[b], in0=xo[b], scalar=s_[:, 3:4], in1=fg, op0=ALU.mult, op1=ALU.mult),
                     R=[s_b, fg_b], WP=[xo_b[b]])
                P.dma("act", k.out[t * 128:(t + 1) * 128, :], xo[b], R=[xo_b[b]], WP=[k.out_b])
    P.barrier()


def mla_layer(k, li, j, last):
    P, nc, A, din = k.P, k.nc, k.arena, k.din
    pre = "mla%d_" % j
    C = MLA_COLS
    xsrc = din["x_in"] if li == 0 else k.xr
    A.reset()
    a = phase_a_common(k, li, (8, 128))
    wa = A.alloc([8, C["n"]], BF16); wa_b = Buf()
    load_cast_weights(k, wa, din[pre + "wa"], C["n"], 8, a.stg, a.stg_b, wa_b)
    wqn = A.alloc([2, 1024], BF16); wqn_b = Buf()
    load_cast_weights(k, wqn, din[pre + "wqn"], 1024, 2, a.stg, a.stg_b, wqn_b)
    wqr = A.alloc([2, 512], BF16); wqr_b = Buf()
    load_cast_weights(k, wqr, din[pre + "wqr"], 512, 2, a.stg, a.stg_b, wqr_b)
    wqrs = A.alloc([2, 512], BF16); wqrs_b = Buf()
    load_cast_weights(k, wqrs, din[pre + "wqrs"], 512, 2, a.stg, a.stg_b, wqrs_b)
    wkn = A.alloc([1, 1024], BF16); wkn_b = Buf()
    load_cast_weights(k, wkn, din[pre + "wkn"], 1024, 1, a.stg, a.stg_b, wkn_b)
    wv = A.alloc([1, 1024], BF16); wv_b = Buf()
    load_cast_weights(k, wv, din[pre + "wv"], 1024, 1, a.stg, a.stg_b, wv_b)
    qg = A.alloc([2], F32); qg_b = Buf()
    kvg = A.alloc([1], F32); kvg_b = Buf()
    P.dma("sp", qg, din[pre + "qg"], W=[qg_b])
    P.dma("sp", kvg, din[pre + "kvg"], W=[kvg_b])
    lat = A.alloc([3, 512], F32); lat_b = Buf()
    sq = A.alloc([3, 512], F32); sq_b = Buf()
    rs = [A.alloc([512], F32) for _ in range(2)]
    rs_b = [Buf() for _ in range(2)]
    latn = A.alloc([3, 512], BF16); latn_b = Buf()
    nxt = hT_prime(k, a, li, xsrc, k.xr_b)
    for st in range(NST):
        tok0 = st * 512
        ts = slice(tok0, tok0 + 512)
        hT, hT_b, rcs, rcs_b = nxt
        rhs_h = lambda c, hT=hT: hT[:, c, :]
        hT_step_begin(k, a, st, xsrc, k.xr_b)
        for ci in range(3):
            pm, pm_b = mm_psum(k, a)
            fm_group(k, pm, pm_b, wa, wa_b, 8, ci * 128, 128, rhs_h, [hT_b])
            P.op("act", lambda e, ci=ci, pm=pm: e.activation(out=lat[:, ci, :], in_=pm[:, :], func=AF.Copy), R=[pm_b], WP=[lat_b])
            P.op("act", lambda e, ci=ci, pm=pm: e.activation(out=sq[:, ci, :], in_=pm[:, :], func=AF.Square), R=[pm_b], WP=[sq_b])
        g = Group(k, a, k.KT[1024:1088, ts], k.KT_b, 1, rows_last=64)
        job_rope(k, a, wa, wa_b, 8, C["kpe"], wa, wa_b, C["kpes"], 64, rhs_h, [hT_b], rcs, rcs_b, g.slot(0), g.b)
        g.flush()
        for h4 in range(2):
            g = Group(k, a, k.GT[h4 * 512:(h4 + 1) * 512, ts], k.GT_b, 4)
            for j4 in range(4):
                job_fm(k, a, wa, wa_b, 8, C["gate"] + (h4 * 4 + j4) * 128, 128, rhs_h, [hT_b], g.slot(j4), g.b, act=AF.Silu)
            g.flush()
        for which, chunks, nfeat, gsb, gsb_b in ((0, (0, 1), 256, qg, qg_b), (1, (2,), 128, kvg, kvg_b)):
            pm, pm_b = mm_psum(k, a)
            for jj, ci in enumerate(chunks):
                P.op("pe", lambda e, ci=ci, jj=jj, pm=pm, chunks=chunks: e.matmul(pm[:, :], lhsT=k.onesf[:], rhs=sq[:, ci, :], start=(jj == 0), stop=(jj == len(chunks) - 1)),
                     R=[k.onesf_b, sq_b], W=[pm_b], inc=(jj == len(chunks) - 1))
            r_, r_b = rs[which], rs_b[which]
            P.op("act", lambda e, r_=r_, pm=pm, nfeat=nfeat: e.activation(out=r_, in_=pm[:, :], func=AF.Sqrt, scale=1.0 / nfeat, bias=EPS), R=[pm_b], W=[r_b])
            P.op("dve", lambda e, r_=r_: e.reciprocal(out=r_, in_=r_), R=[r_b], W=[r_b])
            for jj, ci in enumerate(chunks):
                P.op("dve", lambda e, ci=ci, jj=jj, r_=r_, gsb=gsb: e.scalar_tensor_tensor(out=latn[:, ci, :], in0=lat[:, ci, :], scalar=gsb[:, jj:jj + 1], in1=r_, op0=ALU.mult, op1=ALU.mult),
                     R=[lat_b, r_b, gsb_b], WP=[latn_b])
        nxt = hT_step_mid(k, a, li, st, xsrc, k.xr_b)
        rhs_q = lambda c: latn[:, c, :]
        rhs_kv = lambda c: latn[:, 2, :]
        for h4 in range(2):
            g = Group(k, a, k.QT[h4 * 512:(h4 + 1) * 512, ts], k.QT_b, 4)
            for j4 in range(4):
                job_fm(k, a, wqn, wqn_b, 2, (h4 * 4 + j4) * 128, 128, rhs_q, [latn_b], g.slot(j4), g.b, eng="dve")
            g.flush()
        g = Group(k, a, k.QT[1024:1536, ts], k.QT_b, 4)
        for hp in range(4):
            job_rope(k, a, wqr, wqr_b, 2, hp * 128, wqrs, wqrs_b, hp * 128, 128, rhs_q, [latn_b], rcs, rcs_b, g.slot(hp), g.b)
        g.flush()
        for h4 in range(2):
            g = Group(k, a, k.KT[h4 * 512:(h4 + 1) * 512, ts], k.KT_b, 4)
            for j4 in range(4):
                job_fm(k, a, wkn, wkn_b, 1, (h4 * 4 + j4) * 128, 128, rhs_kv, [latn_b], g.slot(j4), g.b, eng="dve")
            g.flush()
        for sub in range(4):
            for half in range(2):
                job_tm(k, a, wv, wv_b, 1, half * 512, 512, lambda c, sub=sub: latn[:, 2, sub * 128:(sub + 1) * 128], [latn_b],
                       a.vst[:, half * 4:(half + 1) * 4, sub, :], a.vst_b)
        P.dma("sp", k.VT[:, :, st * 4:(st + 1) * 4, :].rearrange("h p t d -> p h t d"), a.vst, R=[a.vst_b], WP=[k.VT_b])
    P.barrier()
    A.reset()
    kpe = A.alloc([S], BF16); kpe_b = Buf()
    P.dma("sp", kpe[0:64, :], k.KT[1024:1088, :], R=[k.KT_b], W=[kpe_b])
    hb = []
    for i in range(2):
        hb.append(dict(qn=A.alloc([S], BF16), qr=A.alloc([S], BF16), kn=A.alloc([S], BF16), v=A.alloc([NT, 128], BF16),
                       g=A.alloc([S], BF16), b=Buf()))
    pt_bufs = [(A.alloc([512], BF16), Buf()) for _ in range(3)]
    rd = [A.alloc([512], F32) for _ in range(2)]
    rd_b = [Buf() for _ in range(2)]
    og = [A.alloc([512], F32) for _ in range(2)]
    og_b = [Buf() for _ in range(2)]
    ost = [A.alloc([512], BF16) for _ in range(2)]
    ost_b = [Buf() for _ in range(2)]
    dacc = [A.alloc([512], F32) for _ in range(2)]
    dacc_b = [Buf() for _ in range(2)]
    scale = 192.0 ** -0.5
    ep_n = [0]
    for h in range(8):
        hbuf = hb[h % 2]
        b_ = hbuf["b"]
        P.dma("sp", hbuf["qn"], k.QT[h * 128:(h + 1) * 128, :], R=[k.QT_b], WP=[b_])
        P.dma("sp", hbuf["qr"][0:64, :], k.QT[1024 + h * 64:1024 + (h + 1) * 64, :], R=[k.QT_b], WP=[b_])
        P.dma("sp", hbuf["kn"], k.KT[h * 128:(h + 1) * 128, :], R=[k.KT_b], WP=[b_])
        P.dma("sp", hbuf["v"], k.VT[h], R=[k.VT_b], WP=[b_])
        P.dma("sp", hbuf["g"], k.GT[h * 128:(h + 1) * 128, :], R=[k.GT_b], WP=[b_])
        rows = []
        for qi in range(NST):
            qs = slice(qi * 512, (qi + 1) * 512)
            ksteps = []
            for ki in range(4 * qi + 4):
                ks = slice(ki * 128, (ki + 1) * 128)
                s_list = [(hbuf["kn"][:, ks], hbuf["qn"][:, qs], [b_]),
                          (kpe[0:64, ks], hbuf["qr"][0:64, qs], [b_, kpe_b])]
                kk = ki - 4 * qi
                if kk >= 0:
                    s_list.append((k.identb[:], k.masks[:, MASK_CAUSAL[kk], :], [k.identb_b, k.masks_b]))
                pv = [("O", hbuf["v"][:, ki, :], [b_], 128)]
                ksteps.append(dict(s=s_list, pv=pv, cols=band_cols(max(kk, 0), None)))

            ri_ = len(rows)

            def epilogue(O, O_b, Dn, Dn_b, h=h, qs=qs, hbuf=hbuf, b_=b_, ri_=ri_):
                i = ri_ % 2
                Dp, Dp_b = k.ps[6 + i], k.ps_b[6 + i]

                def rest():
                    P.op("pe", lambda e: e.matmul(Dp[:, :], lhsT=k.onesf[:], rhs=dacc[i], start=True, stop=True), R=[k.onesf_b, dacc_b[i]], W=[Dp_b])
                    P.op("act", lambda e: e.activation(out=rd[i], in_=Dp[:, :], func=AF.Ln), R=[Dp_b], W=[rd_b[i]])
                    P.op("act", lambda e: e.activation(out=rd[i], in_=rd[i], func=AF.Exp, scale=-1.0), R=[rd_b[i]], W=[rd_b[i]])
                    P.op("dve", lambda e: e.tensor_tensor(out=og[i], in0=O[:, :], in1=rd[i], op=ALU.mult), R=[O_b, rd_b[i]], W=[og_b[i]])
                    P.op("pool", lambda e: e.tensor_tensor(out=ost[i], in0=og[i], in1=hbuf["g"][:, qs], op=ALU.mult), R=[og_b[i], b_], W=[ost_b[i]])
                    P.dma("pool", k.OT[h * 128:(h + 1) * 128, qs], ost[i], R=[ost_b[i]], WP=[k.OT_b])
                return [None, None, rest]
            rows.append(dict(ksteps=ksteps, o_banks=3, dacc=(dacc[ri_ % 2], dacc_b[ri_ % 2]), epilogue=epilogue))
        attention_rows(k, rows, pt_bufs, scale)
    phase_c(k, li, pre + "wout", last)


def mla_rope_q(k, a, wqr, wqr_b, wqrs, wqrs_b, hp, rhs_q, latn_b, tok0):
    P = k.P
    pm, pm_b = mm_psum(k, a)
    fm_group(k, pm, pm_b, wqr, wqr_b, 2, hp * 128, 128, rhs_q, [latn_b])
    pm2, pm2_b = mm_psum(k, a)
    fm_group(k, pm2, pm2_b, wqrs, wqrs_b, 2, hp * 128, 128, rhs_q, [latn_b])
    r0, r0_b, r1, r1_b = a.rt[0], a.rt_b[0], a.rt[1], a.rt_b[1]
    rcs, rcs_b = a.cur_rcs, a.cur_rcs_b
    P.op("dve", lambda e: e.tensor_tensor(out=r0, in0=pm[:, :], in1=rcs[:, 0, :], op=ALU.mult), R=[pm_b, rcs_b], W=[r0_b])
    P.op("dve", lambda e: e.tensor_tensor(out=r1, in0=pm2[:, :], in1=rcs[:, 1, :], op=ALU.mult), R=[pm2_b, rcs_b], W=[r1_b])
    st, st_b = fst_next(a)
    P.op("pool", lambda e: e.tensor_tensor(out=st, in0=r0, in1=r1, op=ALU.add), R=[r0_b, r1_b], W=[st_b])
    P.dma("pool", k.QT[1024 + hp * 128:1024 + (hp + 1) * 128, tok0:tok0 + 512], st, R=[st_b], WP=[k.QT_b])


def swa_layer(k, li, last):
    P, nc, A, din = k.P, k.nc, k.arena, k.din
    C = SWA_COLS
    xsrc = din["x_in"] if li == 0 else k.xr
    A.reset()
    a = phase_a_common(k, li, (2, 64))
    wa = A.alloc([8, C["n"]], BF16); wa_b = Buf()
    load_cast_weights(k, wa, din["swa_wa"], C["n"], 8, a.stg, a.stg_b, wa_b)
    nxt = hT_prime(k, a, li, xsrc, k.xr_b)
    for st in range(NST):
        tok0 = st * 512
        ts = slice(tok0, tok0 + 512)
        hT, hT_b, rcs, rcs_b = nxt
        rhs_h = lambda c, hT=hT: hT[:, c, :]
        hT_step_begin(k, a, st, xsrc, k.xr_b)
        g = Group(k, a, k.KT[0:128, ts], k.KT_b, 1)
        job_rope(k, a, wa, wa_b, 8, C["k"], wa, wa_b, C["ks"], 128, rhs_h, [hT_b], rcs, rcs_b, g.slot(0), g.b)
        g.flush()
        for m4 in range(2):
            g = Group(k, a, k.QT[m4 * 512:(m4 + 1) * 512, ts], k.QT_b, 4)
            for j4 in range(4):
                m = m4 * 4 + j4
                job_rope(k, a, wa, wa_b, 8, C["q"] + m * 128, wa, wa_b, C["qs"] + m * 128, 128, rhs_h, [hT_b], rcs, rcs_b, g.slot(j4), g.b)
            g.flush()
        for sub in range(4):
            job_tm(k, a, wa, wa_b, 8, C["v"], 128, lambda c, sub=sub, hT=hT: hT[:, c, sub * 128:(sub + 1) * 128], [hT_b],
                   a.vst[:, :, sub, :], a.vst_b)
        P.dma("sp", k.VS[0:2, :, st * 4:(st + 1) * 4, :].rearrange("h p t d -> p h t d"), a.vst, R=[a.vst_b], WP=[k.VS_b])
        nxt = hT_step_mid(k, a, li, st, xsrc, k.xr_b)
        for m4 in range(2):
            g = Group(k, a, k.GT[m4 * 512:(m4 + 1) * 512, ts], k.GT_b, 4)
            for j4 in range(4):
                job_fm(k, a, wa, wa_b, 8, C["gate"] + (m4 * 4 + j4) * 128, 128, rhs_h, [hT_b], g.slot(j4), g.b, act=AF.Silu)
            g.flush()
    P.barrier()
    A.reset()
    kT = [A.alloc([S], BF16) for _ in range(2)]
    kT_b = Buf()
    vv = [A.alloc([NT, 64], BF16) for _ in range(2)]
    vv_b = Buf()
    for kv in range(2):
        P.dma("sp", kT[kv][0:64, :], k.KT[kv * 64:(kv + 1) * 64, :], R=[k.KT_b], WP=[kT_b])
        P.dma("sp", vv[kv], k.VS[kv], R=[k.VS_b], WP=[vv_b])
    snk = A.alloc([16], F32); snk_b = Buf()
    P.dma("sp", snk, din["swa_sinks"].partition_broadcast(128), W=[snk_b])
    P.op("act", lambda e: e.activation(out=snk, in_=snk, func=AF.Exp), R=[snk_b], W=[snk_b])
    hb = [dict(q=A.alloc([S], BF16), g=A.alloc([S], BF16), b=Buf()) for _ in range(2)]
    pt_bufs = [(A.alloc([512], BF16), Buf()) for _ in range(3)]
    rd = [A.alloc([512], F32) for _ in range(2)]; rd_b = [Buf() for _ in range(2)]
    og = [A.alloc([512], F32) for _ in range(2)]; og_b = [Buf() for _ in range(2)]
    ost = [A.alloc([512], BF16) for _ in range(2)]; ost_b = [Buf() for _ in range(2)]
    ep_n = [0]
    for hh in range(16):
        kv = hh // 8
        hbuf = hb[hh % 2]
        b_ = hbuf["b"]
        P.dma("sp", hbuf["q"][0:64, :], k.QT[hh * 64:(hh + 1) * 64, :], R=[k.QT_b], WP=[b_])
        P.dma("sp", hbuf["g"][0:64, :], k.GT[hh * 64:(hh + 1) * 64, :], R=[k.GT_b], WP=[b_])
        rows = []
        for qi in range(NST):
            qs = slice(qi * 512, (qi + 1) * 512)
            ksteps = []
            for kk in range(-1, 4):
                ki = 4 * qi + kk
                if ki < 0:
                    continue
                ks = slice(ki * 128, (ki + 1) * 128)
                s_list = [(kT[kv][0:64, ks], hbuf["q"][0:64, qs], [b_, kT_b]),
                          (k.identb[:], k.masks[:, MASK_SWA[kk], :], [k.identb_b, k.masks_b])]
                pv = [("O", vv[kv][:, ki, :], [vv_b], 64), ("D", k.onesb[:, 0:64], [k.onesb_b], 64)]
                ksteps.append(dict(s=s_list, pv=pv, cols=band_cols(kk, 128)))

            def epilogue(O, O_b, Dn, Dn_b, hh=hh, qs=qs, hbuf=hbuf, b_=b_):
                i = ep_n[0] % 2
                ep_n[0] += 1
                P.op("dve", lambda e: e.tensor_scalar(out=rd[i][0:64, :], in0=Dn[0:64, :], scalar1=snk[0:64, hh:hh + 1], scalar2=None, op0=ALU.add),
                     R=[Dn_b, snk_b], W=[rd_b[i]])
                P.op("dve", lambda e: e.reciprocal(out=rd[i][0:64, :], in_=rd[i][0:64, :]), R=[rd_b[i]], W=[rd_b[i]])
                P.op("dve", lambda e: e.tensor_tensor(out=og[i][0:64, :], in0=O[0:64, :], in1=rd[i][0:64, :], op=ALU.mult), R=[O_b, rd_b[i]], W=[og_b[i]])
                P.op("pool", lambda e: e.tensor_tensor(out=ost[i][0:64, :], in0=og[i][0:64, :], in1=hbuf["g"][0:64, qs], op=ALU.mult), R=[og_b[i], b_], W=[ost_b[i]])
                P.dma("pool", k.OT[hh * 64:(hh + 1) * 64, qs], ost[i][0:64, :], R=[ost_b[i]], WP=[k.OT_b])
            rows.append(dict(ksteps=ksteps, epilogue=epilogue))
        attention_rows(k, rows, pt_bufs, 0.125)
    phase_c(k, li, "swa_wout", last)


def nsa_layer(k, li, last):
    P, nc, A, din = k.P, k.nc, k.arena, k.din
    C = NSA_COLS
    xsrc = din["x_in"] if li == 0 else k.xr
    A.reset()
    a = phase_a_common(k, li, (8, 64))
    wa = A.alloc([8, C["n"]], BF16); wa_b = Buf()
    load_cast_weights(k, wa, din["nsa_wa"], C["n"], 8, a.stg, a.stg_b, wa_b)
    nxt = hT_prime(k, a, li, xsrc, k.xr_b)
    for st in range(NST):
        tok0 = st * 512
        ts = slice(tok0, tok0 + 512)
        hT, hT_b, rcs, rcs_b = nxt
        rhs_h = lambda c, hT=hT: hT[:, c, :]
        hT_step_begin(k, a, st, xsrc, k.xr_b)
        g = Group(k, a, k.KT[0:512, ts], k.KT_b, 4)
        for m in range(2):
            job_rope(k, a, wa, wa_b, 8, C["kslc"] + m * 128, wa, wa_b, C["kslcs"] + m * 128, 128, rhs_h, [hT_b], rcs, rcs_b, g.slot(m), g.b)
        for m in range(2):
            job_rope(k, a, wa, wa_b, 8, C["kwin"] + m * 128, wa, wa_b, C["kwins"] + m * 128, 128, rhs_h, [hT_b], rcs, rcs_b, g.slot(2 + m), g.b)
        g.flush()
        g = Group(k, a, k.KT[512:1024, ts], k.KT_b, 4)
        for m in range(2):
            job_fm(k, a, wa, wa_b, 8, C["kcmp"] + m * 128, 128, rhs_h, [hT_b], g.slot(m), g.b, eng="dve")
        for m in range(2):
            job_fm(k, a, wa, wa_b, 8, C["vcmp"] + m * 128, 128, rhs_h, [hT_b], g.slot(2 + m), g.b, eng="dve")
        g.flush()
        for m4 in range(2):
            g = Group(k, a, k.QT[m4 * 512:(m4 + 1) * 512, ts], k.QT_b, 4)
            for j4 in range(4):
                m = m4 * 4 + j4
                job_rope(k, a, wa, wa_b, 8, C["q"] + m * 128, wa, wa_b, C["qs"] + m * 128, 128, rhs_h, [hT_b], rcs, rcs_b, g.slot(j4), g.b)
            g.flush()
        g = Group(k, a, k.GT[1024:1072, ts], k.GT_b, 1, rows_last=48)
        job_fm(k, a, wa, wa_b, 8, C["gbr"], 48, rhs_h, [hT_b], g.slot(0), g.b, act=AF.Sigmoid)
        g.flush()
        for sub in range(4):
            lf = lambda c, sub=sub, hT=hT: hT[:, c, sub * 128:(sub + 1) * 128]
            job_tm(k, a, wa, wa_b, 8, C["vslc"], 256, lf, [hT_b], a.vst[:, 0:4, sub, :], a.vst_b)
            job_tm(k, a, wa, wa_b, 8, C["vwin"], 256, lf, [hT_b], a.vst[:, 4:8, sub, :], a.vst_b)
        P.dma("sp", k.VS[:, :, st * 4:(st + 1) * 4, :].rearrange("h p t d -> p h t d"), a.vst, R=[a.vst_b], WP=[k.VS_b])
        nxt = hT_step_mid(k, a, li, st, xsrc, k.xr_b)
        for m4 in range(2):
            g = Group(k, a, k.GT[m4 * 512:(m4 + 1) * 512, ts], k.GT_b, 4)
            for j4 in range(4):
                job_fm(k, a, wa, wa_b, 8, C["gate"] + (m4 * 4 + j4) * 128, 128, rhs_h, [hT_b], g.slot(j4), g.b, act=AF.Silu)
            g.flush()
    import os
    STOP = os.environ.get("NSA_STOP", "")
    if STOP == "A":
        return phase_c(k, li, "nsa_wout", last)
    P.barrier()
    A.reset()
    kc = [A.alloc([256], BF16) for _ in range(4)]; kc_b = Buf()
    vc = [A.alloc([2, 64], BF16) for _ in range(4)]; vc_b = Buf()
    keep = A.off
    stg = [A.alloc([8, 256], F32) for _ in range(2)]
    stg_b = [Buf() for _ in range(2)]
    k.stg_n = 0
    w1 = [A.alloc([16, 128], BF16) for _ in range(2)]; w1_b = [Buf(), Buf()]
    w2f = A.alloc([2, 64], F32); w2f_b = Buf()
    w2 = A.alloc([2, 64], BF16); w2_b = Buf()
    pef = A.alloc([16], F32); pef_b = Buf()
    peb = A.alloc([16], BF16); peb_b = Buf()
    b1 = A.alloc([2], F32); b1_b = Buf()
    x2 = [A.alloc([S], BF16) for _ in range(2)]; x2_b = [Buf(), Buf()]
    hid = [A.alloc([256], BF16) for _ in range(2)]; hid_b = [Buf(), Buf()]
    for wi, nm in enumerate(("nsa_wk1", "nsa_wv1")):
        for half in range(2):
            b = k.stg_n % 2
            k.stg_n += 1
            P.dma("sp", stg[b][:, :, 0:128], din[nm][half * 1024:(half + 1) * 1024, :].rearrange("(m p) j -> p m j", p=128), W=[stg_b[b]])
            P.op("dve", lambda e, b=b, wi=wi, half=half: e.tensor_copy(out=w1[wi][:, half * 8:(half + 1) * 8, :], in_=stg[b][:, :, 0:128]),
                 R=[stg_b[b]], WP=[w1_b[wi]])
    P.dma("sp", w2f[:, 0, :], din["nsa_wk2"], WP=[w2f_b])
    P.dma("sp", w2f[:, 1, :], din["nsa_wv2"], WP=[w2f_b])
    P.op("dve", lambda e: e.tensor_copy(out=w2, in_=w2f), R=[w2f_b], W=[w2_b])
    P.dma("sp", pef, din["nsa_pe"], W=[pef_b])
    P.op("dve", lambda e: e.tensor_copy(out=peb, in_=pef), R=[pef_b], W=[peb_b])
    for wi in range(2):
        P.op("pool", lambda e, wi=wi: e.memset(hid[wi][:, 255:256], 0.0), WP=[hid_b[wi]])
        pm, pm_b = k.ps[wi], k.ps_b[wi]
        for m in range(16):
            P.op("pe", lambda e, wi=wi, m=m, pm=pm: e.matmul(pm[:, 0:1], lhsT=w1[wi][:, m, :], rhs=peb[:, m:m + 1], start=(m == 0), stop=(m == 15)),
                 R=[w1_b[wi], peb_b], W=[pm_b], inc=(m == 15))
        P.op("act", lambda e, wi=wi, pm=pm: e.activation(out=b1[:, wi:wi + 1], in_=pm[:, 0:1], func=AF.Copy), R=[pm_b], WP=[b1_b])
    n = 0
    for kv in range(4):
        for wi in range(2):
            xb = n % 2
            n += 1
            row0 = (512 if wi == 0 else 768) + kv * 64
            P.dma("sp", x2[xb][0:64, :], k.KT[row0:row0 + 64, :], R=[k.KT_b], WP=[x2_b[xb]])
            P.dma("sp", x2[xb][64:128, 0:S - 1], k.KT[row0:row0 + 64, 1:S], R=[k.KT_b], WP=[x2_b[xb]])
            x2v = x2[xb].rearrange("p (i l) -> p i l", l=16)
            pm, pm_b = k.ps[2 + xb], k.ps_b[2 + xb]
            for m in range(16):
                l2 = 2 * m
                rhs = x2v[:, 0:255, l2] if l2 < 16 else x2v[:, 1:256, l2 - 16]
                P.op("pe", lambda e, wi=wi, m=m, pm=pm, rhs=rhs: e.matmul(pm[:, 0:255], lhsT=w1[wi][:, m, :], rhs=rhs, start=(m == 0), stop=(m == 15)),
                     R=[w1_b[wi], x2_b[xb]], W=[pm_b], inc=(m == 15))
            P.op("act", lambda e, wi=wi, pm=pm: e.activation(out=hid[wi][:, 0:255], in_=pm[:, 0:255], func=AF.Silu, bias=b1[:, wi:wi + 1]),
                 R=[pm_b, b1_b], WP=[hid_b[wi]])
            if wi == 0:
                pm2, pm2_b = k.ps[4], k.ps_b[4]
                P.op("pe", lambda e, pm2=pm2: e.matmul(pm2[0:64, 0:256], lhsT=w2[:, 0, :], rhs=hid[0][:, :], start=True, stop=True),
                     R=[w2_b, hid_b[0]], W=[pm2_b])
                P.op("dve", lambda e, kv=kv, pm2=pm2: e.tensor_copy(out=kc[kv][0:64, :], in_=pm2[0:64, 0:256]), R=[pm2_b], WP=[kc_b])
            else:
                pm2, pm2_b = k.ps[5], k.ps_b[5]
                for nt in range(2):
                    P.op("pe", lambda e, nt=nt, pm2=pm2: e.matmul(pm2[:, nt * 64:(nt + 1) * 64], lhsT=hid[1][:, nt * 128:(nt + 1) * 128], rhs=w2[:, 1, :], start=True, stop=True),
                         R=[w2_b, hid_b[1]], W=[pm2_b], inc=(nt == 1))
                P.op("dve", lambda e, kv=kv, pm2=pm2: e.tensor_copy(out=vc[kv], in_=pm2[:, 0:128].rearrange("p (a b) -> p a b", a=2)), R=[pm2_b], WP=[vc_b])
    if STOP == "B0":
        return phase_c(k, li, "nsa_wout", last)
    P.barrier()
    A.reset(keep)
    bf = BF16
    cmask = A.alloc([4608], bf); cmask_b = Buf()
    ovl = A.alloc([2, 64], bf); ovl_b = Buf()
    sel3 = A.alloc([48, 64], bf); sel3_b = Buf()
    cmul = A.alloc([24, 64], bf); cadd = A.alloc([24, 64], bf); ctab_b = Buf()
    gsig = A.alloc([S], bf); gsig_b = Buf()
    P.dma("sp", cmask, din["c_cmask"], W=[cmask_b])
    P.dma("sp", ovl, din["c_ovl"], W=[ovl_b])
    P.dma("sp", sel3[0:48], din["c_sel3"][0:48], W=[sel3_b])
    P.dma("sp", cmul, din["c_cmul"][:, 8:32, :], WP=[ctab_b])
    P.dma("sp", cadd, din["c_cadd"][:, 8:32, :], WP=[ctab_b])
    P.dma("sp", gsig[0:48, :], k.GT[1024:1072, :], R=[k.GT_b], W=[gsig_b])
    QS = [A.alloc([S], bf) for _ in range(4)]
    QS_b = [Buf() for _ in range(4)]
    KE = A.alloc([S], bf); KE_b = Buf()
    kwin = A.alloc([S], bf); kwin_b = Buf()
    vs = A.alloc([NT, 128], bf); vs_b = Buf()
    vw = A.alloc([NT, 128], bf); vw_b = Buf()
    P.op("pool", lambda e: e.memset(vs[:, :, 64:128], 1.0), WP=[vs_b])
    P.op("pool", lambda e: e.memset(vw[:, :, 64:128], 1.0), WP=[vw_b])
    dsb_b = [Buf(), Buf()]
    impT = A.alloc([S], F32); impT_b = Buf()
    G = [A.alloc([512], bf) for _ in range(2)]; G_b = [Buf(), Buf()]
    pt_bufs = [(A.alloc([512], bf), Buf()) for _ in range(3)]
    rd = [A.alloc([512], F32) for _ in range(2)]; rd_b = [Buf() for _ in range(2)]
    tmp = [A.alloc([512], F32) for _ in range(2)]; tmp_b = [Buf() for _ in range(2)]
    acc = [A.alloc([512], F32) for _ in range(2)]; acc_b = [Buf() for _ in range(2)]
    ocl = [A.alloc([512], F32) for _ in range(2)]; ocl_b = [Buf() for _ in range(2)]
    ocs, ocs_b = ocl, ocl_b
    ost = [A.alloc([512], bf) for _ in range(2)]; ost_b = [Buf() for _ in range(2)]
    NW = 2
    impm_l = [A.alloc([64], F32) for _ in range(NW)]; impm_bl = [Buf() for _ in range(NW)]
    imp2_l = [A.alloc([64], F32) for _ in range(NW)]; imp2_bl = [Buf() for _ in range(NW)]
    gq = [A.alloc([512], bf) for _ in range(6)]; gq_b = [Buf() for _ in range(6)]
    m1 = A.alloc([8], F32); m1_b = Buf()
    m2 = A.alloc([8], F32); m2_b = Buf()
    selb_l = [A.alloc([128], F32) for _ in range(NW)]; selb_bl = [Buf() for _ in range(NW)]
    cmp3_l = [A.alloc([4096], BF16) for _ in range(NW)]; cmp3_bl = [Buf() for _ in range(NW)]
    dlo = [cmp3_l[i_].bitcast(F32)[:, 0:512] for i_ in range(2)]; dlo_b = cmp3_bl
    for w_ in range(NW):
        P.op("pool", lambda e, w_=w_: e.memset(selb_l[w_], 0.0), W=[selb_bl[w_]])
    P.dma("sp", KE[64:128, :], din["c_ebig"][64:128, :], WP=[KE_b])
    for g in range(4):
        P.dma("sp", QS[g][64:128, 0:1024], din["c_selst"][64:128, :], WP=[QS_b[g]])
    cn = [0]
    for kv in range(4):
        for g in range(4):
            hh = kv * 4 + g
            P.dma("sp", QS[g][0:64, :], k.QT[hh * 64:(hh + 1) * 64, :], R=[k.QT_b], WP=[QS_b[g]])
        P.dma("sp", KE[0:64, :], k.KT[kv * 64:(kv + 1) * 64, :], R=[k.KT_b], WP=[KE_b])
        P.dma("sp", kwin[0:64, :], k.KT[256 + kv * 64:256 + (kv + 1) * 64, :], R=[k.KT_b], W=[kwin_b])
        P.dma("sp", vs[:, :, 0:64], k.VS[kv], R=[k.VS_b], WP=[vs_b])
        P.dma("sp", vw[:, :, 0:64], k.VS[4 + kv], R=[k.VS_b], WP=[vw_b])
        rows = []
        for g in range(4):
            hh = kv * 4 + g
            for qi in range(NST):
                qs = slice(qi * 512, (qi + 1) * 512)
                ksteps = []
                for nt in range(2):
                    if nt == 1 and qi < 4:
                        continue
                    s_list = [(kc[kv][0:64, nt * 128:(nt + 1) * 128], QS[g][0:64, qs], [kc_b, QS_b[g]])]
                    if not (nt == 0 and qi >= 5):
                        cm0 = qi * 512 if nt == 0 else 2560 + (qi - 4) * 512
                        s_list.append((k.identb[:], cmask[:, cm0:cm0 + 512], [k.identb_b, cmask_b]))
                    pv = [("O", vc[kv][:, nt, :], [vc_b], 64), ("D", k.onesb[:, 0:64], [k.onesb_b], 64), ("A", ovl[:, nt, :], [ovl_b], 64)]
                    ksteps.append(dict(s=s_list, pv=pv))

                def epilogue(O, O_b, Dn, Dn_b, hh=hh, g=g, qs=qs):
                    i = cn[0] % 2
                    cn[0] += 1
                    Aa, Aa_b = k.ps[7], k.ps_b[7]
                    P.op("act", lambda e: e.activation(out=rd[i][0:64, :], in_=Dn[0:64, :], func=AF.Ln, bias=1e-18), R=[Dn_b], W=[rd_b[i]])
                    P.op("act", lambda e: e.activation(out=rd[i][0:64, :], in_=rd[i][0:64, :], func=AF.Exp, scale=-1.0), R=[rd_b[i]], W=[rd_b[i]])
                    P.op("dve", lambda e: e.tensor_tensor(out=ocs[i][0:64, :], in0=O[0:64, :], in1=rd[i][0:64, :], op=ALU.mult), R=[O_b, rd_b[i]], W=[ocs_b[i]])
                    P.dma("pool", k.OC[hh * 64:(hh + 1) * 64, qs], ocs[i][0:64, :], R=[ocs_b[i]], WP=[k.OC_b])
                    if g == 0:
                        P.op("dve", lambda e: e.tensor_tensor(out=impT[0:64, qs], in0=Aa[0:64, :], in1=rd[i][0:64, :], op=ALU.mult), R=[Aa_b, rd_b[i]], WP=[impT_b])
                    else:
                        P.op("dve", lambda e: e.tensor_tensor(out=tmp[i][0:64, :], in0=Aa[0:64, :], in1=rd[i][0:64, :], op=ALU.mult), R=[Aa_b, rd_b[i]], W=[tmp_b[i]])
                        P.op("pool", lambda e: e.tensor_tensor(out=impT[0:64, qs], in0=impT[0:64, qs], in1=tmp[i][0:64, :], op=ALU.add), R=[tmp_b[i]], WP=[impT_b])
                rows.append(dict(ksteps=ksteps, epilogue=epilogue))
        attention_rows(k, rows, pt_bufs, 0.125)
        if STOP == "B1":
            return phase_c(k, li, "nsa_wout", last)
        for w_ in range(NW):
            P.op("pool", lambda e, w_=w_: e.memset(selb_l[w_][:, 64:128], -30000.0), WP=[selb_bl[w_]])

        def sel_stages(t, w_):
            impm, impm_b, imp2, imp2_b = impm_l[w_], impm_bl[w_], imp2_l[w_], imp2_bl[w_]
            selb, selb_b, cmp3, cmp3_b = selb_l[w_], selb_bl[w_], cmp3_l[w_], cmp3_bl[w_]
            pT, pT_b = k.ps[w_], k.ps_b[w_]
            p2, p2_b = k.ps[2 + w_], k.ps_b[2 + w_]
            tsl = slice(t * 128, (t + 1) * 128)
            ns = 2 * t + 2
            in0 = impm[:, 0:ns].unsqueeze(1).to_broadcast([128, ns, ns])
            in1 = impm[:, 0:ns].unsqueeze(2).to_broadcast([128, ns, ns])
            c3 = cmp3[:, 0:ns * ns].rearrange("p (a b) -> p a b", a=ns)
            st_ = []
            st_.append(lambda: P.op("pe", lambda e: e.transpose(out=pT[:, 0:64], in_=impT[0:64, tsl], identity=k.ident[0:64, 0:64]),
                                    R=[impT_b, k.ident_b], W=[pT_b]))
            st_.append(lambda: P.op("dve", lambda e: e.tensor_tensor(out=impm, in0=pT[:, 0:64], in1=cmul[:, t - 8, :], op=ALU.mult), R=[pT_b, ctab_b], W=[impm_b]))
            st_.append(lambda: P.op("dve", lambda e: e.tensor_tensor(out=impm, in0=impm, in1=cadd[:, t - 8, :], op=ALU.add), R=[ctab_b, impm_b], W=[impm_b]))
            st_.append(lambda: P.op("dve", lambda e: e.tensor_tensor(out=c3, in0=in0, in1=in1, op=ALU.is_gt), R=[impm_b], W=[cmp3_b]))
            st_.append(lambda: P.op("dve", lambda e: e.tensor_reduce(out=imp2[:, 0:ns], in_=c3, axis=mybir.AxisListType.X, op=ALU.add), R=[cmp3_b], W=[imp2_b]))
            st_.append(lambda: P.op("dve", lambda e: e.tensor_scalar(out=selb[:, 64:64 + ns], in0=imp2[:, 0:ns], scalar1=15.5, scalar2=30000.0, op0=ALU.is_lt, op1=ALU.mult),
                                    R=[imp2_b], WP=[selb_b]))
            st_.append(lambda: P.op("dve", lambda e: e.tensor_scalar(out=selb[:, 64:64 + ns], in0=selb[:, 64:64 + ns], scalar1=-30000.0, scalar2=None, op0=ALU.add),
                                    R=[selb_b], WP=[selb_b]))
            st_.append(lambda: P.op("pe", lambda e: e.transpose(out=p2[:, 0:128], in_=selb, identity=k.ident[:]), R=[selb_b, k.ident_b], W=[p2_b]))
            for g in range(4):
                st_.append(lambda g=g: P.op("act", lambda e: e.activation(out=QS[g][64:128, tsl], in_=p2[64:128, 0:128], func=AF.Copy), R=[p2_b], WP=[QS_b[g]]))
            return st_

        for t0 in range(8, NT, NW):
            chains = [sel_stages(t0 + w_, w_) for w_ in range(NW)]
            for si_ in range(len(chains[0])):
                for ch in chains:
                    ch[si_]()
        if STOP == "SEL":
            return phase_c(k, li, "nsa_wout", last)
        for g in range(4):
            hh = kv * 4 + g
            rows = []
            for qi in range(NST):
                qs = slice(qi * 512, (qi + 1) * 512)
                ksteps = []
                for ki in range(4 * qi + 4):
                    ks = slice(ki * 128, (ki + 1) * 128)
                    s_list = [(KE[:, ks], QS[g][:, qs], [KE_b, QS_b[g]])]
                    kk = ki - 4 * qi
                    if kk >= 0:
                        s_list.append((k.identb[:], k.masks[:, MASK_CAUSAL[kk], :], [k.identb_b, k.masks_b]))
                    pv = [("O", vs[:, ki, :], [vs_b], 128)]
                    ksteps.append(dict(s=s_list, pv=pv, cols=band_cols(max(kk, 0), None)))

                def gate_pre(r, gi, qs=qs):
                    def f():
                        gb, gb_b = k.ps[7], k.ps_b[7]
                        P.op("pe", lambda e: e.matmul(gb[0:64, :], lhsT=sel3[0:48, r, :], rhs=gsig[0:48, qs], start=True, stop=True), R=[sel3_b, gsig_b], W=[gb_b])
                        P.op("act", lambda e: e.activation(out=gq[gi][0:64, :], in_=gb[0:64, :], func=AF.Copy), R=[gb_b], W=[gq_b[gi]])
                    return f

                def den_shift(O, O_b, i):
                    Dsb = rd[i][64:128, :]
                    P.op("act", lambda e: e.activation(out=Dsb, in_=O[64:128, :], func=AF.Copy), R=[O_b], W=[dsb_b[i]])
                    P.dma("sp", dlo[i][0:64, :], Dsb, R=[dsb_b[i]], W=[dlo_b[i]])

                    def f():
                        P.op("dve", lambda e: e.reciprocal(out=rd[i][0:64, :], in_=dlo[i][0:64, :]), R=[dlo_b[i]], W=[rd_b[i]])
                    return f

                def ep_sel(O, O_b, Dn, Dn_b, hh=hh, qs=qs, qi=qi):
                    i = qi % 2
                    f0 = den_shift(O, O_b, i)

                    def rest():
                        f0()
                        P.op("dve", lambda e: e.tensor_tensor(out=tmp[i][0:64, :], in0=O[0:64, :], in1=rd[i][0:64, :], op=ALU.mult), R=[O_b, rd_b[i]], W=[tmp_b[i]])
                        P.op("dve", lambda e: e.tensor_tensor(out=acc[i][0:64, :], in0=tmp[i][0:64, :], in1=gq[i][0:64, :], op=ALU.mult), R=[gq_b[i], tmp_b[i]], W=[acc_b[i]])
                    return [rest]
                rows.append(dict(ksteps=ksteps, o_banks=4, epilogue=ep_sel, pre=[gate_pre(hh * 3 + 1, qi % 2)]))
                ksteps = []
                for kk in range(-4, 4):
                    ki = 4 * qi + kk
                    if ki < 0:
                        continue
                    ks = slice(ki * 128, (ki + 1) * 128)
                    s_list = [(kwin[0:64, ks], QS[g][0:64, qs], [kwin_b, QS_b[g]]),
                              (k.identb[:], k.masks[:, MASK_WIN[kk], :], [k.identb_b, k.masks_b])]
                    pv = [("O", vw[:, ki, :], [vw_b], 128)]
                    ksteps.append(dict(s=s_list, pv=pv, cols=band_cols(kk, 512)))

                def ep_win(O, O_b, Dn, Dn_b, hh=hh, qs=qs, qi=qi):
                    i = qi % 2
                    Gh, Gh_b = G[i], G_b[i]
                    P.dma("sp", ocl[i][0:64, :], k.OC[hh * 64:(hh + 1) * 64, qs], R=[k.OC_b], W=[ocl_b[i]])
                    P.dma("sp", Gh[0:64, :], k.GT[hh * 64:(hh + 1) * 64, qs], R=[k.GT_b], W=[Gh_b])
                    f0 = den_shift(O, O_b, i)

                    def rest():
                        f0()
                        P.op("dve", lambda e: e.tensor_tensor(out=tmp[i][0:64, :], in0=O[0:64, :], in1=rd[i][0:64, :], op=ALU.mult), R=[O_b, rd_b[i]], W=[tmp_b[i]])
                        P.op("dve", lambda e: e.tensor_tensor(out=tmp[i][0:64, :], in0=tmp[i][0:64, :], in1=gq[2 + i][0:64, :], op=ALU.mult), R=[gq_b[2 + i], tmp_b[i]], W=[tmp_b[i]])
                        P.op("pool", lambda e: e.tensor_tensor(out=acc[i][0:64, :], in0=acc[i][0:64, :], in1=tmp[i][0:64, :], op=ALU.add), R=[tmp_b[i], acc_b[i]], W=[acc_b[i]])
                        P.op("dve", lambda e: e.tensor_tensor(out=tmp[i][0:64, :], in0=ocl[i][0:64, :], in1=gq[4 + i][0:64, :], op=ALU.mult), R=[gq_b[4 + i], ocl_b[i]], W=[tmp_b[i]])
                        P.op("pool", lambda e: e.tensor_tensor(out=acc[i][0:64, :], in0=acc[i][0:64, :], in1=tmp[i][0:64, :], op=ALU.add), R=[tmp_b[i], acc_b[i]], W=[acc_b[i]])
                        P.op("pool", lambda e: e.tensor_tensor(out=ost[i][0:64, :], in0=acc[i][0:64, :], in1=Gh[0:64, :], op=ALU.mult), R=[acc_b[i], Gh_b], W=[ost_b[i]])
                        P.dma("pool", k.OT[hh * 64:(hh + 1) * 64, qs], ost[i][0:64, :], R=[ost_b[i]], WP=[k.OT_b])
                    return [rest]
                rows.append(dict(ksteps=ksteps, o_banks=4, epilogue=ep_win, pre=[gate_pre(hh * 3 + 2, 2 + qi % 2), gate_pre(hh * 3 + 0, 4 + qi % 2)]))
            attention_rows(k, rows, pt_bufs, 0.125)
    phase_c(k, li, "nsa_wout", last)


def _np_dt(a):
    if a.dtype == np.float32:
        return F32
    if a.dtype == np.int32:
        return I32
    return BF16


def make_in_maps(inputs, cores):
    w = _prep_weights({kk: np.asarray(v) for kk, v in inputs.items()})
    cst = _consts()
    shared = {}
    shared.update(w)
    shared.update(cst)
    x = np.asarray(inputs["x"], np.float32)
    c = np.asarray(inputs["c"], np.float32)
    pos = np.asarray(inputs["positions"], np.int32)
    maps = []
    for b in cores:
        m = dict(shared)
        m["x_in"] = np.ascontiguousarray(x[b])
        m["c_fm"] = _fm(c[b], 8)
        m["pos"] = np.ascontiguousarray(pos[b].reshape(1, S))
        maps.append(m)
    return maps


def kernel(**inputs):
    maps = make_in_maps(inputs, list(range(8)))
    shapes = {kk: (v.shape, _np_dt(v)) for kk, v in maps[0].items()}
    nc, P = build(DEPTH, shapes)
    res = run_bass_kernel_spmd(nc, maps, core_ids=list(range(8)))
    out = np.stack([np.asarray(r["out"], np.float32) for r in res.results], axis=0)
    return out
```

```python
import numpy as np
import ml_dtypes
import concourse.bass as bass
import concourse.mybir as mybir
from concourse.bass_utils import run_bass_kernel_spmd

F32 = mybir.dt.float32
BF16 = mybir.dt.bfloat16
I32 = mybir.dt.int32
U8 = mybir.dt.uint8
AF = mybir.ActivationFunctionType
ALU = mybir.AluOpType

S = 4096
D = 1024
NT = S // 128
NST = S // 512
DEPTH = 4
NEG = -30000.0
EPS = 1e-6


class Buf:
    __slots__ = ("w", "r")

    def __init__(self):
        self.w = {}
        self.r = {}


class Prog:
    CE = ("pe", "act", "dve", "pool")
    ALLQ = ("pe", "act", "dve", "pool", "sp")

    def __init__(self, nc, n_dma_sems=32):
        self.nc = nc
        self.ops = {e: [] for e in self.ALLQ}
        self.count = {e: 0 for e in self.CE}
        self.sems = {}
        self.known = {e: {} for e in self.ALLQ}
        self.n_dma = n_dma_sems
        self.dma_cnt = [0] * n_dma_sems
        self.dma_rr = 0
        self.dma_rr_pool = 0
        self._stack = []
        for e in self.CE:
            self.sems[e] = self._sem("s_" + e)
        for i in range(n_dma_sems):
            self.sems[("dma", i)] = self._sem("s_dma%d" % i)
        self.n_ops = 0

    def _sem(self, name):
        cm = self.nc.semaphore(name)
        h = cm.__enter__()
        self._stack.append(cm)
        return h

    def _gather(self, eng, R, W, WP):
        need = {}
        for b in R:
            for k, v in b.w.items():
                if need.get(k, 0) < v:
                    need[k] = v
        for b in list(W) + list(WP):
            for k, v in b.w.items():
                if need.get(k, 0) < v:
                    need[k] = v
            for k, v in b.r.items():
                if need.get(k, 0) < v:
                    need[k] = v
        out = []
        kn = self.known[eng]
        for k, v in need.items():
            if eng == "pe" and k == "pe":
                continue
            if kn.get(k, 0) >= v:
                continue
            kn[k] = v
            out.append((k, v))
        return out

    def _record(self, tok, R, W, WP):
        k, v = tok
        for b in W:
            b.w = {k: v}
            b.r = {}
        for b in WP:
            if b.w.get(k, 0) < v:
                b.w[k] = v
        for b in R:
            if b.r.get(k, 0) < v:
                b.r[k] = v

    def op(self, eng, fn, R=(), W=(), WP=(), inc=True):
        waits = self._gather(eng, R, W, WP)
        if inc:
            self.count[eng] += 1
            ms = self.count[eng]
        else:
            ms = self.count[eng] + 1
        self.ops[eng].append((fn, waits, (eng, 1) if inc else None))
        self._record((eng, ms), R, W, WP)
        self.n_ops += 1

    def dma(self, q, out, in_, R=(), W=(), WP=()):
        half = self.n_dma // 2
        if q == "pool":
            s = half + self.dma_rr_pool
            self.dma_rr_pool = (self.dma_rr_pool + 1) % half
        else:
            s = self.dma_rr
            self.dma_rr = (self.dma_rr + 1) % half
        key = ("dma", s)
        waits = self._gather(q, R, W, WP)
        prev = 16 * self.dma_cnt[s]
        if prev > 0 and self.known[q].get(key, 0) < prev:
            self.known[q][key] = prev
            waits.append((key, prev))
        self.dma_cnt[s] += 1
        val = 16 * self.dma_cnt[s]

        def fn(e, out=out, in_=in_):
            return e.dma_start(out=out, in_=in_)
        self.ops[q].append((fn, waits, (key, 16)))
        self._record((key, val), R, W, WP)
        self.n_ops += 1

    def barrier(self):
        for e in self.ALLQ:
            waits = []
            kn = self.known[e]
            for c in self.CE:
                if c == e:
                    continue
                v = self.count[c]
                if v > 0 and kn.get(c, 0) < v:
                    kn[c] = v
                    waits.append((c, v))
            for i in range(self.n_dma):
                v = 16 * self.dma_cnt[i]
                k = ("dma", i)
                if v > 0 and kn.get(k, 0) < v:
                    kn[k] = v
                    waits.append((k, v))
            if waits:
                self.ops[e].append((None, waits, None))

    def emit(self):
        nc = self.nc
        sems = self.sems
        ops = self.ops

        def run(e, lst):
            for fn, waits, inc in lst:
                for k, v in waits:
                    e.wait_ge(sems[k], v)
                if fn is None:
                    continue
                ins = fn(e)
                if inc is not None:
                    ins.then_inc(sems[inc[0]], inc[1])

        with nc.Block() as blk:
            @blk.sync
            def _(e):
                run(e, ops["sp"])

            @blk.tensor
            def _(e):
                run(e, ops["pe"])

            @blk.scalar
            def _(e):
                run(e, ops["act"])

            @blk.vector
            def _(e):
                run(e, ops["dve"])

            @blk.gpsimd
            def _(e):
                run(e, ops["pool"])


class Arena:
    def __init__(self, nc, nbytes):
        self.t = nc.alloc_sbuf_tensor("arena", [128, nbytes], U8)
        self.n = nbytes
        self.off = 0

    def reset(self, off=0):
        self.off = off

    def alloc(self, free_shape, dtype):
        n = int(np.prod(free_shape))
        nb = n * mybir.dt.size(dtype)
        nb = (nb + 63) // 64 * 64
        assert self.off + nb <= self.n, ("arena overflow", self.off, nb, self.n)
        ap = self.t[:, self.off:self.off + nb].bitcast(dtype)[:, 0:n]
        self.off += nb
        if len(free_shape) == 2:
            ap = ap.rearrange("p (a b) -> p a b", a=free_shape[0])
        elif len(free_shape) == 3:
            ap = ap.rearrange("p (a b c) -> p a b c", a=free_shape[0], b=free_shape[1])
        return ap


def _mask(kk, W):
    kp = np.arange(128)[:, None]
    qf = np.arange(512)[None, :]
    dlt = qf - 128 * kk - kp
    ok = dlt >= 0
    if W is not None:
        ok &= dlt < W
    return np.where(ok, 0.0, NEG).astype(np.float32)


MASK_CAUSAL = {kk: kk for kk in range(4)}
MASK_SWA = {kk: 4 + (kk + 1) for kk in range(-1, 4)}
MASK_WIN = {kk: 9 + (kk + 4) for kk in range(-4, 0)}
for _kk in range(4):
    MASK_WIN[_kk] = MASK_CAUSAL[_kk]
NMASK = 13


def _consts():
    bf = ml_dtypes.bfloat16
    masks = np.zeros((128, NMASK, 512), np.float32)
    for kk in range(4):
        masks[:, MASK_CAUSAL[kk]] = _mask(kk, None)
    for kk in range(-1, 4):
        masks[:, MASK_SWA[kk]] = _mask(kk, 128)
    for kk in range(-4, 0):
        masks[:, MASK_WIN[kk]] = _mask(kk, 512)
    n = np.arange(256)[:, None]
    q = np.arange(S)[None, :]
    cm = np.where(16 * n + 31 <= q, 0.0, NEG).astype(np.float32)
    cmask = np.concatenate([cm[0:128, 0:2560], cm[128:256, 2048:4096]], axis=1)
    ov = np.zeros((256, 64), np.float32)
    for nn in range(255):
        a0, a1 = 16 * nn, 16 * nn + 32
        for s in range(64):
            o = min(a1, 64 * s + 64) - max(a0, 64 * s)
            if o > 0:
                ov[nn, s] = o / 32.0
    ovl = np.stack([ov[0:128], ov[128:256]], axis=1)
    ebig = np.zeros((128, S), np.float32)
    ebig[64 + (np.arange(S) // 64), np.arange(S)] = 1.0
    qq = np.arange(S)
    qb = qq // 64
    s = np.arange(64)[None, :]
    causal = s <= qb[:, None]
    forced = (s == 0) | (s == qb[:, None]) | (s == qb[:, None] - 1)
    cmul = causal.astype(np.float32)
    cadd = np.where(causal, np.where(forced, 1e4, 0.0), -1.0).astype(np.float32)
    cmul = cmul.reshape(32, 128, 64).transpose(1, 0, 2)
    cadd = cadd.reshape(32, 128, 64).transpose(1, 0, 2)
    sel3 = np.zeros((128, 48, 64), np.float32)
    for r in range(48):
        sel3[r, r, :] = 1.0
    half = 32
    inv = (10000.0 ** (-np.arange(half, dtype=np.float32) / half)).astype(np.float32)
    invf = np.tile(inv, 4)[:, None].astype(np.float32)
    sgn = np.where((np.arange(128) % 64) < 32, -1.0, 1.0).astype(np.float32)[:, None]
    return {
        "c_ident": np.eye(128, dtype=np.float32),
        "c_masks": masks.astype(bf),
        "c_cmask": cmask.astype(bf),
        "c_ovl": ovl.astype(bf),
        "c_ebig": ebig.astype(bf),
        "c_cmul": np.ascontiguousarray(cmul).astype(bf),
        "c_cadd": np.ascontiguousarray(cadd).astype(bf),
        "c_sel3": sel3.astype(bf),
        "c_selst": np.where((np.arange(128)[:, None] - 64) <= (np.arange(1024)[None, :] // 64), 0.0, NEG).astype(np.float32).astype(bf),
        "c_invf": invf,
        "c_sgn": sgn,
    }


def _swap64(w):
    k, n = w.shape
    return np.ascontiguousarray(w.reshape(k, n // 64, 2, 32)[:, :, ::-1, :].reshape(k, n))


def _fm(v, nchunk):
    return np.ascontiguousarray(np.asarray(v, np.float32).reshape(nchunk, 128).T)


MLA_COLS = dict(qa=0, kva=256, kpe=384, kpes=448, gate=512, n=1536)
SWA_COLS = dict(q=0, qs=1024, k=2048, ks=2176, v=2304, gate=2432, n=3456)
NSA_COLS = dict(q=0, qs=1024, kslc=2048, kslcs=2304, kwin=2560, kwins=2816, kcmp=3072, vcmp=3328,
                vslc=3584, vwin=3840, gbr=4096, gate=4144, n=5168)


def _prep_weights(inp):
    out = {}
    for j in range(2):
        w = inp["mla_w_in"][j]
        kpe = w[:, 384:448]
        out["mla%d_wa" % j] = np.ascontiguousarray(np.concatenate(
            [w[:, 0:256], w[:, 256:384], kpe, _swap64(kpe), w[:, 448:1472]], axis=1))
        qb = inp["mla_w_q_b"][j].reshape(256, 8, 192)
        out["mla%d_wqn" % j] = np.ascontiguousarray(qb[:, :, 0:128].reshape(256, 1024))
        qr = np.ascontiguousarray(qb[:, :, 128:192].reshape(256, 512))
        out["mla%d_wqr" % j] = qr
        out["mla%d_wqrs" % j] = _swap64(qr)
        kvb = inp["mla_w_kv_b"][j].reshape(128, 8, 256)
        out["mla%d_wkn" % j] = np.ascontiguousarray(kvb[:, :, 0:128].reshape(128, 1024))
        out["mla%d_wv" % j] = np.ascontiguousarray(kvb[:, :, 128:256].reshape(128, 1024))
        out["mla%d_wout" % j] = np.ascontiguousarray(inp["mla_w_out"][j])
        out["mla%d_qg" % j] = _fm(inp["mla_q_norm_g"][j], 2)
        out["mla%d_kvg" % j] = _fm(inp["mla_kv_norm_g"][j], 1)
    w = inp["swa_w_in"][0]
    q, k, v, g = w[:, 0:1024], w[:, 1024:1152], w[:, 1152:1280], w[:, 1280:2304]
    out["swa_wa"] = np.ascontiguousarray(np.concatenate([q, _swap64(q), k, _swap64(k), v, g], axis=1))
    out["swa_wout"] = np.ascontiguousarray(inp["swa_w_out"][0])
    out["swa_sinks"] = np.ascontiguousarray(inp["swa_sinks"][0].reshape(1, 16).astype(np.float32))
    w = inp["nsa_w_in"][0]
    q = w[:, 0:1024]
    kcmp, vcmp, kslc, vslc, kwin, vwin = [w[:, 1024 + 256 * i:1280 + 256 * i] for i in range(6)]
    gbr = w[:, 2560:2608]
    g = w[:, 2608:3632]
    out["nsa_wa"] = np.ascontiguousarray(np.concatenate(
        [q, _swap64(q), kslc, _swap64(kslc), kwin, _swap64(kwin), kcmp, vcmp, vslc, vwin, gbr, g], axis=1))
    out["nsa_wout"] = np.ascontiguousarray(inp["nsa_w_out"][0])
    out["nsa_pe"] = np.ascontiguousarray(inp["nsa_cmp_pos"][0].reshape(16, 128).T.astype(np.float32))
    out["nsa_wk1"] = np.ascontiguousarray(inp["nsa_w_cmp_k1"][0])
    out["nsa_wk2"] = np.ascontiguousarray(inp["nsa_w_cmp_k2"][0])
    out["nsa_wv1"] = np.ascontiguousarray(inp["nsa_w_cmp_v1"][0])
    out["nsa_wv2"] = np.ascontiguousarray(inp["nsa_w_cmp_v2"][0])
    out["ada_w"] = np.ascontiguousarray(inp["ada_w"])
    out["ada_b"] = np.ascontiguousarray(inp["ada_b"].reshape(4, 24, 128).transpose(2, 0, 1))
    out["norm_g"] = np.ascontiguousarray(inp["norm_g"].reshape(4, 8, 128).transpose(2, 0, 1))
    out["final_g"] = np.ascontiguousarray(inp["final_norm_g"].reshape(1, 1024))
    return out


class K:
    pass


def build(n_layers=DEPTH, shapes=None):
    nc = bass.Bass("TRN2", target_bir_lowering=False)
    P = Prog(nc)
    k = K()
    k.nc, k.P = nc, P
    dram_in = {}

    def din(name, shape, dt):
        dram_in[name] = nc.dram_tensor(name, list(shape), dt, kind="ExternalInput").ap()
        return dram_in[name]

    for name, (shape, dt) in shapes.items():
        din(name, shape, dt)
    k.din = dram_in
    out = nc.dram_tensor("out", [S, D], F32, kind="ExternalOutput").ap()
    k.out = out
    k.xr = nc.dram_tensor("xr", [S, D], F32).ap()
    k.QT = nc.dram_tensor("QT", [1536, S], BF16).ap()
    k.KT = nc.dram_tensor("KT", [1280, S], BF16).ap()
    k.VT = nc.dram_tensor("VT", [8, 128, NT, 128], BF16).ap()
    k.VS = nc.dram_tensor("VS", [8, 128, NT, 64], BF16).ap()
    k.GT = nc.dram_tensor("GT", [1024 + 128, S], BF16).ap()
    k.OT = nc.dram_tensor("OT", [1024, S], BF16).ap()
    k.OC = nc.dram_tensor("OC", [1024, S], F32).ap()
    k.modrow = nc.dram_tensor("modrow", [4, 3072], F32).ap()
    k.xr_b = [Buf() for _ in range(NT)]
    k.QT_b, k.KT_b, k.VT_b, k.GT_b, k.OT_b, k.OC_b, k.modrow_b, k.out_b, k.VS_b = [Buf() for _ in range(9)]

    sb = nc.alloc_sbuf_tensor
    k.ident = sb("ident", [128, 128], F32); k.ident_b = Buf()
    k.identb = sb("identb", [128, 128], BF16); k.identb_b = Buf()
    k.onesb = sb("onesb", [128, 128], BF16); k.onesb_b = Buf()
    k.onesf = sb("onesf", [128, 128], F32); k.onesf_b = Buf()
    k.masks = sb("masks", [128, NMASK, 512], BF16); k.masks_b = Buf()
    k.ropeD = nc.dram_tensor("ropeD", [2, 128, S], F32).ap()
    k.rope_b = Buf()
    k.mod = sb("mod", [128, 4, 24], F32); k.mod_b = Buf()
    k.gmod = sb("gmod", [128, 4, 8], F32); k.gmod_b = Buf()
    k.ps = [nc.alloc_psum_tensor("ps%d" % i, [128, 512], F32) for i in range(8)]
    k.ps_b = [Buf() for _ in range(8)]
    k.arena = Arena(nc, 160 * 1024)

    prologue(k)
    import os
    kinds = os.environ.get("K_KINDS", "mla,swa,nsa,mla").split(",")
    for i in range(n_layers):
        kind = kinds[i]
        j = i // 3
        last = (i == n_layers - 1)
        if kind == "mla":
            mla_layer(k, i, j, last)
        elif kind == "swa":
            swa_layer(k, i, last)
        else:
            nsa_layer(k, i, last)
    P.barrier()
    P.emit()
    return nc, P


def prologue(k):
    P, nc, A = k.P, k.nc, k.arena
    din = k.din
    A.reset()
    P.dma("sp", k.ident[:], din["c_ident"], W=[k.ident_b])
    P.dma("sp", k.masks[:], din["c_masks"], W=[k.masks_b])
    P.op("pool", lambda e: e.tensor_copy(out=k.identb[:], in_=k.ident[:]), R=[k.ident_b], W=[k.identb_b])
    P.op("pool", lambda e: e.memset(k.onesb[:], 1.0), W=[k.onesb_b])
    P.op("pool", lambda e: e.memset(k.onesf[:], 1.0), W=[k.onesf_b])
    posi = A.alloc([S], I32); posi_b = Buf()
    ang = A.alloc([S], F32); ang_b = Buf()
    kf = A.alloc([S], F32); kf_b = Buf()
    ki = A.alloc([S], I32); ki_b = Buf()
    rr = A.alloc([S], F32); rr_b = Buf()
    ivf = A.alloc([1], F32); ivf_b = Buf()
    sgn = A.alloc([1], F32); sgn_b = Buf()
    P.dma("sp", posi, din["pos"].partition_broadcast(128), W=[posi_b])
    P.dma("sp", ivf, din["c_invf"], W=[ivf_b])
    P.dma("sp", sgn, din["c_sgn"], W=[sgn_b])
    P.op("dve", lambda e: e.tensor_copy(out=kf, in_=posi), R=[posi_b], W=[kf_b])
    P.op("dve", lambda e: e.tensor_scalar(out=ang, in0=kf, scalar1=ivf[:, 0:1], scalar2=None, op0=ALU.mult),
         R=[kf_b, ivf_b], W=[ang_b])
    TWO_PI = 2 * np.pi
    c1 = float(np.float32(6.28125))
    c2 = float(TWO_PI - 6.28125)
    for dst, shift, use_sgn in ((0, np.pi / 2, False), (1, 0.0, True)):
        P.op("dve", lambda e, shift=shift: e.tensor_scalar(out=kf, in0=ang, scalar1=float(shift), scalar2=float(1.0 / TWO_PI),
                                                          op0=ALU.add, op1=ALU.mult), R=[ang_b], W=[kf_b])
        P.op("dve", lambda e: e.tensor_copy(out=ki, in_=kf), R=[kf_b], W=[ki_b])
        P.op("dve", lambda e: e.tensor_copy(out=kf, in_=ki), R=[ki_b], W=[kf_b])
        P.op("dve", lambda e: e.scalar_tensor_tensor(out=rr, in0=kf, scalar=-c1, in1=ang, op0=ALU.mult, op1=ALU.add),
             R=[kf_b, ang_b], W=[rr_b])
        P.op("dve", lambda e: e.scalar_tensor_tensor(out=rr, in0=kf, scalar=-c2, in1=rr, op0=ALU.mult, op1=ALU.add),
             R=[kf_b, rr_b], W=[rr_b])
        P.op("dve", lambda e, shift=shift: e.tensor_scalar(out=kf, in0=rr, scalar1=float(shift), scalar2=float(np.pi),
                                                          op0=ALU.add, op1=ALU.is_gt), R=[rr_b], W=[kf_b])
        P.op("dve", lambda e, shift=shift: e.tensor_scalar(out=rr, in0=rr, scalar1=float(shift), scalar2=None, op0=ALU.add),
             R=[rr_b], W=[rr_b])
        P.op("dve", lambda e: e.scalar_tensor_tensor(out=rr, in0=kf, scalar=-TWO_PI, in1=rr, op0=ALU.mult, op1=ALU.add),
             R=[kf_b, rr_b], W=[rr_b])
        P.op("dve", lambda e: e.tensor_scalar(out=kf, in0=rr, scalar1=float(-np.pi), scalar2=None, op0=ALU.is_lt),
             R=[rr_b], W=[kf_b])
        P.op("dve", lambda e: e.scalar_tensor_tensor(out=rr, in0=kf, scalar=TWO_PI, in1=rr, op0=ALU.mult, op1=ALU.add),
             R=[kf_b, rr_b], W=[rr_b])
        P.op("dve", lambda e: e.tensor_scalar(out=rr, in0=rr, scalar1=3.141592, scalar2=-3.141592, op0=ALU.min, op1=ALU.max),
             R=[rr_b], W=[rr_b])
        if use_sgn:
            P.op("act", lambda e: e.activation(out=rr, in_=rr, func=AF.Sin, scale=sgn[:, 0:1]),
                 R=[rr_b, sgn_b], W=[rr_b])
        else:
            P.op("act", lambda e: e.activation(out=rr, in_=rr, func=AF.Sin), R=[rr_b], W=[rr_b])
        P.dma("sp", k.ropeD[dst], rr, R=[rr_b], WP=[k.rope_b])
    P.barrier()
    A.reset()
    cfm = A.alloc([8], F32); cfm_b = Buf()
    cond = A.alloc([8], F32); cond_b = Buf()
    adab = A.alloc([4, 24], F32); adab_b = Buf()
    ng = A.alloc([4, 8], F32); ng_b = Buf()
    P.dma("sp", cfm, din["c_fm"], W=[cfm_b])
    P.dma("sp", adab, din["ada_b"], W=[adab_b])
    P.dma("sp", ng, din["norm_g"], W=[ng_b])
    P.op("act", lambda e: e.activation(out=cond, in_=cfm, func=AF.Silu), R=[cfm_b], W=[cond_b])
    wst = [A.alloc([8, 512], F32) for _ in range(2)]
    wst_b = [Buf() for _ in range(2)]
    modT = A.alloc([4, 128], F32); modT_b = Buf()
    n = 0
    for i in range(DEPTH):
        pm = k.ps[i % 2]
        pm_b = k.ps_b[i % 2]
        for mg in range(6):
            b = n % 2
            n += 1
            P.dma("sp", wst[b], din["ada_w"][i, :, mg * 512:(mg + 1) * 512].rearrange("(c p) n -> p c n", p=128), W=[wst_b[b]])
            for m4 in range(4):
                m = mg * 4 + m4
                for c in range(8):
                    P.op("pe", lambda e, b=b, m4=m4, m=m, c=c, pm=pm: e.matmul(
                        pm[:, m:m + 1], lhsT=wst[b][:, c, m4 * 128:(m4 + 1) * 128], rhs=cond[:, c:c + 1],
                        start=(c == 0), stop=(c == 7)),
                        R=[wst_b[b], cond_b], WP=[pm_b], inc=(c == 7))
        P.op("dve", lambda e, i=i, pm=pm: e.tensor_tensor(out=k.mod[:, i, :], in0=pm[:, 0:24], in1=adab[:, i, :], op=ALU.add),
             R=[pm_b, adab_b], WP=[k.mod_b])
        P.op("dve", lambda e, i=i: e.scalar_tensor_tensor(out=k.gmod[:, i, :], in0=k.mod[:, i, 8:16], scalar=1.0, in1=ng[:, i, :],
                                                         op0=ALU.add, op1=ALU.mult), R=[k.mod_b, ng_b], WP=[k.gmod_b])
        pT, pT_b = k.ps[2 + i % 2], k.ps_b[2 + i % 2]
        P.op("pe", lambda e, i=i, pT=pT: e.transpose(out=pT[0:24, 0:128], in_=k.mod[:, i, :], identity=k.ident[:]),
             R=[k.mod_b, k.ident_b], W=[pT_b])
        P.op("act", lambda e, i=i, pT=pT: e.activation(out=modT[0:24, i, :], in_=pT[0:24, 0:128], func=AF.Copy), R=[pT_b], WP=[modT_b])
        P.dma("sp", k.modrow[i].rearrange("(c p) -> c p", p=128), modT[0:24, i, :], R=[modT_b], WP=[k.modrow_b])
    P.barrier()


def load_cast_weights(k, dst, src, ncols, kchunks, stg, stg_b, dst_b, cw=256):
    P = k.P
    n = 0
    for c0 in range(0, ncols, cw):
        w = min(cw, ncols - c0)
        b = k.stg_n % 2
        k.stg_n += 1
        P.dma("sp", stg[b][:, 0:kchunks, 0:w], src[:, c0:c0 + w].rearrange("(c p) n -> p c n", p=128), W=[stg_b[b]])
        if n % 2 == 0:
            P.op("act", lambda e, b=b, c0=c0, w=w: e.activation(out=dst[:, :, c0:c0 + w], in_=stg[b][:, 0:kchunks, 0:w], func=AF.Copy),
                 R=[stg_b[b]], WP=[dst_b])
        else:
            P.op("dve", lambda e, b=b, c0=c0, w=w: e.tensor_copy(out=dst[:, :, c0:c0 + w], in_=stg[b][:, 0:kchunks, 0:w]),
                 R=[stg_b[b]], WP=[dst_b])
        n += 1


class Group:
    def __init__(self, k, a, dst_rows, dst_b, nch, rows_last=128):
        i = a.grp_n % 2
        a.grp_n += 1
        self.k, self.buf, self.b = k, a.gst[i], a.gst_b[i]
        self.dst_rows, self.dst_b, self.nch, self.rows_last = dst_rows, dst_b, nch, rows_last

    def slot(self, j):
        return self.buf[:, j, :]

    def flush(self):
        P = self.k.P
        if self.rows_last == 128:
            P.dma("sp", self.dst_rows.rearrange("(c p) t -> p c t", p=128), self.buf[:, 0:self.nch, :], R=[self.b], WP=[self.dst_b])
        else:
            assert self.nch == 1
            P.dma("sp", self.dst_rows, self.buf[0:self.rows_last, 0, :], R=[self.b], WP=[self.dst_b])


def phase_a_common(k, li, vshape):
    A, P = k.arena, k.P
    a = K()
    a.xt4s = [A.alloc([4, D], F32) for _ in range(2)]
    a.xt4_bss = [[Buf(), Buf()], [Buf(), Buf()]]
    xflat = a.xt4s[1].rearrange("p s d -> p (s d)")
    a.stg = [xflat[:, i * 2048:(i + 1) * 2048].rearrange("p (c n) -> p c n", c=8) for i in range(2)]
    a.stg_b = a.xt4_bss[1]
    a.junk = A.alloc([D], F32); a.junk_b = Buf()
    a.ss = A.alloc([16], F32); a.ss_b = Buf()
    a.hT = [A.alloc([8, 512], BF16) for _ in range(2)]
    a.hT_b = [Buf() for _ in range(2)]
    a.gst = [A.alloc([4, 512], BF16) for _ in range(2)]
    a.gst_b = [Buf() for _ in range(2)]
    a.grp_n = 0
    a.vst = A.alloc([vshape[0], 4, vshape[1]], BF16); a.vst_b = Buf()
    a.rt = [A.alloc([512], F32) for _ in range(2)]
    a.rt_b = [Buf() for _ in range(2)]
    a.rcs = [A.alloc([2, 512], F32) for _ in range(2)]
    a.rcs_b = [Buf() for _ in range(2)]
    k.stg_n = 0
    a.mm_n = 0
    a.tr_n = 0
    return a


def hT_load(k, a, st, xsrc, xsrc_b):
    P = k.P
    hb = st % 2
    P.dma("sp", a.xt4s[hb], xsrc[st * 512:(st + 1) * 512, :].rearrange("(s p) d -> p s d", p=128),
          R=[xsrc_b[st * 4 + s_] for s_ in range(4)], W=list(a.xt4_bss[hb]))


def hT_front(k, a, st):
    P = k.P
    hb = st % 2
    xt4, xt4_bs, ss, ss_b = a.xt4s[hb], a.xt4_bss[hb], a.ss, a.ss_b
    for sub in range(4):
        P.op("act", lambda e, sub=sub: e.activation(out=a.junk, in_=xt4[:, sub, :], func=AF.Square, accum_out=ss[:, sub:sub + 1]),
             R=list(xt4_bs), W=[a.junk_b], WP=[ss_b])
    P.op("dve", lambda e: e.tensor_scalar(out=ss[:, 4:8], in0=ss[:, 0:4], scalar1=1.0 / D, scalar2=EPS, op0=ALU.mult, op1=ALU.add),
         R=[ss_b], WP=[ss_b])
    P.op("act", lambda e: e.activation(out=ss[:, 8:12], in_=ss[:, 4:8], func=AF.Sqrt), R=[ss_b], WP=[ss_b])
    P.op("dve", lambda e: e.reciprocal(out=ss[:, 12:16], in_=ss[:, 8:12]), R=[ss_b], WP=[ss_b])
    for sub in range(4):
        P.op("dve", lambda e, sub=sub: e.tensor_scalar(out=xt4[:, sub, :], in0=xt4[:, sub, :], scalar1=ss[:, 12 + sub:13 + sub], scalar2=None, op0=ALU.mult),
             R=[ss_b], WP=list(xt4_bs))


def hT_back(k, a, li, st):
    P = k.P
    hb = st % 2
    hT, hT_b = a.hT[hb], a.hT_b[hb]
    xt4, xt4_bs = a.xt4s[hb], a.xt4_bss[hb]
    for sub in range(4):
        for half in range(2):
            pi = 4 + a.tr_n % 4
            a.tr_n += 1
            pt, pt_b = k.ps[pi], k.ps_b[pi]
            for c4 in range(4):
                c = half * 4 + c4
                P.op("pe", lambda e, sub=sub, c=c, c4=c4, pt=pt: e.transpose(out=pt[:, c4 * 128:(c4 + 1) * 128], in_=xt4[:, sub, c * 128:(c + 1) * 128], identity=k.ident[:]),
                     R=list(xt4_bs) + [k.ident_b], W=[pt_b], inc=(c4 == 3))
            for c4 in range(4):
                c = half * 4 + c4
                o = hT[:, c, sub * 128:(sub + 1) * 128]
                i_ = pt[:, c4 * 128:(c4 + 1) * 128]
                if c % 2 == 0:
                    P.op("act", lambda e, o=o, i_=i_, c=c: e.activation(out=o, in_=i_, func=AF.Identity, scale=k.gmod[:, li, c:c + 1], bias=k.mod[:, li, c:c + 1]),
                         R=[pt_b, k.gmod_b, k.mod_b], WP=[hT_b])
                else:
                    P.op("dve", lambda e, o=o, i_=i_, c=c: e.tensor_scalar(out=o, in0=i_, scalar1=k.gmod[:, li, c:c + 1], scalar2=k.mod[:, li, c:c + 1], op0=ALU.mult, op1=ALU.add),
                         R=[pt_b, k.gmod_b, k.mod_b], WP=[hT_b])
    return hT, hT_b, a.rcs[hb], a.rcs_b[hb]


def rcs_load(k, a, st):
    hb = st % 2
    k.P.dma("sp", a.rcs[hb], k.ropeD[:, :, st * 512:(st + 1) * 512].rearrange("j p t -> p j t"), R=[k.rope_b], W=[a.rcs_b[hb]])


def hT_prime(k, a, li, xsrc, xsrc_b):
    rcs_load(k, a, 0)
    hT_load(k, a, 0, xsrc, xsrc_b)
    hT_load(k, a, 1, xsrc, xsrc_b)
    hT_front(k, a, 0)
    return hT_back(k, a, li, 0)


def hT_step_begin(k, a, st, xsrc, xsrc_b):
    if st + 1 < NST:
        rcs_load(k, a, st + 1)
        hT_front(k, a, st + 1)


def hT_step_mid(k, a, li, st, xsrc, xsrc_b):
    nxt = None
    if st + 1 < NST:
        nxt = hT_back(k, a, li, st + 1)
    if st + 2 < NST:
        hT_load(k, a, st + 2, xsrc, xsrc_b)
    return nxt


def mm_psum(k, a):
    i = a.mm_n % 4
    a.mm_n += 1
    return k.ps[i], k.ps_b[i]


def fm_group(k, pm, pm_b, w, w_b, kch, col0, ncols, rhs_fn, rhs_b):
    P = k.P
    for c in range(kch):
        P.op("pe", lambda e, c=c: e.matmul(pm[0:ncols, :], lhsT=w[:, c, col0:col0 + ncols], rhs=rhs_fn(c), start=(c == 0), stop=(c == kch - 1)),
             R=[w_b] + list(rhs_b), W=[pm_b], inc=(c == kch - 1))


def job_fm(k, a, w, w_b, kch, col0, ncols, rhs_fn, rhs_b, out, out_b, act=None, eng="act"):
    P = k.P
    pm, pm_b = mm_psum(k, a)
    fm_group(k, pm, pm_b, w, w_b, kch, col0, ncols, rhs_fn, rhs_b)
    o = out[0:ncols, :]
    if act is not None:
        P.op("act", lambda e: e.activation(out=o, in_=pm[0:ncols, :], func=act), R=[pm_b], WP=[out_b])
    elif eng == "act":
        P.op("act", lambda e: e.activation(out=o, in_=pm[0:ncols, :], func=AF.Copy), R=[pm_b], WP=[out_b])
    else:
        P.op("dve", lambda e: e.tensor_copy(out=o, in_=pm[0:ncols, :]), R=[pm_b], WP=[out_b])


def job_rope(k, a, w, w_b, kch, col0, w2, w2_b, cols0, ncols, rhs_fn, rhs_b, rcs, rcs_b, out, out_b):
    P = k.P
    pm, pm_b = mm_psum(k, a)
    fm_group(k, pm, pm_b, w, w_b, kch, col0, ncols, rhs_fn, rhs_b)
    pm2, pm2_b = mm_psum(k, a)
    fm_group(k, pm2, pm2_b, w2, w2_b, kch, cols0, ncols, rhs_fn, rhs_b)
    r0, r0_b, r1, r1_b = a.rt[0], a.rt_b[0], a.rt[1], a.rt_b[1]
    P.op("dve", lambda e: e.tensor_tensor(out=r0[0:ncols, :], in0=pm[0:ncols, :], in1=rcs[0:ncols, 0, :], op=ALU.mult),
         R=[pm_b, rcs_b], W=[r0_b])
    P.op("dve", lambda e: e.tensor_tensor(out=r1[0:ncols, :], in0=pm2[0:ncols, :], in1=rcs[0:ncols, 1, :], op=ALU.mult),
         R=[pm2_b, rcs_b], W=[r1_b])
    P.op("pool", lambda e: e.tensor_tensor(out=out[0:ncols, :], in0=r0[0:ncols, :], in1=r1[0:ncols, :], op=ALU.add),
         R=[r0_b, r1_b], WP=[out_b])


def job_tm(k, a, w, w_b, kch, col0, ncols, lhs_fn, lhs_b, out, out_b):
    P = k.P
    pm, pm_b = mm_psum(k, a)
    for c in range(kch):
        P.op("pe", lambda e, c=c: e.matmul(pm[:, 0:ncols], lhsT=lhs_fn(c), rhs=w[:, c, col0:col0 + ncols], start=(c == 0), stop=(c == kch - 1)),
             R=[w_b] + list(lhs_b), W=[pm_b], inc=(c == kch - 1))
    dv = out.shape[-1]
    P.op("act", lambda e: e.activation(out=out, in_=pm[:, 0:ncols].rearrange("p (h d) -> p h d", d=dv), func=AF.Copy), R=[pm_b], WP=[out_b])


def band_cols(kk, W):
    lo = max(0, 128 * kk)
    hi = 512 if W is None else min(512, 128 * kk + 127 + W)
    return (lo, hi)


def attention_rows(k, rows, pt_bufs, scale, s_banks=(0, 1, 2), look=2):
    P = k.P
    steps = []
    pending = []
    for ri, row in enumerate(rows):
        n = len(row["ksteps"])
        for si, stp in enumerate(row["ksteps"]):
            steps.append((ri, si, n, stp))

    def issue_s(idx):
        ri, si, n, stp = steps[idx]
        sb_i = s_banks[idx % len(s_banks)]
        ps, ps_b = k.ps[sb_i], k.ps_b[sb_i]
        kp = stp.get("kp", 128)
        c0, c1 = stp.get("cols", (0, 512))
        ns = len(stp["s"])
        for j, (lhsT, rhs, bufs) in enumerate(stp["s"]):
            P.op("pe", lambda e, lhsT=lhsT, rhs=rhs, j=j, ps=ps, kp=kp, c0=c0, c1=c1: e.matmul(ps[0:kp, c0:c1], lhsT=lhsT, rhs=rhs[:, c0:c1], start=(j == 0), stop=(j == ns - 1)),
                 R=list(bufs), W=[ps_b], inc=(j == ns - 1))
        pt, pt_b = pt_bufs[idx % len(pt_bufs)]
        P.op("act", lambda e, pt=pt, ps=ps, kp=kp, c0=c0, c1=c1: e.activation(out=pt[0:kp, c0:c1], in_=ps[0:kp, c0:c1], func=AF.Exp, scale=float(scale)),
             R=[ps_b], W=[pt_b])

    def issue_pv(idx):
        ri, si, n, stp = steps[idx]
        pt, pt_b = pt_bufs[idx % len(pt_bufs)]
        kp = stp.get("kp", 128)
        c0, c1 = stp.get("cols", (0, 512))
        nob = rows[ri].get("o_banks", 2)
        par = ri % nob
        dacc = rows[ri].get("dacc")
        if dacc is not None:
            acc_t, acc_b_ = dacc
            if si == 0:
                P.op("dve", lambda e, pt=pt, kp=kp, c0=c0, c1=c1: e.tensor_copy(out=acc_t[0:kp, c0:c1], in_=pt[0:kp, c0:c1]), R=[pt_b], W=[acc_b_])
            else:
                P.op("dve", lambda e, pt=pt, kp=kp, c0=c0, c1=c1: e.tensor_tensor(out=acc_t[0:kp, c0:c1], in0=acc_t[0:kp, c0:c1], in1=pt[0:kp, c0:c1], op=ALU.add),
                     R=[pt_b], WP=[acc_b_])
        for (which, lhsT, bufs, mrows) in stp["pv"]:
            pi = 7 if which == "A" else (3 + par if which == "O" else 5 + ri % 2)
            po, po_b = k.ps[pi], k.ps_b[pi]
            P.op("pe", lambda e, lhsT=lhsT, po=po, pt=pt, kp=kp, mrows=mrows, si=si, n=n, c0=c0, c1=c1: e.matmul(po[0:mrows, c0:c1], lhsT=lhsT, rhs=pt[0:kp, c0:c1], start=(si == 0), stop=(si == n - 1), skip_group_check=True),
                 R=list(bufs) + [pt_b], W=[po_b], inc=True)
        if si == 0:
            pending.extend(rows[ri].get("pre", ()))
        if si == n - 1:
            pending[:] = [p_ for p_ in pending if p_ is not None]
            while pending:
                pending.pop(0)()
            ret = rows[ri]["epilogue"](k.ps[3 + par], k.ps_b[3 + par], k.ps[5 + ri % 2], k.ps_b[5 + ri % 2])
            if ret:
                pending.extend(ret)
        elif pending:
            p_ = pending.pop(0)
            if p_ is not None:
                p_()

    N = len(steps)
    if N == 0:
        return
    assert len(pt_bufs) >= look + 1 and len(s_banks) >= look + 1
    for j in range(min(look, N)):
        issue_s(j)
    for idx in range(N):
        if idx + look < N:
            issue_s(idx + look)
        issue_pv(idx)
    while pending:
        p_ = pending.pop(0)
        if p_ is not None:
            p_()


def phase_c(k, li, wout_name, last):
    P, nc, A, din = k.P, k.nc, k.arena, k.din
    P.barrier()
    A.reset()
    wo = A.alloc([8, D], BF16); wo_b = Buf()
    stg = [A.alloc([8, 256], F32) for _ in range(2)]
    stg_b = [Buf() for _ in range(2)]
    k.stg_n = 0
    load_cast_weights(k, wo, din[wout_name], D, 8, stg, stg_b, wo_b)
    gbc = A.alloc([D], F32); gbc_b = Buf()
    P.dma("sp", gbc, k.modrow[li:li + 1, 2048:3072].partition_broadcast(128), R=[k.modrow_b], W=[gbc_b])
    if last:
        fg = A.alloc([D], F32); fg_b = Buf()
        P.dma("sp", fg, din["final_g"].partition_broadcast(128), W=[fg_b])
    ots = [A.alloc([8, 512], BF16) for _ in range(2)]
    ots_b = [Buf() for _ in range(2)]
    xt = [A.alloc([D], F32) for _ in range(2)]
    xt_b = [Buf() for _ in range(2)]
    xo = [A.alloc([D], F32) for _ in range(2)]
    xo_b = [Buf() for _ in range(2)]
    junk = A.alloc([D], F32); junk_b = Buf()
    ss = [A.alloc([4], F32) for _ in range(2)]
    ss_b = [Buf() for _ in range(2)]
    xsrc = din["x_in"] if li == 0 else k.xr
    n = 0
    for st in range(NST):
        ob = st % 2
        P.dma("sp", ots[ob], k.OT[:, st * 512:(st + 1) * 512].rearrange("(c p) t -> p c t", p=128), R=[k.OT_b], W=[ots_b[ob]])
        for sub in range(4):
            t = st * 4 + sub
            b = t % 2
            P.dma("sp", xt[b], xsrc[t * 128:(t + 1) * 128, :], R=[k.xr_b[t]], W=[xt_b[b]])
            for half in range(2):
                pm, pm_b = k.ps[n % 4], k.ps_b[n % 4]
                n += 1
                for c in range(8):
                    P.op("pe", lambda e, c=c, ob=ob, sub=sub, half=half, pm=pm: e.matmul(pm[:, :], lhsT=ots[ob][:, c, sub * 128:(sub + 1) * 128], rhs=wo[:, c, half * 512:(half + 1) * 512], start=(c == 0), stop=(c == 7)),
                         R=[ots_b[ob], wo_b], W=[pm_b], inc=(c == 7))
                hs = slice(half * 512, (half + 1) * 512)
                P.op("dve", lambda e, b=b, hs=hs, pm=pm: e.tensor_tensor(out=xo[b][:, hs], in0=pm[:, :], in1=gbc[:, hs], op=ALU.mult),
                     R=[pm_b, gbc_b], WP=[xo_b[b]])
                P.op("pool", lambda e, b=b, hs=hs: e.tensor_tensor(out=xo[b][:, hs], in0=xo[b][:, hs], in1=xt[b][:, hs], op=ALU.add),
                     R=[xt_b[b]], WP=[xo_b[b]])
            if not last:
                P.dma("act", k.xr[t * 128:(t + 1) * 128, :], xo[b], R=[xo_b[b]], W=[k.xr_b[t]])
            else:
                s_, s_b = ss[b], ss_b[b]
                P.op("act", lambda e, b=b, s_=s_: e.activation(out=junk, in_=xo[b], func=AF.Square, accum_out=s_[:, 0:1]),
                     R=[xo_b[b]], W=[junk_b], WP=[s_b])
                P.op("dve", lambda e, s_=s_: e.tensor_scalar(out=s_[:, 1:2], in0=s_[:, 0:1], scalar1=1.0 / D, scalar2=EPS, op0=ALU.mult, op1=ALU.add),
                     R=[s_b], WP=[s_b])
                P.op("act", lambda e, s_=s_: e.activation(out=s_[:, 2:3], in_=s_[:, 1:2], func=AF.Sqrt), R=[s_b], WP=[s_b])
                P.op("dve", lambda e, s_=s_: e.reciprocal(out=s_[:, 3:4], in_=s_[:, 2:3]), R=[s_b], WP=[s_b])
                P.op("dve", lambda e, b=b, s_=s_: e.scalar_tensor_tensor(out=xo[b], in0=xo[b], scalar=s_[:, 3:4], in1=fg, op0=ALU.mult, op1=ALU.mult),
                     R=[s_b, fg_b], WP=[xo_b[b]])
                P.dma("act", k.out[t * 128:(t + 1) * 128, :], xo[b], R=[xo_b[b]], WP=[k.out_b])
    P.barrier()


def mla_layer(k, li, j, last):
    P, nc, A, din = k.P, k.nc, k.arena, k.din
    pre = "mla%d_" % j
    C = MLA_COLS
    xsrc = din["x_in"] if li == 0 else k.xr
    A.reset()
    a = phase_a_common(k, li, (8, 128))
    wa = A.alloc([8, C["n"]], BF16); wa_b = Buf()
    load_cast_weights(k, wa, din[pre + "wa"], C["n"], 8, a.stg, a.stg_b, wa_b)
    wqn = A.alloc([2, 1024], BF16); wqn_b = Buf()
    load_cast_weights(k, wqn, din[pre + "wqn"], 1024, 2, a.stg, a.stg_b, wqn_b)
    wqr = A.alloc([2, 512], BF16); wqr_b = Buf()
    load_cast_weights(k, wqr, din[pre + "wqr"], 512, 2, a.stg, a.stg_b, wqr_b)
    wqrs = A.alloc([2, 512], BF16); wqrs_b = Buf()
    load_cast_weights(k, wqrs, din[pre + "wqrs"], 512, 2, a.stg, a.stg_b, wqrs_b)
    wkn = A.alloc([1, 1024], BF16); wkn_b = Buf()
    load_cast_weights(k, wkn, din[pre + "wkn"], 1024, 1, a.stg, a.stg_b, wkn_b)
    wv = A.alloc([1, 1024], BF16); wv_b = Buf()
    load_cast_weights(k, wv, din[pre + "wv"], 1024, 1, a.stg, a.stg_b, wv_b)
    qg = A.alloc([2], F32); qg_b = Buf()
    kvg = A.alloc([1], F32); kvg_b = Buf()
    P.dma("sp", qg, din[pre + "qg"], W=[qg_b])
    P.dma("sp", kvg, din[pre + "kvg"], W=[kvg_b])
    lat = A.alloc([3, 512], F32); lat_b = Buf()
    sq = A.alloc([3, 512], F32); sq_b = Buf()
    rs = [A.alloc([512], F32) for _ in range(2)]
    rs_b = [Buf() for _ in range(2)]
    latn = A.alloc([3, 512], BF16); latn_b = Buf()
    nxt = hT_prime(k, a, li, xsrc, k.xr_b)
    for st in range(NST):
        tok0 = st * 512
        ts = slice(tok0, tok0 + 512)
        hT, hT_b, rcs, rcs_b = nxt
        rhs_h = lambda c, hT=hT: hT[:, c, :]
        hT_step_begin(k, a, st, xsrc, k.xr_b)
        for ci in range(3):
            pm, pm_b = mm_psum(k, a)
            fm_group(k, pm, pm_b, wa, wa_b, 8, ci * 128, 128, rhs_h, [hT_b])
            P.op("act", lambda e, ci=ci, pm=pm: e.activation(out=lat[:, ci, :], in_=pm[:, :], func=AF.Copy), R=[pm_b], WP=[lat_b])
            P.op("act", lambda e, ci=ci, pm=pm: e.activation(out=sq[:, ci, :], in_=pm[:, :], func=AF.Square), R=[pm_b], WP=[sq_b])
        g = Group(k, a, k.KT[1024:1088, ts], k.KT_b, 1, rows_last=64)
        job_rope(k, a, wa, wa_b, 8, C["kpe"], wa, wa_b, C["kpes"], 64, rhs_h, [hT_b], rcs, rcs_b, g.slot(0), g.b)
        g.flush()
        for h4 in range(2):
            g = Group(k, a, k.GT[h4 * 512:(h4 + 1) * 512, ts], k.GT_b, 4)
            for j4 in range(4):
                job_fm(k, a, wa, wa_b, 8, C["gate"] + (h4 * 4 + j4) * 128, 128, rhs_h, [hT_b], g.slot(j4), g.b, act=AF.Silu)
            g.flush()
        for which, chunks, nfeat, gsb, gsb_b in ((0, (0, 1), 256, qg, qg_b), (1, (2,), 128, kvg, kvg_b)):
            pm, pm_b = mm_psum(k, a)
            for jj, ci in enumerate(chunks):
                P.op("pe", lambda e, ci=ci, jj=jj, pm=pm, chunks=chunks: e.matmul(pm[:, :], lhsT=k.onesf[:], rhs=sq[:, ci, :], start=(jj == 0), stop=(jj == len(chunks) - 1)),
                     R=[k.onesf_b, sq_b], W=[pm_b], inc=(jj == len(chunks) - 1))
            r_, r_b = rs[which], rs_b[which]
            P.op("act", lambda e, r_=r_, pm=pm, nfeat=nfeat: e.activation(out=r_, in_=pm[:, :], func=AF.Sqrt, scale=1.0 / nfeat, bias=EPS), R=[pm_b], W=[r_b])
            P.op("dve", lambda e, r_=r_: e.reciprocal(out=r_, in_=r_), R=[r_b], W=[r_b])
            for jj, ci in enumerate(chunks):
                P.op("dve", lambda e, ci=ci, jj=jj, r_=r_, gsb=gsb: e.scalar_tensor_tensor(out=latn[:, ci, :], in0=lat[:, ci, :], scalar=gsb[:, jj:jj + 1], in1=r_, op0=ALU.mult, op1=ALU.mult),
                     R=[lat_b, r_b, gsb_b], WP=[latn_b])
        nxt = hT_step_mid(k, a, li, st, xsrc, k.xr_b)
        rhs_q = lambda c: latn[:, c, :]
        rhs_kv = lambda c: latn[:, 2, :]
        for h4 in range(2):
            g = Group(k, a, k.QT[h4 * 512:(h4 + 1) * 512, ts], k.QT_b, 4)
            for j4 in range(4):
                job_fm(k, a, wqn, wqn_b, 2, (h4 * 4 + j4) * 128, 128, rhs_q, [latn_b], g.slot(j4), g.b, eng="dve")
            g.flush()
        g = Group(k, a, k.QT[1024:1536, ts], k.QT_b, 4)
        for hp in range(4):
            job_rope(k, a, wqr, wqr_b, 2, hp * 128, wqrs, wqrs_b, hp * 128, 128, rhs_q, [latn_b], rcs, rcs_b, g.slot(hp), g.b)
        g.flush()
        for h4 in range(2):
            g = Group(k, a, k.KT[h4 * 512:(h4 + 1) * 512, ts], k.KT_b, 4)
            for j4 in range(4):
                job_fm(k, a, wkn, wkn_b, 1, (h4 * 4 + j4) * 128, 128, rhs_kv, [latn_b], g.slot(j4), g.b, eng="dve")
            g.flush()
        for sub in range(4):
            for half in range(2):
                job_tm(k, a, wv, wv_b, 1, half * 512, 512, lambda c, sub=sub: latn[:, 2, sub * 128:(sub + 1) * 128], [latn_b],
                       a.vst[:, half * 4:(half + 1) * 4, sub, :], a.vst_b)
        P.dma("sp", k.VT[:, :, st * 4:(st + 1) * 4, :].rearrange("h p t d -> p h t d"), a.vst, R=[a.vst_b], WP=[k.VT_b])
    P.barrier()
    A.reset()
    kpe = A.alloc([S], BF16); kpe_b = Buf()
    P.dma("sp", kpe[0:64, :], k.KT[1024:1088, :], R=[k.KT_b], W=[kpe_b])
    hb = []
    for i in range(2):
        hb.append(dict(qn=A.alloc([S], BF16), qr=A.alloc([S], BF16), kn=A.alloc([S], BF16), v=A.alloc([NT, 128], BF16),
                       g=A.alloc([S], BF16), b=Buf()))
    pt_bufs = [(A.alloc([512], BF16), Buf()) for _ in range(5)]
    rd = [A.alloc([512], F32) for _ in range(2)]
    rd_b = [Buf() for _ in range(2)]
    og = [A.alloc([512], F32) for _ in range(2)]
    og_b = [Buf() for _ in range(2)]
    ost = [A.alloc([512], BF16) for _ in range(2)]
    ost_b = [Buf() for _ in range(2)]
    dacc = [A.alloc([512], F32) for _ in range(2)]
    dacc_b = [Buf() for _ in range(2)]
    scale = 192.0 ** -0.5
    ep_n = [0]
    for h in range(8):
        hbuf = hb[h % 2]
        b_ = hbuf["b"]
        P.dma("sp", hbuf["qn"], k.QT[h * 128:(h + 1) * 128, :], R=[k.QT_b], WP=[b_])
        P.dma("sp", hbuf["qr"][0:64, :], k.QT[1024 + h * 64:1024 + (h + 1) * 64, :], R=[k.QT_b], WP=[b_])
        P.dma("sp", hbuf["kn"], k.KT[h * 128:(h + 1) * 128, :], R=[k.KT_b], WP=[b_])
        P.dma("sp", hbuf["v"], k.VT[h], R=[k.VT_b], WP=[b_])
        P.dma("sp", hbuf["g"], k.GT[h * 128:(h + 1) * 128, :], R=[k.GT_b], WP=[b_])
        rows = []
        for qi in range(NST):
            qs = slice(qi * 512, (qi + 1) * 512)
            ksteps = []
            for ki in range(4 * qi + 4):
                ks = slice(ki * 128, (ki + 1) * 128)
                s_list = [(hbuf["kn"][:, ks], hbuf["qn"][:, qs], [b_]),
                          (kpe[0:64, ks], hbuf["qr"][0:64, qs], [b_, kpe_b])]
                kk = ki - 4 * qi
                if kk >= 0:
                    s_list.append((k.identb[:], k.masks[:, MASK_CAUSAL[kk], :], [k.identb_b, k.masks_b]))
                pv = [("O", hbuf["v"][:, ki, :], [b_], 128)]
                ksteps.append(dict(s=s_list, pv=pv, cols=band_cols(max(kk, 0), None)))

            ri_ = len(rows)

            def epilogue(O, O_b, Dn, Dn_b, h=h, qs=qs, hbuf=hbuf, b_=b_, ri_=ri_):
                i = ri_ % 2
                Dp, Dp_b = k.ps[6], k.ps_b[6]

                def rest():
                    P.op("pe", lambda e: e.matmul(Dp[:, :], lhsT=k.onesf[:], rhs=dacc[i], start=True, stop=True), R=[k.onesf_b, dacc_b[i]], W=[Dp_b])
                    P.op("act", lambda e: e.activation(out=rd[i], in_=Dp[:, :], func=AF.Ln), R=[Dp_b], W=[rd_b[i]])
                    P.op("act", lambda e: e.activation(out=rd[i], in_=rd[i], func=AF.Exp, scale=-1.0), R=[rd_b[i]], W=[rd_b[i]])
                    P.op("dve", lambda e: e.tensor_tensor(out=og[i], in0=O[:, :], in1=rd[i], op=ALU.mult), R=[O_b, rd_b[i]], W=[og_b[i]])
                    P.op("pool", lambda e: e.tensor_tensor(out=ost[i], in0=og[i], in1=hbuf["g"][:, qs], op=ALU.mult), R=[og_b[i], b_], W=[ost_b[i]])
                    P.dma("pool", k.OT[h * 128:(h + 1) * 128, qs], ost[i], R=[ost_b[i]], WP=[k.OT_b])
                return [None, None, rest]
            rows.append(dict(ksteps=ksteps, o_banks=3, dacc=(dacc[ri_ % 2], dacc_b[ri_ % 2]), epilogue=epilogue))
        attention_rows(k, rows, pt_bufs, scale, s_banks=(0, 1, 2, 7), look=3)
    phase_c(k, li, pre + "wout", last)


def mla_rope_q(k, a, wqr, wqr_b, wqrs, wqrs_b, hp, rhs_q, latn_b, tok0):
    P = k.P
    pm, pm_b = mm_psum(k, a)
    fm_group(k, pm, pm_b, wqr, wqr_b, 2, hp * 128, 128, rhs_q, [latn_b])
    pm2, pm2_b = mm_psum(k, a)
    fm_group(k, pm2, pm2_b, wqrs, wqrs_b, 2, hp * 128, 128, rhs_q, [latn_b])
    r0, r0_b, r1, r1_b = a.rt[0], a.rt_b[0], a.rt[1], a.rt_b[1]
    rcs, rcs_b = a.cur_rcs, a.cur_rcs_b
    P.op("dve", lambda e: e.tensor_tensor(out=r0, in0=pm[:, :], in1=rcs[:, 0, :], op=ALU.mult), R=[pm_b, rcs_b], W=[r0_b])
    P.op("dve", lambda e: e.tensor_tensor(out=r1, in0=pm2[:, :], in1=rcs[:, 1, :], op=ALU.mult), R=[pm2_b, rcs_b], W=[r1_b])
    st, st_b = fst_next(a)
    P.op("pool", lambda e: e.tensor_tensor(out=st, in0=r0, in1=r1, op=ALU.add), R=[r0_b, r1_b], W=[st_b])
    P.dma("pool", k.QT[1024 + hp * 128:1024 + (hp + 1) * 128, tok0:tok0 + 512], st, R=[st_b], WP=[k.QT_b])


def swa_layer(k, li, last):
    P, nc, A, din = k.P, k.nc, k.arena, k.din
    C = SWA_COLS
    xsrc = din["x_in"] if li == 0 else k.xr
    A.reset()
    a = phase_a_common(k, li, (2, 64))
    wa = A.alloc([8, C["n"]], BF16); wa_b = Buf()
    load_cast_weights(k, wa, din["swa_wa"], C["n"], 8, a.stg, a.stg_b, wa_b)
    nxt = hT_prime(k, a, li, xsrc, k.xr_b)
    for st in range(NST):
        tok0 = st * 512
        ts = slice(tok0, tok0 + 512)
        hT, hT_b, rcs, rcs_b = nxt
        rhs_h = lambda c, hT=hT: hT[:, c, :]
        hT_step_begin(k, a, st, xsrc, k.xr_b)
        g = Group(k, a, k.KT[0:128, ts], k.KT_b, 1)
        job_rope(k, a, wa, wa_b, 8, C["k"], wa, wa_b, C["ks"], 128, rhs_h, [hT_b], rcs, rcs_b, g.slot(0), g.b)
        g.flush()
        for m4 in range(2):
            g = Group(k, a, k.QT[m4 * 512:(m4 + 1) * 512, ts], k.QT_b, 4)
            for j4 in range(4):
                m = m4 * 4 + j4
                job_rope(k, a, wa, wa_b, 8, C["q"] + m * 128, wa, wa_b, C["qs"] + m * 128, 128, rhs_h, [hT_b], rcs, rcs_b, g.slot(j4), g.b)
            g.flush()
        for sub in range(4):
            job_tm(k, a, wa, wa_b, 8, C["v"], 128, lambda c, sub=sub, hT=hT: hT[:, c, sub * 128:(sub + 1) * 128], [hT_b],
                   a.vst[:, :, sub, :], a.vst_b)
        P.dma("sp", k.VS[0:2, :, st * 4:(st + 1) * 4, :].rearrange("h p t d -> p h t d"), a.vst, R=[a.vst_b], WP=[k.VS_b])
        nxt = hT_step_mid(k, a, li, st, xsrc, k.xr_b)
        for m4 in range(2):
            g = Group(k, a, k.GT[m4 * 512:(m4 + 1) * 512, ts], k.GT_b, 4)
            for j4 in range(4):
                job_fm(k, a, wa, wa_b, 8, C["gate"] + (m4 * 4 + j4) * 128, 128, rhs_h, [hT_b], g.slot(j4), g.b, act=AF.Silu)
            g.flush()
    P.barrier()
    A.reset()
    kT = [A.alloc([S], BF16) for _ in range(2)]
    kT_b = Buf()
    vv = [A.alloc([NT, 64], BF16) for _ in range(2)]
    vv_b = Buf()
    for kv in range(2):
        P.dma("sp", kT[kv][0:64, :], k.KT[kv * 64:(kv + 1) * 64, :], R=[k.KT_b], WP=[kT_b])
        P.dma("sp", vv[kv], k.VS[kv], R=[k.VS_b], WP=[vv_b])
    snk = A.alloc([16], F32); snk_b = Buf()
    P.dma("sp", snk, din["swa_sinks"].partition_broadcast(128), W=[snk_b])
    P.op("act", lambda e: e.activation(out=snk, in_=snk, func=AF.Exp), R=[snk_b], W=[snk_b])
    hb = [dict(q=A.alloc([S], BF16), g=A.alloc([S], BF16), b=Buf()) for _ in range(2)]
    pt_bufs = [(A.alloc([512], BF16), Buf()) for _ in range(5)]
    rd = [A.alloc([512], F32) for _ in range(2)]; rd_b = [Buf() for _ in range(2)]
    og = [A.alloc([512], F32) for _ in range(2)]; og_b = [Buf() for _ in range(2)]
    ost = [A.alloc([512], BF16) for _ in range(2)]; ost_b = [Buf() for _ in range(2)]
    ep_n = [0]
    for hh in range(16):
        kv = hh // 8
        hbuf = hb[hh % 2]
        b_ = hbuf["b"]
        P.dma("sp", hbuf["q"][0:64, :], k.QT[hh * 64:(hh + 1) * 64, :], R=[k.QT_b], WP=[b_])
        P.dma("sp", hbuf["g"][0:64, :], k.GT[hh * 64:(hh + 1) * 64, :], R=[k.GT_b], WP=[b_])
        rows = []
        for qi in range(NST):
            qs = slice(qi * 512, (qi + 1) * 512)
            ksteps = []
            for kk in range(-1, 4):
                ki = 4 * qi + kk
                if ki < 0:
                    continue
                ks = slice(ki * 128, (ki + 1) * 128)
                s_list = [(kT[kv][0:64, ks], hbuf["q"][0:64, qs], [b_, kT_b]),
                          (k.identb[:], k.masks[:, MASK_SWA[kk], :], [k.identb_b, k.masks_b])]
                pv = [("O", vv[kv][:, ki, :], [vv_b], 64), ("D", k.onesb[:, 0:64], [k.onesb_b], 64)]
                ksteps.append(dict(s=s_list, pv=pv, cols=band_cols(kk, 128)))

            def epilogue(O, O_b, Dn, Dn_b, hh=hh, qs=qs, hbuf=hbuf, b_=b_):
                i = ep_n[0] % 2
                ep_n[0] += 1
                P.op("dve", lambda e: e.tensor_scalar(out=rd[i][0:64, :], in0=Dn[0:64, :], scalar1=snk[0:64, hh:hh + 1], scalar2=None, op0=ALU.add),
                     R=[Dn_b, snk_b], W=[rd_b[i]])
                P.op("dve", lambda e: e.reciprocal(out=rd[i][0:64, :], in_=rd[i][0:64, :]), R=[rd_b[i]], W=[rd_b[i]])
                P.op("dve", lambda e: e.tensor_tensor(out=og[i][0:64, :], in0=O[0:64, :], in1=rd[i][0:64, :], op=ALU.mult), R=[O_b, rd_b[i]], W=[og_b[i]])
                P.op("pool", lambda e: e.tensor_tensor(out=ost[i][0:64, :], in0=og[i][0:64, :], in1=hbuf["g"][0:64, qs], op=ALU.mult), R=[og_b[i], b_], W=[ost_b[i]])
                P.dma("pool", k.OT[hh * 64:(hh + 1) * 64, qs], ost[i][0:64, :], R=[ost_b[i]], WP=[k.OT_b])
            rows.append(dict(ksteps=ksteps, epilogue=epilogue))
        attention_rows(k, rows, pt_bufs, 0.125, s_banks=(0, 1, 2, 7), look=3)
    phase_c(k, li, "swa_wout", last)


def nsa_layer(k, li, last):
    P, nc, A, din = k.P, k.nc, k.arena, k.din
    C = NSA_COLS
    xsrc = din["x_in"] if li == 0 else k.xr
    A.reset()
    a = phase_a_common(k, li, (8, 64))
    wa = A.alloc([8, C["n"]], BF16); wa_b = Buf()
    load_cast_weights(k, wa, din["nsa_wa"], C["n"], 8, a.stg, a.stg_b, wa_b)
    nxt = hT_prime(k, a, li, xsrc, k.xr_b)
    for st in range(NST):
        tok0 = st * 512
        ts = slice(tok0, tok0 + 512)
        hT, hT_b, rcs, rcs_b = nxt
        rhs_h = lambda c, hT=hT: hT[:, c, :]
        hT_step_begin(k, a, st, xsrc, k.xr_b)
        g = Group(k, a, k.KT[0:512, ts], k.KT_b, 4)
        for m in range(2):
            job_rope(k, a, wa, wa_b, 8, C["kslc"] + m * 128, wa, wa_b, C["kslcs"] + m * 128, 128, rhs_h, [hT_b], rcs, rcs_b, g.slot(m), g.b)
        for m in range(2):
            job_rope(k, a, wa, wa_b, 8, C["kwin"] + m * 128, wa, wa_b, C["kwins"] + m * 128, 128, rhs_h, [hT_b], rcs, rcs_b, g.slot(2 + m), g.b)
        g.flush()
        g = Group(k, a, k.KT[512:1024, ts], k.KT_b, 4)
        for m in range(2):
            job_fm(k, a, wa, wa_b, 8, C["kcmp"] + m * 128, 128, rhs_h, [hT_b], g.slot(m), g.b, eng="dve")
        for m in range(2):
            job_fm(k, a, wa, wa_b, 8, C["vcmp"] + m * 128, 128, rhs_h, [hT_b], g.slot(2 + m), g.b, eng="dve")
        g.flush()
        for m4 in range(2):
            g = Group(k, a, k.QT[m4 * 512:(m4 + 1) * 512, ts], k.QT_b, 4)
            for j4 in range(4):
                m = m4 * 4 + j4
                job_rope(k, a, wa, wa_b, 8, C["q"] + m * 128, wa, wa_b, C["qs"] + m * 128, 128, rhs_h, [hT_b], rcs, rcs_b, g.slot(j4), g.b)
            g.flush()
        g = Group(k, a, k.GT[1024:1072, ts], k.GT_b, 1, rows_last=48)
        job_fm(k, a, wa, wa_b, 8, C["gbr"], 48, rhs_h, [hT_b], g.slot(0), g.b, act=AF.Sigmoid)
        g.flush()
        for sub in range(4):
            lf = lambda c, sub=sub, hT=hT: hT[:, c, sub * 128:(sub + 1) * 128]
            job_tm(k, a, wa, wa_b, 8, C["vslc"], 256, lf, [hT_b], a.vst[:, 0:4, sub, :], a.vst_b)
            job_tm(k, a, wa, wa_b, 8, C["vwin"], 256, lf, [hT_b], a.vst[:, 4:8, sub, :], a.vst_b)
        P.dma("sp", k.VS[:, :, st * 4:(st + 1) * 4, :].rearrange("h p t d -> p h t d"), a.vst, R=[a.vst_b], WP=[k.VS_b])
        nxt = hT_step_mid(k, a, li, st, xsrc, k.xr_b)
        for m4 in range(2):
            g = Group(k, a, k.GT[m4 * 512:(m4 + 1) * 512, ts], k.GT_b, 4)
            for j4 in range(4):
                job_fm(k, a, wa, wa_b, 8, C["gate"] + (m4 * 4 + j4) * 128, 128, rhs_h, [hT_b], g.slot(j4), g.b, act=AF.Silu)
            g.flush()
    import os
    STOP = os.environ.get("NSA_STOP", "")
    if STOP == "A":
        return phase_c(k, li, "nsa_wout", last)
    P.barrier()
    A.reset()
    kc = [A.alloc([256], BF16) for _ in range(4)]; kc_b = Buf()
    vc = [A.alloc([2, 64], BF16) for _ in range(4)]; vc_b = Buf()
    keep = A.off
    stg = [A.alloc([8, 256], F32) for _ in range(2)]
    stg_b = [Buf() for _ in range(2)]
    k.stg_n = 0
    w1 = [A.alloc([16, 128], BF16) for _ in range(2)]; w1_b = [Buf(), Buf()]
    w2f = A.alloc([2, 64], F32); w2f_b = Buf()
    w2 = A.alloc([2, 64], BF16); w2_b = Buf()
    pef = A.alloc([16], F32); pef_b = Buf()
    peb = A.alloc([16], BF16); peb_b = Buf()
    b1 = A.alloc([2], F32); b1_b = Buf()
    x2 = [A.alloc([S], BF16) for _ in range(2)]; x2_b = [Buf(), Buf()]
    hid = [A.alloc([256], BF16) for _ in range(2)]; hid_b = [Buf(), Buf()]
    for wi, nm in enumerate(("nsa_wk1", "nsa_wv1")):
        for half in range(2):
            b = k.stg_n % 2
            k.stg_n += 1
            P.dma("sp", stg[b][:, :, 0:128], din[nm][half * 1024:(half + 1) * 1024, :].rearrange("(m p) j -> p m j", p=128), W=[stg_b[b]])
            P.op("dve", lambda e, b=b, wi=wi, half=half: e.tensor_copy(out=w1[wi][:, half * 8:(half + 1) * 8, :], in_=stg[b][:, :, 0:128]),
                 R=[stg_b[b]], WP=[w1_b[wi]])
    P.dma("sp", w2f[:, 0, :], din["nsa_wk2"], WP=[w2f_b])
    P.dma("sp", w2f[:, 1, :], din["nsa_wv2"], WP=[w2f_b])
    P.op("dve", lambda e: e.tensor_copy(out=w2, in_=w2f), R=[w2f_b], W=[w2_b])
    P.dma("sp", pef, din["nsa_pe"], W=[pef_b])
    P.op("dve", lambda e: e.tensor_copy(out=peb, in_=pef), R=[pef_b], W=[peb_b])
    for wi in range(2):
        P.op("pool", lambda e, wi=wi: e.memset(hid[wi][:, 255:256], 0.0), WP=[hid_b[wi]])
        pm, pm_b = k.ps[wi], k.ps_b[wi]
        for m in range(16):
            P.op("pe", lambda e, wi=wi, m=m, pm=pm: e.matmul(pm[:, 0:1], lhsT=w1[wi][:, m, :], rhs=peb[:, m:m + 1], start=(m == 0), stop=(m == 15)),
                 R=[w1_b[wi], peb_b], W=[pm_b], inc=(m == 15))
        P.op("act", lambda e, wi=wi, pm=pm: e.activation(out=b1[:, wi:wi + 1], in_=pm[:, 0:1], func=AF.Copy), R=[pm_b], WP=[b1_b])
    n = 0
    for kv in range(4):
        for wi in range(2):
            xb = n % 2
            n += 1
            row0 = (512 if wi == 0 else 768) + kv * 64
            P.dma("sp", x2[xb][0:64, :], k.KT[row0:row0 + 64, :], R=[k.KT_b], WP=[x2_b[xb]])
            P.dma("sp", x2[xb][64:128, 0:S - 1], k.KT[row0:row0 + 64, 1:S], R=[k.KT_b], WP=[x2_b[xb]])
            x2v = x2[xb].rearrange("p (i l) -> p i l", l=16)
            pm, pm_b = k.ps[2 + xb], k.ps_b[2 + xb]
            for m in range(16):
                l2 = 2 * m
                rhs = x2v[:, 0:255, l2] if l2 < 16 else x2v[:, 1:256, l2 - 16]
                P.op("pe", lambda e, wi=wi, m=m, pm=pm, rhs=rhs: e.matmul(pm[:, 0:255], lhsT=w1[wi][:, m, :], rhs=rhs, start=(m == 0), stop=(m == 15)),
                     R=[w1_b[wi], x2_b[xb]], W=[pm_b], inc=(m == 15))
            P.op("act", lambda e, wi=wi, pm=pm: e.activation(out=hid[wi][:, 0:255], in_=pm[:, 0:255], func=AF.Silu, bias=b1[:, wi:wi + 1]),
                 R=[pm_b, b1_b], WP=[hid_b[wi]])
            if wi == 0:
                pm2, pm2_b = k.ps[4], k.ps_b[4]
                P.op("pe", lambda e, pm2=pm2: e.matmul(pm2[0:64, 0:256], lhsT=w2[:, 0, :], rhs=hid[0][:, :], start=True, stop=True),
                     R=[w2_b, hid_b[0]], W=[pm2_b])
                P.op("dve", lambda e, kv=kv, pm2=pm2: e.tensor_copy(out=kc[kv][0:64, :], in_=pm2[0:64, 0:256]), R=[pm2_b], WP=[kc_b])
            else:
                pm2, pm2_b = k.ps[5], k.ps_b[5]
                for nt in range(2):
                    P.op("pe", lambda e, nt=nt, pm2=pm2: e.matmul(pm2[:, nt * 64:(nt + 1) * 64], lhsT=hid[1][:, nt * 128:(nt + 1) * 128], rhs=w2[:, 1, :], start=True, stop=True),
                         R=[w2_b, hid_b[1]], W=[pm2_b], inc=(nt == 1))
                P.op("dve", lambda e, kv=kv, pm2=pm2: e.tensor_copy(out=vc[kv], in_=pm2[:, 0:128].rearrange("p (a b) -> p a b", a=2)), R=[pm2_b], WP=[vc_b])
    if STOP == "B0":
        return phase_c(k, li, "nsa_wout", last)
    P.barrier()
    A.reset(keep)
    bf = BF16
    cmask = A.alloc([4608], bf); cmask_b = Buf()
    ovl = A.alloc([2, 64], bf); ovl_b = Buf()
    sel3 = A.alloc([48, 64], bf); sel3_b = Buf()
    cmul = A.alloc([24, 64], bf); cadd = A.alloc([24, 64], bf); ctab_b = Buf()
    gsig = A.alloc([S], bf); gsig_b = Buf()
    P.dma("sp", cmask, din["c_cmask"], W=[cmask_b])
    P.dma("sp", ovl, din["c_ovl"], W=[ovl_b])
    P.dma("sp", sel3[0:48], din["c_sel3"][0:48], W=[sel3_b])
    P.dma("sp", cmul, din["c_cmul"][:, 8:32, :], WP=[ctab_b])
    P.dma("sp", cadd, din["c_cadd"][:, 8:32, :], WP=[ctab_b])
    P.dma("sp", gsig[0:48, :], k.GT[1024:1072, :], R=[k.GT_b], W=[gsig_b])
    QS = [A.alloc([S], bf) for _ in range(4)]
    QS_b = [Buf() for _ in range(4)]
    KE = A.alloc([S], bf); KE_b = Buf()
    kwin = A.alloc([S], bf); kwin_b = Buf()
    vs = A.alloc([NT, 128], bf); vs_b = Buf()
    vw = A.alloc([NT, 128], bf); vw_b = Buf()
    P.op("pool", lambda e: e.memset(vs[:, :, 64:128], 1.0), WP=[vs_b])
    P.op("pool", lambda e: e.memset(vw[:, :, 64:128], 1.0), WP=[vw_b])
    dsb_b = [Buf(), Buf()]
    impT = A.alloc([S], F32); impT_b = Buf()
    G = [A.alloc([512], bf) for _ in range(2)]; G_b = [Buf(), Buf()]
    pt_bufs = [(A.alloc([512], bf), Buf()) for _ in range(3)]
    rd = [A.alloc([512], F32) for _ in range(2)]; rd_b = [Buf() for _ in range(2)]
    tmp = [A.alloc([512], F32) for _ in range(2)]; tmp_b = [Buf() for _ in range(2)]
    acc = [A.alloc([512], F32) for _ in range(2)]; acc_b = [Buf() for _ in range(2)]
    ocl = [A.alloc([512], F32) for _ in range(2)]; ocl_b = [Buf() for _ in range(2)]
    ocs, ocs_b = ocl, ocl_b
    ost = [A.alloc([512], bf) for _ in range(2)]; ost_b = [Buf() for _ in range(2)]
    NW = 2
    impm_l = [A.alloc([64], F32) for _ in range(NW)]; impm_bl = [Buf() for _ in range(NW)]
    imp2_l = [A.alloc([64], F32) for _ in range(NW)]; imp2_bl = [Buf() for _ in range(NW)]
    gq = [A.alloc([512], bf) for _ in range(6)]; gq_b = [Buf() for _ in range(6)]
    m1 = A.alloc([8], F32); m1_b = Buf()
    m2 = A.alloc([8], F32); m2_b = Buf()
    selb_l = [A.alloc([128], F32) for _ in range(NW)]; selb_bl = [Buf() for _ in range(NW)]
    cmp3_l = [A.alloc([4096], BF16) for _ in range(NW)]; cmp3_bl = [Buf() for _ in range(NW)]
    dlo = [cmp3_l[i_].bitcast(F32)[:, 0:512] for i_ in range(2)]; dlo_b = cmp3_bl
    for w_ in range(NW):
        P.op("pool", lambda e, w_=w_: e.memset(selb_l[w_], 0.0), W=[selb_bl[w_]])
    P.dma("sp", KE[64:128, :], din["c_ebig"][64:128, :], WP=[KE_b])
    for g in range(4):
        P.dma("sp", QS[g][64:128, 0:1024], din["c_selst"][64:128, :], WP=[QS_b[g]])
    cn = [0]
    for kv in range(4):
        for g in range(4):
            hh = kv * 4 + g
            P.dma("sp", QS[g][0:64, :], k.QT[hh * 64:(hh + 1) * 64, :], R=[k.QT_b], WP=[QS_b[g]])
        P.dma("sp", KE[0:64, :], k.KT[kv * 64:(kv + 1) * 64, :], R=[k.KT_b], WP=[KE_b])
        P.dma("sp", kwin[0:64, :], k.KT[256 + kv * 64:256 + (kv + 1) * 64, :], R=[k.KT_b], W=[kwin_b])
        P.dma("sp", vs[:, :, 0:64], k.VS[kv], R=[k.VS_b], WP=[vs_b])
        P.dma("sp", vw[:, :, 0:64], k.VS[4 + kv], R=[k.VS_b], WP=[vw_b])
        rows = []
        for g in range(4):
            hh = kv * 4 + g
            for qi in range(NST):
                qs = slice(qi * 512, (qi + 1) * 512)
                ksteps = []
                for nt in range(2):
                    if nt == 1 and qi < 4:
                        continue
                    s_list = [(kc[kv][0:64, nt * 128:(nt + 1) * 128], QS[g][0:64, qs], [kc_b, QS_b[g]])]
                    if not (nt == 0 and qi >= 5):
                        cm0 = qi * 512 if nt == 0 else 2560 + (qi - 4) * 512
                        s_list.append((k.identb[:], cmask[:, cm0:cm0 + 512], [k.identb_b, cmask_b]))
                    pv = [("O", vc[kv][:, nt, :], [vc_b], 64), ("D", k.onesb[:, 0:64], [k.onesb_b], 64), ("A", ovl[:, nt, :], [ovl_b], 64)]
                    ksteps.append(dict(s=s_list, pv=pv))

                def epilogue(O, O_b, Dn, Dn_b, hh=hh, g=g, qs=qs):
                    i = cn[0] % 2
                    cn[0] += 1
                    Aa, Aa_b = k.ps[7], k.ps_b[7]
                    P.op("act", lambda e: e.activation(out=rd[i][0:64, :], in_=Dn[0:64, :], func=AF.Ln, bias=1e-18), R=[Dn_b], W=[rd_b[i]])
                    P.op("act", lambda e: e.activation(out=rd[i][0:64, :], in_=rd[i][0:64, :], func=AF.Exp, scale=-1.0), R=[rd_b[i]], W=[rd_b[i]])
                    P.op("dve", lambda e: e.tensor_tensor(out=ocs[i][0:64, :], in0=O[0:64, :], in1=rd[i][0:64, :], op=ALU.mult), R=[O_b, rd_b[i]], W=[ocs_b[i]])
                    P.dma("pool", k.OC[hh * 64:(hh + 1) * 64, qs], ocs[i][0:64, :], R=[ocs_b[i]], WP=[k.OC_b])
                    if g == 0:
                        P.op("dve", lambda e: e.tensor_tensor(out=impT[0:64, qs], in0=Aa[0:64, :], in1=rd[i][0:64, :], op=ALU.mult), R=[Aa_b, rd_b[i]], WP=[impT_b])
                    else:
                        P.op("dve", lambda e: e.tensor_tensor(out=tmp[i][0:64, :], in0=Aa[0:64, :], in1=rd[i][0:64, :], op=ALU.mult), R=[Aa_b, rd_b[i]], W=[tmp_b[i]])
                        P.op("pool", lambda e: e.tensor_tensor(out=impT[0:64, qs], in0=impT[0:64, qs], in1=tmp[i][0:64, :], op=ALU.add), R=[tmp_b[i]], WP=[impT_b])
                rows.append(dict(ksteps=ksteps, epilogue=epilogue))
        attention_rows(k, rows, pt_bufs, 0.125)
        if STOP == "B1":
            return phase_c(k, li, "nsa_wout", last)
        for w_ in range(NW):
            P.op("pool", lambda e, w_=w_: e.memset(selb_l[w_][:, 64:128], -30000.0), WP=[selb_bl[w_]])

        def sel_stages(t, w_):
            impm, impm_b, imp2, imp2_b = impm_l[w_], impm_bl[w_], imp2_l[w_], imp2_bl[w_]
            selb, selb_b, cmp3, cmp3_b = selb_l[w_], selb_bl[w_], cmp3_l[w_], cmp3_bl[w_]
            pT, pT_b = k.ps[w_], k.ps_b[w_]
            p2, p2_b = k.ps[2 + w_], k.ps_b[2 + w_]
            tsl = slice(t * 128, (t + 1) * 128)
            ns = 2 * t + 2
            in0 = impm[:, 0:ns].unsqueeze(1).to_broadcast([128, ns, ns])
            in1 = impm[:, 0:ns].unsqueeze(2).to_broadcast([128, ns, ns])
            c3 = cmp3[:, 0:ns * ns].rearrange("p (a b) -> p a b", a=ns)
            st_ = []
            st_.append(lambda: P.op("pe", lambda e: e.transpose(out=pT[:, 0:64], in_=impT[0:64, tsl], identity=k.ident[0:64, 0:64]),
                                    R=[impT_b, k.ident_b], W=[pT_b]))
            st_.append(lambda: P.op("dve", lambda e: e.tensor_tensor(out=impm, in0=pT[:, 0:64], in1=cmul[:, t - 8, :], op=ALU.mult), R=[pT_b, ctab_b], W=[impm_b]))
            st_.append(lambda: P.op("dve", lambda e: e.tensor_tensor(out=impm, in0=impm, in1=cadd[:, t - 8, :], op=ALU.add), R=[ctab_b, impm_b], W=[impm_b]))
            st_.append(lambda: P.op("dve", lambda e: e.tensor_tensor(out=c3, in0=in0, in1=in1, op=ALU.is_gt), R=[impm_b], W=[cmp3_b]))
            st_.append(lambda: P.op("dve", lambda e: e.tensor_reduce(out=imp2[:, 0:ns], in_=c3, axis=mybir.AxisListType.X, op=ALU.add), R=[cmp3_b], W=[imp2_b]))
            st_.append(lambda: P.op("dve", lambda e: e.tensor_scalar(out=selb[:, 64:64 + ns], in0=imp2[:, 0:ns], scalar1=15.5, scalar2=30000.0, op0=ALU.is_lt, op1=ALU.mult),
                                    R=[imp2_b], WP=[selb_b]))
            st_.append(lambda: P.op("dve", lambda e: e.tensor_scalar(out=selb[:, 64:64 + ns], in0=selb[:, 64:64 + ns], scalar1=-30000.0, scalar2=None, op0=ALU.add),
                                    R=[selb_b], WP=[selb_b]))
            st_.append(lambda: P.op("pe", lambda e: e.transpose(out=p2[:, 0:128], in_=selb, identity=k.ident[:]), R=[selb_b, k.ident_b], W=[p2_b]))
            for g in range(4):
                st_.append(lambda g=g: P.op("act", lambda e: e.activation(out=QS[g][64:128, tsl], in_=p2[64:128, 0:128], func=AF.Copy), R=[p2_b], WP=[QS_b[g]]))
            return st_

        for t0 in range(8, NT, NW):
            chains = [sel_stages(t0 + w_, w_) for w_ in range(NW)]
            for si_ in range(len(chains[0])):
                for ch in chains:
                    ch[si_]()
        if STOP == "SEL":
            return phase_c(k, li, "nsa_wout", last)
        for g in range(4):
            hh = kv * 4 + g
            rows = []
            for qi in range(NST):
                qs = slice(qi * 512, (qi + 1) * 512)
                ksteps = []
                for ki in range(4 * qi + 4):
                    ks = slice(ki * 128, (ki + 1) * 128)
                    s_list = [(KE[:, ks], QS[g][:, qs], [KE_b, QS_b[g]])]
                    kk = ki - 4 * qi
                    if kk >= 0:
                        s_list.append((k.identb[:], k.masks[:, MASK_CAUSAL[kk], :], [k.identb_b, k.masks_b]))
                    pv = [("O", vs[:, ki, :], [vs_b], 128)]
                    ksteps.append(dict(s=s_list, pv=pv, cols=band_cols(max(kk, 0), None)))

                def gate_pre(r, gi, qs=qs):
                    def f():
                        gb, gb_b = k.ps[7], k.ps_b[7]
                        P.op("pe", lambda e: e.matmul(gb[0:64, :], lhsT=sel3[0:48, r, :], rhs=gsig[0:48, qs], start=True, stop=True), R=[sel3_b, gsig_b], W=[gb_b])
                        P.op("act", lambda e: e.activation(out=gq[gi][0:64, :], in_=gb[0:64, :], func=AF.Copy), R=[gb_b], W=[gq_b[gi]])
                    return f

                def den_shift(O, O_b, i):
                    Dsb = rd[i][64:128, :]
                    P.op("act", lambda e: e.activation(out=Dsb, in_=O[64:128, :], func=AF.Copy), R=[O_b], W=[dsb_b[i]])
                    P.dma("sp", dlo[i][0:64, :], Dsb, R=[dsb_b[i]], W=[dlo_b[i]])

                    def f():
                        P.op("dve", lambda e: e.reciprocal(out=rd[i][0:64, :], in_=dlo[i][0:64, :]), R=[dlo_b[i]], W=[rd_b[i]])
                    return f

                def ep_sel(O, O_b, Dn, Dn_b, hh=hh, qs=qs, qi=qi):
                    i = qi % 2
                    f0 = den_shift(O, O_b, i)

                    def rest():
                        f0()
                        P.op("dve", lambda e: e.tensor_tensor(out=tmp[i][0:64, :], in0=O[0:64, :], in1=rd[i][0:64, :], op=ALU.mult), R=[O_b, rd_b[i]], W=[tmp_b[i]])
                        P.op("dve", lambda e: e.tensor_tensor(out=acc[i][0:64, :], in0=tmp[i][0:64, :], in1=gq[i][0:64, :], op=ALU.mult), R=[gq_b[i], tmp_b[i]], W=[acc_b[i]])
                    return [rest]
                rows.append(dict(ksteps=ksteps, o_banks=4, epilogue=ep_sel, pre=[gate_pre(hh * 3 + 1, qi % 2)]))
                ksteps = []
                for kk in range(-4, 4):
                    ki = 4 * qi + kk
                    if ki < 0:
                        continue
                    ks = slice(ki * 128, (ki + 1) * 128)
                    s_list = [(kwin[0:64, ks], QS[g][0:64, qs], [kwin_b, QS_b[g]]),
                              (k.identb[:], k.masks[:, MASK_WIN[kk], :], [k.identb_b, k.masks_b])]
                    pv = [("O", vw[:, ki, :], [vw_b], 128)]
                    ksteps.append(dict(s=s_list, pv=pv, cols=band_cols(kk, 512)))

                def ep_win(O, O_b, Dn, Dn_b, hh=hh, qs=qs, qi=qi):
                    i = qi % 2
                    Gh, Gh_b = G[i], G_b[i]
                    P.dma("sp", ocl[i][0:64, :], k.OC[hh * 64:(hh + 1) * 64, qs], R=[k.OC_b], W=[ocl_b[i]])
                    P.dma("sp", Gh[0:64, :], k.GT[hh * 64:(hh + 1) * 64, qs], R=[k.GT_b], W=[Gh_b])
                    f0 = den_shift(O, O_b, i)

                    def rest():
                        f0()
                        P.op("dve", lambda e: e.tensor_tensor(out=tmp[i][0:64, :], in0=O[0:64, :], in1=rd[i][0:64, :], op=ALU.mult), R=[O_b, rd_b[i]], W=[tmp_b[i]])
                        P.op("dve", lambda e: e.tensor_tensor(out=tmp[i][0:64, :], in0=tmp[i][0:64, :], in1=gq[2 + i][0:64, :], op=ALU.mult), R=[gq_b[2 + i], tmp_b[i]], W=[tmp_b[i]])
                        P.op("pool", lambda e: e.tensor_tensor(out=acc[i][0:64, :], in0=acc[i][0:64, :], in1=tmp[i][0:64, :], op=ALU.add), R=[tmp_b[i], acc_b[i]], W=[acc_b[i]])
                        P.op("dve", lambda e: e.tensor_tensor(out=tmp[i][0:64, :], in0=ocl[i][0:64, :], in1=gq[4 + i][0:64, :], op=ALU.mult), R=[gq_b[4 + i], ocl_b[i]], W=[tmp_b[i]])
                        P.op("pool", lambda e: e.tensor_tensor(out=acc[i][0:64, :], in0=acc[i][0:64, :], in1=tmp[i][0:64, :], op=ALU.add), R=[tmp_b[i], acc_b[i]], W=[acc_b[i]])
                        P.op("pool", lambda e: e.tensor_tensor(out=ost[i][0:64, :], in0=acc[i][0:64, :], in1=Gh[0:64, :], op=ALU.mult), R=[acc_b[i], Gh_b], W=[ost_b[i]])
                        P.dma("pool", k.OT[hh * 64:(hh + 1) * 64, qs], ost[i][0:64, :], R=[ost_b[i]], WP=[k.OT_b])
                    return [rest]
                rows.append(dict(ksteps=ksteps, o_banks=4, epilogue=ep_win, pre=[gate_pre(hh * 3 + 2, 2 + qi % 2), gate_pre(hh * 3 + 0, 4 + qi % 2)]))
            attention_rows(k, rows, pt_bufs, 0.125)
    phase_c(k, li, "nsa_wout", last)


def _np_dt(a):
    if a.dtype == np.float32:
        return F32
    if a.dtype == np.int32:
        return I32
    return BF16


def make_in_maps(inputs, cores):
    w = _prep_weights({kk: np.asarray(v) for kk, v in inputs.items()})
    cst = _consts()
    shared = {}
    shared.update(w)
    shared.update(cst)
    x = np.asarray(inputs["x"], np.float32)
    c = np.asarray(inputs["c"], np.float32)
    pos = np.asarray(inputs["positions"], np.int32)
    maps = []
    for b in cores:
        m = dict(shared)
        m["x_in"] = np.ascontiguousarray(x[b])
        m["c_fm"] = _fm(c[b], 8)
        m["pos"] = np.ascontiguousarray(pos[b].reshape(1, S))
        maps.append(m)
    return maps


def kernel(**inputs):
    maps = make_in_maps(inputs, list(range(8)))
    shapes = {kk: (v.shape, _np_dt(v)) for kk, v in maps[0].items()}
    nc, P = build(DEPTH, shapes)
    res = run_bass_kernel_spmd(nc, maps, core_ids=list(range(8)))
    out = np.stack([np.asarray(r["out"], np.float32) for r in res.results], axis=0)
    return out
```

```python
import numpy as np
import ml_dtypes
import concourse.bass as bass
import concourse.mybir as mybir
from concourse.bass_utils import run_bass_kernel_spmd

F32 = mybir.dt.float32
BF16 = mybir.dt.bfloat16
I32 = mybir.dt.int32
U8 = mybir.dt.uint8
AF = mybir.ActivationFunctionType
ALU = mybir.AluOpType

S = 4096
D = 1024
NT = S // 128
NST = S // 512
DEPTH = 4
NEG = -30000.0
EPS = 1e-6


class Buf:
    __slots__ = ("w", "r")

    def __init__(self):
        self.w = {}
        self.r = {}


class Prog:
    CE = ("pe", "act", "dve", "pool")
    ALLQ = ("pe", "act", "dve", "pool", "sp")

    def __init__(self, nc, n_dma_sems=32):
        self.nc = nc
        self.ops = {e: [] for e in self.ALLQ}
        self.count = {e: 0 for e in self.CE}
        self.sems = {}
        self.known = {e: {} for e in self.ALLQ}
        self.n_dma = n_dma_sems
        self.dma_cnt = [0] * n_dma_sems
        self.dma_rr = 0
        self.dma_rr_pool = 0
        self._stack = []
        for e in self.CE:
            self.sems[e] = self._sem("s_" + e)
        for i in range(n_dma_sems):
            self.sems[("dma", i)] = self._sem("s_dma%d" % i)
        self.n_ops = 0

    def _sem(self, name):
        cm = self.nc.semaphore(name)
        h = cm.__enter__()
        self._stack.append(cm)
        return h

    def _gather(self, eng, R, W, WP):
        need = {}
        for b in R:
            for k, v in b.w.items():
                if need.get(k, 0) < v:
                    need[k] = v
        for b in list(W) + list(WP):
            for k, v in b.w.items():
                if need.get(k, 0) < v:
                    need[k] = v
            for k, v in b.r.items():
                if need.get(k, 0) < v:
                    need[k] = v
        out = []
        kn = self.known[eng]
        for k, v in need.items():
            if eng == "pe" and k == "pe":
                continue
            if kn.get(k, 0) >= v:
                continue
            kn[k] = v
            out.append((k, v))
        return out

    def _record(self, tok, R, W, WP):
        k, v = tok
        for b in W:
            b.w = {k: v}
            b.r = {}
        for b in WP:
            if b.w.get(k, 0) < v:
                b.w[k] = v
        for b in R:
            if b.r.get(k, 0) < v:
                b.r[k] = v

    def op(self, eng, fn, R=(), W=(), WP=(), inc=True):
        waits = self._gather(eng, R, W, WP)
        if inc:
            self.count[eng] += 1
            ms = self.count[eng]
        else:
            ms = self.count[eng] + 1
        self.ops[eng].append((fn, waits, (eng, 1) if inc else None))
        self._record((eng, ms), R, W, WP)
        self.n_ops += 1

    def dma(self, q, out, in_, R=(), W=(), WP=()):
        half = self.n_dma // 2
        if q == "pool":
            s = half + self.dma_rr_pool
            self.dma_rr_pool = (self.dma_rr_pool + 1) % half
        else:
            s = self.dma_rr
            self.dma_rr = (self.dma_rr + 1) % half
        key = ("dma", s)
        waits = self._gather(q, R, W, WP)
        prev = 16 * self.dma_cnt[s]
        if prev > 0 and self.known[q].get(key, 0) < prev:
            self.known[q][key] = prev
            waits.append((key, prev))
        self.dma_cnt[s] += 1
        val = 16 * self.dma_cnt[s]

        def fn(e, out=out, in_=in_):
            return e.dma_start(out=out, in_=in_)
        self.ops[q].append((fn, waits, (key, 16)))
        self._record((key, val), R, W, WP)
        self.n_ops += 1

    def barrier(self):
        for e in self.ALLQ:
            waits = []
            kn = self.known[e]
            for c in self.CE:
                if c == e:
                    continue
                v = self.count[c]
                if v > 0 and kn.get(c, 0) < v:
                    kn[c] = v
                    waits.append((c, v))
            for i in range(self.n_dma):
                v = 16 * self.dma_cnt[i]
                k = ("dma", i)
                if v > 0 and kn.get(k, 0) < v:
                    kn[k] = v
                    waits.append((k, v))
            if waits:
                self.ops[e].append((None, waits, None))

    def emit(self):
        nc = self.nc
        sems = self.sems
        ops = self.ops

        def run(e, lst):
            for fn, waits, inc in lst:
                for k, v in waits:
                    e.wait_ge(sems[k], v)
                if fn is None:
                    continue
                ins = fn(e)
                if inc is not None:
                    ins.then_inc(sems[inc[0]], inc[1])

        with nc.Block() as blk:
            @blk.sync
            def _(e):
                run(e, ops["sp"])

            @blk.tensor
            def _(e):
                run(e, ops["pe"])

            @blk.scalar
            def _(e):
                run(e, ops["act"])

            @blk.vector
            def _(e):
                run(e, ops["dve"])

            @blk.gpsimd
            def _(e):
                run(e, ops["pool"])


class Arena:
    def __init__(self, nc, nbytes):
        self.t = nc.alloc_sbuf_tensor("arena", [128, nbytes], U8)
        self.n = nbytes
        self.off = 0

    def reset(self, off=0):
        self.off = off

    def alloc(self, free_shape, dtype):
        n = int(np.prod(free_shape))
        nb = n * mybir.dt.size(dtype)
        nb = (nb + 63) // 64 * 64
        assert self.off + nb <= self.n, ("arena overflow", self.off, nb, self.n)
        ap = self.t[:, self.off:self.off + nb].bitcast(dtype)[:, 0:n]
        self.off += nb
        if len(free_shape) == 2:
            ap = ap.rearrange("p (a b) -> p a b", a=free_shape[0])
        elif len(free_shape) == 3:
            ap = ap.rearrange("p (a b c) -> p a b c", a=free_shape[0], b=free_shape[1])
        return ap


def _mask(kk, W):
    kp = np.arange(128)[:, None]
    qf = np.arange(512)[None, :]
    dlt = qf - 128 * kk - kp
    ok = dlt >= 0
    if W is not None:
        ok &= dlt < W
    return np.where(ok, 0.0, NEG).astype(np.float32)


MASK_CAUSAL = {kk: kk for kk in range(4)}
MASK_SWA = {kk: 4 + (kk + 1) for kk in range(-1, 4)}
MASK_WIN = {kk: 9 + (kk + 4) for kk in range(-4, 0)}
for _kk in range(4):
    MASK_WIN[_kk] = MASK_CAUSAL[_kk]
NMASK = 13


def _consts():
    bf = ml_dtypes.bfloat16
    masks = np.zeros((128, NMASK, 512), np.float32)
    for kk in range(4):
        masks[:, MASK_CAUSAL[kk]] = _mask(kk, None)
    for kk in range(-1, 4):
        masks[:, MASK_SWA[kk]] = _mask(kk, 128)
    for kk in range(-4, 0):
        masks[:, MASK_WIN[kk]] = _mask(kk, 512)
    n = np.arange(256)[:, None]
    q = np.arange(S)[None, :]
    cm = np.where(16 * n + 31 <= q, 0.0, NEG).astype(np.float32)
    cmask = np.concatenate([cm[0:128, 0:2560], cm[128:256, 2048:4096]], axis=1)
    ov = np.zeros((256, 64), np.float32)
    for nn in range(255):
        a0, a1 = 16 * nn, 16 * nn + 32
        for s in range(64):
            o = min(a1, 64 * s + 64) - max(a0, 64 * s)
            if o > 0:
                ov[nn, s] = o / 32.0
    ovl = np.stack([ov[0:128], ov[128:256]], axis=1)
    ebig = np.zeros((128, S), np.float32)
    ebig[64 + (np.arange(S) // 64), np.arange(S)] = 1.0
    qq = np.arange(S)
    qb = qq // 64
    s = np.arange(64)[None, :]
    causal = s <= qb[:, None]
    forced = (s == 0) | (s == qb[:, None]) | (s == qb[:, None] - 1)
    cmul = causal.astype(np.float32)
    cadd = np.where(causal, np.where(forced, 1e4, 0.0), -1.0).astype(np.float32)
    cmul = cmul.reshape(32, 128, 64).transpose(1, 0, 2)
    cadd = cadd.reshape(32, 128, 64).transpose(1, 0, 2)
    sel3 = np.zeros((128, 48, 64), np.float32)
    for r in range(48):
        sel3[r, r, :] = 1.0
    half = 32
    inv = (10000.0 ** (-np.arange(half, dtype=np.float32) / half)).astype(np.float32)
    invf = np.tile(inv, 4)[:, None].astype(np.float32)
    sgn = np.where((np.arange(128) % 64) < 32, -1.0, 1.0).astype(np.float32)[:, None]
    return {
        "c_ident": np.eye(128, dtype=np.float32),
        "c_masks": masks.astype(bf),
        "c_cmask": cmask.astype(bf),
        "c_ovl": ovl.astype(bf),
        "c_ebig": ebig.astype(bf),
        "c_cmul": np.ascontiguousarray(cmul).astype(bf),
        "c_cadd": np.ascontiguousarray(cadd).astype(bf),
        "c_sel3": sel3.astype(bf),
        "c_selst": np.where((np.arange(128)[:, None] - 64) <= (np.arange(1024)[None, :] // 64), 0.0, NEG).astype(np.float32).astype(bf),
        "c_invf": invf,
        "c_sgn": sgn,
    }


def _swap64(w):
    k, n = w.shape
    return np.ascontiguousarray(w.reshape(k, n // 64, 2, 32)[:, :, ::-1, :].reshape(k, n))


def _fm(v, nchunk):
    return np.ascontiguousarray(np.asarray(v, np.float32).reshape(nchunk, 128).T)


MLA_COLS = dict(qa=0, kva=256, kpe=384, kpes=448, gate=512, n=1536)
SWA_COLS = dict(q=0, qs=1024, k=2048, ks=2176, v=2304, gate=2432, n=3456)
NSA_COLS = dict(q=0, qs=1024, kslc=2048, kslcs=2304, kwin=2560, kwins=2816, kcmp=3072, vcmp=3328,
                vslc=3584, vwin=3840, gbr=4096, gate=4144, n=5168)


def _prep_weights(inp):
    out = {}
    for j in range(2):
        w = inp["mla_w_in"][j]
        kpe = w[:, 384:448]
        out["mla%d_wa" % j] = np.ascontiguousarray(np.concatenate(
            [w[:, 0:256], w[:, 256:384], kpe, _swap64(kpe), w[:, 448:1472]], axis=1))
        qb = inp["mla_w_q_b"][j].reshape(256, 8, 192)
        out["mla%d_wqn" % j] = np.ascontiguousarray(qb[:, :, 0:128].reshape(256, 1024))
        qr = np.ascontiguousarray(qb[:, :, 128:192].reshape(256, 512))
        out["mla%d_wqr" % j] = qr
        out["mla%d_wqrs" % j] = _swap64(qr)
        kvb = inp["mla_w_kv_b"][j].reshape(128, 8, 256)
        out["mla%d_wkn" % j] = np.ascontiguousarray(kvb[:, :, 0:128].reshape(128, 1024))
        out["mla%d_wv" % j] = np.ascontiguousarray(kvb[:, :, 128:256].reshape(128, 1024))
        out["mla%d_wout" % j] = np.ascontiguousarray(inp["mla_w_out"][j])
        out["mla%d_qg" % j] = _fm(inp["mla_q_norm_g"][j], 2)
        out["mla%d_kvg" % j] = _fm(inp["mla_kv_norm_g"][j], 1)
    w = inp["swa_w_in"][0]
    q, k, v, g = w[:, 0:1024], w[:, 1024:1152], w[:, 1152:1280], w[:, 1280:2304]
    out["swa_wa"] = np.ascontiguousarray(np.concatenate([q, _swap64(q), k, _swap64(k), v, g], axis=1))
    out["swa_wout"] = np.ascontiguousarray(inp["swa_w_out"][0])
    out["swa_sinks"] = np.ascontiguousarray(inp["swa_sinks"][0].reshape(1, 16).astype(np.float32))
    w = inp["nsa_w_in"][0]
    q = w[:, 0:1024]
    kcmp, vcmp, kslc, vslc, kwin, vwin = [w[:, 1024 + 256 * i:1280 + 256 * i] for i in range(6)]
    gbr = w[:, 2560:2608]
    g = w[:, 2608:3632]
    out["nsa_wa"] = np.ascontiguousarray(np.concatenate(
        [q, _swap64(q), kslc, _swap64(kslc), kwin, _swap64(kwin), kcmp, vcmp, vslc, vwin, gbr, g], axis=1))
    out["nsa_wout"] = np.ascontiguousarray(inp["nsa_w_out"][0])
    out["nsa_pe"] = np.ascontiguousarray(inp["nsa_cmp_pos"][0].reshape(16, 128).T.astype(np.float32))
    out["nsa_wk1"] = np.ascontiguousarray(inp["nsa_w_cmp_k1"][0])
    out["nsa_wk2"] = np.ascontiguousarray(inp["nsa_w_cmp_k2"][0])
    out["nsa_wv1"] = np.ascontiguousarray(inp["nsa_w_cmp_v1"][0])
    out["nsa_wv2"] = np.ascontiguousarray(inp["nsa_w_cmp_v2"][0])
    out["ada_w"] = np.ascontiguousarray(inp["ada_w"])
    out["ada_b"] = np.ascontiguousarray(inp["ada_b"].reshape(4, 24, 128).transpose(2, 0, 1))
    out["ada_b_row"] = np.ascontiguousarray(inp["ada_b"].astype(np.float32))
    out["norm_g"] = np.ascontiguousarray(inp["norm_g"].reshape(4, 8, 128).transpose(2, 0, 1))
    out["final_g"] = np.ascontiguousarray(inp["final_norm_g"].reshape(1, 1024))
    return out


class K:
    pass


def build(n_layers=DEPTH, shapes=None):
    nc = bass.Bass("TRN2", target_bir_lowering=False)
    P = Prog(nc)
    k = K()
    k.nc, k.P = nc, P
    dram_in = {}

    def din(name, shape, dt):
        dram_in[name] = nc.dram_tensor(name, list(shape), dt, kind="ExternalInput").ap()
        return dram_in[name]

    for name, (shape, dt) in shapes.items():
        din(name, shape, dt)
    k.din = dram_in
    out = nc.dram_tensor("out", [S, D], F32, kind="ExternalOutput").ap()
    k.out = out
    k.xr = nc.dram_tensor("xr", [S, D], F32).ap()
    k.QT = nc.dram_tensor("QT", [1536, S], BF16).ap()
    k.KT = nc.dram_tensor("KT", [1280, S], BF16).ap()
    k.VT = nc.dram_tensor("VT", [8, 128, NT, 128], BF16).ap()
    k.VS = nc.dram_tensor("VS", [8, 128, NT, 64], BF16).ap()
    k.GT = nc.dram_tensor("GT", [1024 + 128, S], BF16).ap()
    k.OT = nc.dram_tensor("OT", [1024, S], BF16).ap()
    k.OC = nc.dram_tensor("OC", [1024, S], F32).ap()
    k.modrow = nc.dram_tensor("modrow", [4, 3072], F32).ap()
    k.xr_b = [Buf() for _ in range(NT)]
    k.QT_b, k.KT_b, k.VT_b, k.GT_b, k.OT_b, k.OC_b, k.modrow_b, k.out_b, k.VS_b = [Buf() for _ in range(9)]

    sb = nc.alloc_sbuf_tensor
    k.ident = sb("ident", [128, 128], F32); k.ident_b = Buf()
    k.identb = sb("identb", [128, 128], BF16); k.identb_b = Buf()
    k.onesb = sb("onesb", [128, 128], BF16); k.onesb_b = Buf()
    k.onesf = sb("onesf", [128, 128], F32); k.onesf_b = Buf()
    k.masks = sb("masks", [128, NMASK, 512], BF16); k.masks_b = Buf()
    k.ropeD = nc.dram_tensor("ropeD", [2, 128, S], F32).ap()
    k.rope_b = Buf()
    k.mod = sb("mod", [128, 4, 24], F32); k.mod_b = Buf()
    k.gmod = sb("gmod", [128, 4, 8], F32); k.gmod_b = Buf()
    k.ps = [nc.alloc_psum_tensor("ps%d" % i, [128, 512], F32) for i in range(8)]
    k.ps_b = [Buf() for _ in range(8)]
    k.arena = Arena(nc, 160 * 1024)

    prologue(k)
    import os
    kinds = os.environ.get("K_KINDS", "mla,swa,nsa,mla").split(",")
    for i in range(n_layers):
        kind = kinds[i]
        j = i // 3
        last = (i == n_layers - 1)
        if kind == "mla":
            mla_layer(k, i, j, last)
        elif kind == "swa":
            swa_layer(k, i, last)
        else:
            nsa_layer(k, i, last)
    P.barrier()
    P.emit()
    return nc, P


def prologue(k):
    P, nc, A = k.P, k.nc, k.arena
    din = k.din
    A.reset()
    P.dma("sp", k.ident[:], din["c_ident"], W=[k.ident_b])
    P.dma("sp", k.masks[:], din["c_masks"], W=[k.masks_b])
    P.op("pool", lambda e: e.tensor_copy(out=k.identb[:], in_=k.ident[:]), R=[k.ident_b], W=[k.identb_b])
    P.op("pool", lambda e: e.memset(k.onesb[:], 1.0), W=[k.onesb_b])
    P.op("pool", lambda e: e.memset(k.onesf[:], 1.0), W=[k.onesf_b])
    cfm = A.alloc([8], F32); cfm_b = Buf()
    cond = A.alloc([8], F32); cond_b = Buf()
    adab = A.alloc([4, 24], F32); adab_b = Buf()
    ng = A.alloc([4, 8], F32); ng_b = Buf()
    P.dma("sp", cfm, din["c_fm"], W=[cfm_b])
    P.dma("sp", adab, din["ada_b"], W=[adab_b])
    P.dma("sp", ng, din["norm_g"], W=[ng_b])
    P.op("act", lambda e: e.activation(out=cond, in_=cfm, func=AF.Silu), R=[cfm_b], W=[cond_b])
    wst = [A.alloc([8, 512], F32) for _ in range(2)]
    wst_b = [Buf() for _ in range(2)]
    modr = A.alloc([3072], F32); modr_b = Buf()
    abr = A.alloc([3072], F32); abr_b = Buf()
    one1 = A.alloc([1], F32); one1_b = Buf()
    P.op("pool", lambda e: e.memset(one1, 1.0), W=[one1_b])
    n = 0
    for i in range(DEPTH):
        P.dma("sp", abr[0:1, :], din["ada_b_row"][i:i + 1, :], W=[abr_b])
        for mg in range(6):
            b = n % 2
            n += 1
            P.dma("sp", wst[b], din["ada_w"][i, :, mg * 512:(mg + 1) * 512].rearrange("(c p) n -> p c n", p=128), W=[wst_b[b]])
            pr, pr_b = k.ps[n % 4], k.ps_b[n % 4]
            for c in range(8):
                P.op("pe", lambda e, b=b, c=c, pr=pr: e.matmul(pr[0:1, :], lhsT=cond[:, c:c + 1], rhs=wst[b][:, c, :], start=(c == 0), stop=(c == 7)),
                     R=[wst_b[b], cond_b], W=[pr_b], inc=(c == 7))
            P.op("act", lambda e, mg=mg, pr=pr: e.activation(out=modr[0:1, mg * 512:(mg + 1) * 512], in_=pr[0:1, :], func=AF.Copy), R=[pr_b], WP=[modr_b])
        P.op("pool", lambda e: e.tensor_tensor(out=modr[0:1, :], in0=modr[0:1, :], in1=abr[0:1, :], op=ALU.add), R=[abr_b], WP=[modr_b])
        P.dma("sp", k.modrow[i:i + 1, :], modr[0:1, :], R=[modr_b], WP=[k.modrow_b])
        pm, pm_b = k.ps[4 + i % 2], k.ps_b[4 + i % 2]
        for m in range(24):
            P.op("pe", lambda e, m=m, pm=pm: e.matmul(pm[:, m:m + 1], lhsT=modr[0:1, m * 128:(m + 1) * 128], rhs=one1[0:1, 0:1], start=True, stop=True),
                 R=[modr_b, one1_b], WP=[pm_b], inc=(m == 23))
        P.op("act", lambda e, i=i, pm=pm: e.activation(out=k.mod[:, i, :], in_=pm[:, 0:24], func=AF.Copy), R=[pm_b], WP=[k.mod_b])
        P.op("pool", lambda e, i=i: e.tensor_scalar(out=k.gmod[:, i, :], in0=k.mod[:, i, 8:16], scalar1=1.0, scalar2=None, op0=ALU.add),
             R=[k.mod_b], WP=[k.gmod_b])
        P.op("pool", lambda e, i=i: e.tensor_tensor(out=k.gmod[:, i, :], in0=k.gmod[:, i, :], in1=ng[:, i, :], op=ALU.mult),
             R=[ng_b], WP=[k.gmod_b])
    posi = A.alloc([S], I32); posi_b = Buf()
    ang = A.alloc([S], F32); ang_b = Buf()
    kf = A.alloc([S], F32); kf_b = Buf()
    ki = A.alloc([S], I32); ki_b = Buf()
    rr = A.alloc([S], F32); rr_b = Buf()
    ivf = A.alloc([1], F32); ivf_b = Buf()
    sgn = A.alloc([1], F32); sgn_b = Buf()
    P.dma("sp", posi, din["pos"].partition_broadcast(128), W=[posi_b])
    P.dma("sp", ivf, din["c_invf"], W=[ivf_b])
    P.dma("sp", sgn, din["c_sgn"], W=[sgn_b])
    P.op("dve", lambda e: e.tensor_copy(out=kf, in_=posi), R=[posi_b], W=[kf_b])
    P.op("dve", lambda e: e.tensor_scalar(out=ang, in0=kf, scalar1=ivf[:, 0:1], scalar2=None, op0=ALU.mult),
         R=[kf_b, ivf_b], W=[ang_b])
    TWO_PI = 2 * np.pi
    c1 = float(np.float32(6.28125))
    c2 = float(TWO_PI - 6.28125)
    for dst, shift, use_sgn in ((0, np.pi / 2, False), (1, 0.0, True)):
        P.op("dve", lambda e, shift=shift: e.tensor_scalar(out=kf, in0=ang, scalar1=float(shift), scalar2=float(1.0 / TWO_PI),
                                                          op0=ALU.add, op1=ALU.mult), R=[ang_b], W=[kf_b])
        P.op("dve", lambda e: e.tensor_copy(out=ki, in_=kf), R=[kf_b], W=[ki_b])
        P.op("dve", lambda e: e.tensor_copy(out=kf, in_=ki), R=[ki_b], W=[kf_b])
        P.op("dve", lambda e: e.scalar_tensor_tensor(out=rr, in0=kf, scalar=-c1, in1=ang, op0=ALU.mult, op1=ALU.add),
             R=[kf_b, ang_b], W=[rr_b])
        P.op("dve", lambda e: e.scalar_tensor_tensor(out=rr, in0=kf, scalar=-c2, in1=rr, op0=ALU.mult, op1=ALU.add),
             R=[kf_b, rr_b], W=[rr_b])
        P.op("dve", lambda e, shift=shift: e.tensor_scalar(out=kf, in0=rr, scalar1=float(shift), scalar2=float(np.pi),
                                                          op0=ALU.add, op1=ALU.is_gt), R=[rr_b], W=[kf_b])
        P.op("dve", lambda e, shift=shift: e.tensor_scalar(out=rr, in0=rr, scalar1=float(shift), scalar2=None, op0=ALU.add),
             R=[rr_b], W=[rr_b])
        P.op("dve", lambda e: e.scalar_tensor_tensor(out=rr, in0=kf, scalar=-TWO_PI, in1=rr, op0=ALU.mult, op1=ALU.add),
             R=[kf_b, rr_b], W=[rr_b])
        P.op("dve", lambda e: e.tensor_scalar(out=kf, in0=rr, scalar1=float(-np.pi), scalar2=None, op0=ALU.is_lt),
             R=[rr_b], W=[kf_b])
        P.op("dve", lambda e: e.scalar_tensor_tensor(out=rr, in0=kf, scalar=TWO_PI, in1=rr, op0=ALU.mult, op1=ALU.add),
             R=[kf_b, rr_b], W=[rr_b])
        P.op("dve", lambda e: e.tensor_scalar(out=rr, in0=rr, scalar1=3.141592, scalar2=-3.141592, op0=ALU.min, op1=ALU.max),
             R=[rr_b], W=[rr_b])
        if use_sgn:
            P.op("act", lambda e: e.activation(out=rr, in_=rr, func=AF.Sin, scale=sgn[:, 0:1]),
                 R=[rr_b, sgn_b], W=[rr_b])
        else:
            P.op("act", lambda e: e.activation(out=rr, in_=rr, func=AF.Sin), R=[rr_b], W=[rr_b])
        P.dma("sp", k.ropeD[dst], rr, R=[rr_b], WP=[k.rope_b])
    P.barrier()


def load_cast_weights(k, dst, src, ncols, kchunks, stg, stg_b, dst_b, cw=256):
    P = k.P
    n = 0
    for c0 in range(0, ncols, cw):
        w = min(cw, ncols - c0)
        b = k.stg_n % 2
        k.stg_n += 1
        P.dma("sp", stg[b][:, 0:kchunks, 0:w], src[:, c0:c0 + w].rearrange("(c p) n -> p c n", p=128), W=[stg_b[b]])
        if n % 2 == 0:
            P.op("act", lambda e, b=b, c0=c0, w=w: e.activation(out=dst[:, :, c0:c0 + w], in_=stg[b][:, 0:kchunks, 0:w], func=AF.Copy),
                 R=[stg_b[b]], WP=[dst_b])
        else:
            P.op("dve", lambda e, b=b, c0=c0, w=w: e.tensor_copy(out=dst[:, :, c0:c0 + w], in_=stg[b][:, 0:kchunks, 0:w]),
                 R=[stg_b[b]], WP=[dst_b])
        n += 1


class Group:
    def __init__(self, k, a, dst_rows, dst_b, nch, rows_last=128):
        i = a.grp_n % 2
        a.grp_n += 1
        self.k, self.buf, self.b = k, a.gst[i], a.gst_b[i]
        self.dst_rows, self.dst_b, self.nch, self.rows_last = dst_rows, dst_b, nch, rows_last

    def slot(self, j):
        return self.buf[:, j, :]

    def flush(self):
        P = self.k.P
        if self.rows_last == 128:
            P.dma("sp", self.dst_rows.rearrange("(c p) t -> p c t", p=128), self.buf[:, 0:self.nch, :], R=[self.b], WP=[self.dst_b])
        else:
            assert self.nch == 1
            P.dma("sp", self.dst_rows, self.buf[0:self.rows_last, 0, :], R=[self.b], WP=[self.dst_b])


def phase_a_common(k, li, vshape):
    A, P = k.arena, k.P
    a = K()
    a.xt4s = [A.alloc([4, D], F32) for _ in range(2)]
    a.xt4_bss = [[Buf(), Buf()], [Buf(), Buf()]]
    xflat = a.xt4s[1].rearrange("p s d -> p (s d)")
    a.stg = [xflat[:, i * 2048:(i + 1) * 2048].rearrange("p (c n) -> p c n", c=8) for i in range(2)]
    a.stg_b = a.xt4_bss[1]
    a.junk = A.alloc([D], F32); a.junk_b = Buf()
    a.ss = A.alloc([16], F32); a.ss_b = Buf()
    a.hT = [A.alloc([8, 512], BF16) for _ in range(2)]
    a.hT_b = [Buf() for _ in range(2)]
    a.gst = [A.alloc([4, 512], BF16) for _ in range(2)]
    a.gst_b = [Buf() for _ in range(2)]
    a.grp_n = 0
    a.vst = A.alloc([vshape[0], 4, vshape[1]], BF16); a.vst_b = Buf()
    a.rt = [A.alloc([512], F32) for _ in range(2)]
    a.rt_b = [Buf() for _ in range(2)]
    a.rcs = [A.alloc([2, 512], F32) for _ in range(2)]
    a.rcs_b = [Buf() for _ in range(2)]
    k.stg_n = 0
    a.mm_n = 0
    a.tr_n = 0
    return a


def hT_load(k, a, st, xsrc, xsrc_b):
    P = k.P
    hb = st % 2
    P.dma("sp", a.xt4s[hb], xsrc[st * 512:(st + 1) * 512, :].rearrange("(s p) d -> p s d", p=128),
          R=[xsrc_b[st * 4 + s_] for s_ in range(4)], W=list(a.xt4_bss[hb]))


def hT_front(k, a, st):
    P = k.P
    hb = st % 2
    xt4, xt4_bs, ss, ss_b = a.xt4s[hb], a.xt4_bss[hb], a.ss, a.ss_b
    for sub in range(4):
        P.op("act", lambda e, sub=sub: e.activation(out=a.junk, in_=xt4[:, sub, :], func=AF.Square, accum_out=ss[:, sub:sub + 1]),
             R=list(xt4_bs), W=[a.junk_b], WP=[ss_b])
    P.op("dve", lambda e: e.tensor_scalar(out=ss[:, 4:8], in0=ss[:, 0:4], scalar1=1.0 / D, scalar2=EPS, op0=ALU.mult, op1=ALU.add),
         R=[ss_b], WP=[ss_b])
    P.op("act", lambda e: e.activation(out=ss[:, 8:12], in_=ss[:, 4:8], func=AF.Sqrt), R=[ss_b], WP=[ss_b])
    P.op("dve", lambda e: e.reciprocal(out=ss[:, 12:16], in_=ss[:, 8:12]), R=[ss_b], WP=[ss_b])
    for sub in range(4):
        P.op("dve", lambda e, sub=sub: e.tensor_scalar(out=xt4[:, sub, :], in0=xt4[:, sub, :], scalar1=ss[:, 12 + sub:13 + sub], scalar2=None, op0=ALU.mult),
             R=[ss_b], WP=list(xt4_bs))


def hT_back(k, a, li, st):
    P = k.P
    hb = st % 2
    hT, hT_b = a.hT[hb], a.hT_b[hb]
    xt4, xt4_bs = a.xt4s[hb], a.xt4_bss[hb]
    for sub in range(4):
        for half in range(2):
            pi = 4 + a.tr_n % 4
            a.tr_n += 1
            pt, pt_b = k.ps[pi], k.ps_b[pi]
            for c4 in range(4):
                c = half * 4 + c4
                P.op("pe", lambda e, sub=sub, c=c, c4=c4, pt=pt: e.transpose(out=pt[:, c4 * 128:(c4 + 1) * 128], in_=xt4[:, sub, c * 128:(c + 1) * 128], identity=k.ident[:]),
                     R=list(xt4_bs) + [k.ident_b], W=[pt_b], inc=(c4 == 3))
            for c4 in range(4):
                c = half * 4 + c4
                o = hT[:, c, sub * 128:(sub + 1) * 128]
                i_ = pt[:, c4 * 128:(c4 + 1) * 128]
                if c % 2 == 0:
                    P.op("act", lambda e, o=o, i_=i_, c=c: e.activation(out=o, in_=i_, func=AF.Identity, scale=k.gmod[:, li, c:c + 1], bias=k.mod[:, li, c:c + 1]),
                         R=[pt_b, k.gmod_b, k.mod_b], WP=[hT_b])
                else:
                    P.op("dve", lambda e, o=o, i_=i_, c=c: e.tensor_scalar(out=o, in0=i_, scalar1=k.gmod[:, li, c:c + 1], scalar2=k.mod[:, li, c:c + 1], op0=ALU.mult, op1=ALU.add),
                         R=[pt_b, k.gmod_b, k.mod_b], WP=[hT_b])
    return hT, hT_b, a.rcs[hb], a.rcs_b[hb]


def rcs_load(k, a, st):
    hb = st % 2
    k.P.dma("sp", a.rcs[hb], k.ropeD[:, :, st * 512:(st + 1) * 512].rearrange("j p t -> p j t"), R=[k.rope_b], W=[a.rcs_b[hb]])


def hT_prime(k, a, li, xsrc, xsrc_b):
    rcs_load(k, a, 0)
    hT_load(k, a, 0, xsrc, xsrc_b)
    hT_load(k, a, 1, xsrc, xsrc_b)
    hT_front(k, a, 0)
    return hT_back(k, a, li, 0)


def hT_step_begin(k, a, st, xsrc, xsrc_b):
    if st + 1 < NST:
        rcs_load(k, a, st + 1)
        hT_front(k, a, st + 1)


def hT_step_mid(k, a, li, st, xsrc, xsrc_b):
    nxt = None
    if st + 1 < NST:
        nxt = hT_back(k, a, li, st + 1)
    if st + 2 < NST:
        hT_load(k, a, st + 2, xsrc, xsrc_b)
    return nxt


def mm_psum(k, a):
    i = a.mm_n % 4
    a.mm_n += 1
    return k.ps[i], k.ps_b[i]


def fm_group(k, pm, pm_b, w, w_b, kch, col0, ncols, rhs_fn, rhs_b):
    P = k.P
    for c in range(kch):
        P.op("pe", lambda e, c=c: e.matmul(pm[0:ncols, :], lhsT=w[:, c, col0:col0 + ncols], rhs=rhs_fn(c), start=(c == 0), stop=(c == kch - 1)),
             R=[w_b] + list(rhs_b), W=[pm_b], inc=(c == kch - 1))


def job_fm(k, a, w, w_b, kch, col0, ncols, rhs_fn, rhs_b, out, out_b, act=None, eng="act"):
    P = k.P
    pm, pm_b = mm_psum(k, a)
    fm_group(k, pm, pm_b, w, w_b, kch, col0, ncols, rhs_fn, rhs_b)
    o = out[0:ncols, :]
    if act is not None:
        P.op("act", lambda e: e.activation(out=o, in_=pm[0:ncols, :], func=act), R=[pm_b], WP=[out_b])
    elif eng == "act":
        P.op("act", lambda e: e.activation(out=o, in_=pm[0:ncols, :], func=AF.Copy), R=[pm_b], WP=[out_b])
    else:
        P.op("dve", lambda e: e.tensor_copy(out=o, in_=pm[0:ncols, :]), R=[pm_b], WP=[out_b])


def job_rope(k, a, w, w_b, kch, col0, w2, w2_b, cols0, ncols, rhs_fn, rhs_b, rcs, rcs_b, out, out_b):
    P = k.P
    pm, pm_b = mm_psum(k, a)
    fm_group(k, pm, pm_b, w, w_b, kch, col0, ncols, rhs_fn, rhs_b)
    pm2, pm2_b = mm_psum(k, a)
    fm_group(k, pm2, pm2_b, w2, w2_b, kch, cols0, ncols, rhs_fn, rhs_b)
    r0, r0_b, r1, r1_b = a.rt[0], a.rt_b[0], a.rt[1], a.rt_b[1]
    P.op("dve", lambda e: e.tensor_tensor(out=r0[0:ncols, :], in0=pm[0:ncols, :], in1=rcs[0:ncols, 0, :], op=ALU.mult),
         R=[pm_b, rcs_b], W=[r0_b])
    P.op("dve", lambda e: e.tensor_tensor(out=r1[0:ncols, :], in0=pm2[0:ncols, :], in1=rcs[0:ncols, 1, :], op=ALU.mult),
         R=[pm2_b, rcs_b], W=[r1_b])
    P.op("pool", lambda e: e.tensor_tensor(out=out[0:ncols, :], in0=r0[0:ncols, :], in1=r1[0:ncols, :], op=ALU.add),
         R=[r0_b, r1_b], WP=[out_b])


def job_tm(k, a, w, w_b, kch, col0, ncols, lhs_fn, lhs_b, out, out_b):
    P = k.P
    pm, pm_b = mm_psum(k, a)
    for c in range(kch):
        P.op("pe", lambda e, c=c: e.matmul(pm[:, 0:ncols], lhsT=lhs_fn(c), rhs=w[:, c, col0:col0 + ncols], start=(c == 0), stop=(c == kch - 1)),
             R=[w_b] + list(lhs_b), W=[pm_b], inc=(c == kch - 1))
    dv = out.shape[-1]
    P.op("act", lambda e: e.activation(out=out, in_=pm[:, 0:ncols].rearrange("p (h d) -> p h d", d=dv), func=AF.Copy), R=[pm_b], WP=[out_b])


def band_cols(kk, W):
    lo = max(0, 128 * kk)
    hi = 512 if W is None else min(512, 128 * kk + 127 + W)
    return (lo, hi)


def attention_rows(k, rows, pt_bufs, scale, s_banks=(0, 1, 2), look=2):
    P = k.P
    steps = []
    pending = []
    for ri, row in enumerate(rows):
        n = len(row["ksteps"])
        for si, stp in enumerate(row["ksteps"]):
            steps.append((ri, si, n, stp))

    def issue_s(idx):
        ri, si, n, stp = steps[idx]
        sb_i = s_banks[idx % len(s_banks)]
        ps, ps_b = k.ps[sb_i], k.ps_b[sb_i]
        kp = stp.get("kp", 128)
        c0, c1 = stp.get("cols", (0, 512))
        ns = len(stp["s"])
        for j, (lhsT, rhs, bufs) in enumerate(stp["s"]):
            P.op("pe", lambda e, lhsT=lhsT, rhs=rhs, j=j, ps=ps, kp=kp, c0=c0, c1=c1: e.matmul(ps[0:kp, c0:c1], lhsT=lhsT, rhs=rhs[:, c0:c1], start=(j == 0), stop=(j == ns - 1)),
                 R=list(bufs), W=[ps_b], inc=(j == ns - 1))
        pt, pt_b = pt_bufs[idx % len(pt_bufs)]
        P.op("act", lambda e, pt=pt, ps=ps, kp=kp, c0=c0, c1=c1: e.activation(out=pt[0:kp, c0:c1], in_=ps[0:kp, c0:c1], func=AF.Exp, scale=float(scale)),
             R=[ps_b], W=[pt_b])

    def issue_pv(idx):
        ri, si, n, stp = steps[idx]
        pt, pt_b = pt_bufs[idx % len(pt_bufs)]
        kp = stp.get("kp", 128)
        c0, c1 = stp.get("cols", (0, 512))
        nob = rows[ri].get("o_banks", 2)
        par = ri % nob
        dacc = rows[ri].get("dacc")
        if dacc is not None:
            acc_t, acc_b_ = dacc
            if si == 0:
                P.op("dve", lambda e, pt=pt, kp=kp, c0=c0, c1=c1: e.tensor_copy(out=acc_t[0:kp, c0:c1], in_=pt[0:kp, c0:c1]), R=[pt_b], W=[acc_b_])
            else:
                P.op("dve", lambda e, pt=pt, kp=kp, c0=c0, c1=c1: e.tensor_tensor(out=acc_t[0:kp, c0:c1], in0=acc_t[0:kp, c0:c1], in1=pt[0:kp, c0:c1], op=ALU.add),
                     R=[pt_b], WP=[acc_b_])
        for (which, lhsT, bufs, mrows) in stp["pv"]:
            pi = 7 if which == "A" else (3 + par if which == "O" else 5 + ri % 2)
            po, po_b = k.ps[pi], k.ps_b[pi]
            P.op("pe", lambda e, lhsT=lhsT, po=po, pt=pt, kp=kp, mrows=mrows, si=si, n=n, c0=c0, c1=c1: e.matmul(po[0:mrows, c0:c1], lhsT=lhsT, rhs=pt[0:kp, c0:c1], start=(si == 0), stop=(si == n - 1), skip_group_check=True),
                 R=list(bufs) + [pt_b], W=[po_b], inc=True)
        if si == 0:
            pending.extend(rows[ri].get("pre", ()))
        if si == n - 1:
            pending[:] = [p_ for p_ in pending if p_ is not None]
            while pending:
                pending.pop(0)()
            ret = rows[ri]["epilogue"](k.ps[3 + par], k.ps_b[3 + par], k.ps[5 + ri % 2], k.ps_b[5 + ri % 2])
            if ret:
                pending.extend(ret)
        elif pending:
            p_ = pending.pop(0)
            if p_ is not None:
                p_()

    N = len(steps)
    if N == 0:
        return
    assert len(pt_bufs) >= look + 1 and len(s_banks) >= look + 1
    for j in range(min(look, N)):
        issue_s(j)
    for idx in range(N):
        if idx + look < N:
            issue_s(idx + look)
        issue_pv(idx)
    while pending:
        p_ = pending.pop(0)
        if p_ is not None:
            p_()


def phase_c(k, li, wout_name, last):
    P, nc, A, din = k.P, k.nc, k.arena, k.din
    P.barrier()
    A.reset()
    wo = A.alloc([8, D], BF16); wo_b = Buf()
    stg = [A.alloc([8, 256], F32) for _ in range(2)]
    stg_b = [Buf() for _ in range(2)]
    k.stg_n = 0
    load_cast_weights(k, wo, din[wout_name], D, 8, stg, stg_b, wo_b)
    gbc = A.alloc([D], F32); gbc_b = Buf()
    P.dma("sp", gbc, k.modrow[li:li + 1, 2048:3072].partition_broadcast(128), R=[k.modrow_b], W=[gbc_b])
    if last:
        fg = A.alloc([D], F32); fg_b = Buf()
        P.dma("sp", fg, din["final_g"].partition_broadcast(128), W=[fg_b])
    ots = [A.alloc([8, 512], BF16) for _ in range(2)]
    ots_b = [Buf() for _ in range(2)]
    xt = [A.alloc([D], F32) for _ in range(2)]
    xt_b = [Buf() for _ in range(2)]
    xo = [A.alloc([D], F32) for _ in range(2)]
    xo_b = [Buf() for _ in range(2)]
    junk = A.alloc([D], F32); junk_b = Buf()
    ss = [A.alloc([4], F32) for _ in range(2)]
    ss_b = [Buf() for _ in range(2)]
    xsrc = din["x_in"] if li == 0 else k.xr
    n = 0
    for st in range(NST):
        ob = st % 2
        P.dma("sp", ots[ob], k.OT[:, st * 512:(st + 1) * 512].rearrange("(c p) t -> p c t", p=128), R=[k.OT_b], W=[ots_b[ob]])
        for sub in range(4):
            t = st * 4 + sub
            b = t % 2
            P.dma("sp", xt[b], xsrc[t * 128:(t + 1) * 128, :], R=[k.xr_b[t]], W=[xt_b[b]])
            for half in range(2):
                pm, pm_b = k.ps[n % 4], k.ps_b[n % 4]
                n += 1
                for c in range(8):
                    P.op("pe", lambda e, c=c, ob=ob, sub=sub, half=half, pm=pm: e.matmul(pm[:, :], lhsT=ots[ob][:, c, sub * 128:(sub + 1) * 128], rhs=wo[:, c, half * 512:(half + 1) * 512], start=(c == 0), stop=(c == 7)),
                         R=[ots_b[ob], wo_b], W=[pm_b], inc=(c == 7))
                hs = slice(half * 512, (half + 1) * 512)
                P.op("dve", lambda e, b=b, hs=hs, pm=pm: e.tensor_tensor(out=xo[b][:, hs], in0=pm[:, :], in1=gbc[:, hs], op=ALU.mult),
                     R=[pm_b, gbc_b], WP=[xo_b[b]])
                P.op("pool", lambda e, b=b, hs=hs: e.tensor_tensor(out=xo[b][:, hs], in0=xo[b][:, hs], in1=xt[b][:, hs], op=ALU.add),
                     R=[xt_b[b]], WP=[xo_b[b]])
            if not last:
                P.dma("act", k.xr[t * 128:(t + 1) * 128, :], xo[b], R=[xo_b[b]], W=[k.xr_b[t]])
            else:
                s_, s_b = ss[b], ss_b[b]
                P.op("act", lambda e, b=b, s_=s_: e.activation(out=junk, in_=xo[b], func=AF.Square, accum_out=s_[:, 0:1]),
                     R=[xo_b[b]], W=[junk_b], WP=[s_b])
                P.op("dve", lambda e, s_=s_: e.tensor_scalar(out=s_[:, 1:2], in0=s_[:, 0:1], scalar1=1.0 / D, scalar2=EPS, op0=ALU.mult, op1=ALU.add),
                     R=[s_b], WP=[s_b])
                P.op("act", lambda e, s_=s_: e.activation(out=s_[:, 2:3], in_=s_[:, 1:2], func=AF.Sqrt), R=[s_b], WP=[s_b])
                P.op("dve", lambda e, s_=s_: e.reciprocal(out=s_[:, 3:4], in_=s_[:, 2:3]), R=[s_b], WP=[s_b])
                P.op("dve", lambda e, b=b, s_=s_: e.scalar_tensor_tensor(out=xo[b], in0=xo[b], scalar=s_[:, 3:4], in1=fg, op0=ALU.mult, op1=ALU.mult),
                     R=[s_b, fg_b], WP=[xo_b[b]])
                P.dma("act", k.out[t * 128:(t + 1) * 128, :], xo[b], R=[xo_b[b]], WP=[k.out_b])
    P.barrier()


def mla_layer(k, li, j, last):
    P, nc, A, din = k.P, k.nc, k.arena, k.din
    pre = "mla%d_" % j
    C = MLA_COLS
    xsrc = din["x_in"] if li == 0 else k.xr
    A.reset()
    a = phase_a_common(k, li, (8, 128))
    wa = A.alloc([8, C["n"]], BF16); wa_b = Buf()
    load_cast_weights(k, wa, din[pre + "wa"], C["n"], 8, a.stg, a.stg_b, wa_b)
    wqn = A.alloc([2, 1024], BF16); wqn_b = Buf()
    load_cast_weights(k, wqn, din[pre + "wqn"], 1024, 2, a.stg, a.stg_b, wqn_b)
    wqr = A.alloc([2, 512], BF16); wqr_b = Buf()
    load_cast_weights(k, wqr, din[pre + "wqr"], 512, 2, a.stg, a.stg_b, wqr_b)
    wqrs = A.alloc([2, 512], BF16); wqrs_b = Buf()
    load_cast_weights(k, wqrs, din[pre + "wqrs"], 512, 2, a.stg, a.stg_b, wqrs_b)
    wkn = A.alloc([1, 1024], BF16); wkn_b = Buf()
    load_cast_weights(k, wkn, din[pre + "wkn"], 1024, 1, a.stg, a.stg_b, wkn_b)
    wv = A.alloc([1, 1024], BF16); wv_b = Buf()
    load_cast_weights(k, wv, din[pre + "wv"], 1024, 1, a.stg, a.stg_b, wv_b)
    qg = A.alloc([2], F32); qg_b = Buf()
    kvg = A.alloc([1], F32); kvg_b = Buf()
    P.dma("sp", qg, din[pre + "qg"], W=[qg_b])
    P.dma("sp", kvg, din[pre + "kvg"], W=[kvg_b])
    lat = A.alloc([3, 512], F32); lat_b = Buf()
    sq = A.alloc([3, 512], F32); sq_b = Buf()
    rs = [A.alloc([512], F32) for _ in range(2)]
    rs_b = [Buf() for _ in range(2)]
    latn = A.alloc([3, 512], BF16); latn_b = Buf()
    nxt = hT_prime(k, a, li, xsrc, k.xr_b)
    for st in range(NST):
        tok0 = st * 512
        ts = slice(tok0, tok0 + 512)
        hT, hT_b, rcs, rcs_b = nxt
        rhs_h = lambda c, hT=hT: hT[:, c, :]
        hT_step_begin(k, a, st, xsrc, k.xr_b)
        for ci in range(3):
            pm, pm_b = mm_psum(k, a)
            fm_group(k, pm, pm_b, wa, wa_b, 8, ci * 128, 128, rhs_h, [hT_b])
            P.op("act", lambda e, ci=ci, pm=pm: e.activation(out=lat[:, ci, :], in_=pm[:, :], func=AF.Copy), R=[pm_b], WP=[lat_b])
            P.op("act", lambda e, ci=ci, pm=pm: e.activation(out=sq[:, ci, :], in_=pm[:, :], func=AF.Square), R=[pm_b], WP=[sq_b])
        g = Group(k, a, k.KT[1024:1088, ts], k.KT_b, 1, rows_last=64)
        job_rope(k, a, wa, wa_b, 8, C["kpe"], wa, wa_b, C["kpes"], 64, rhs_h, [hT_b], rcs, rcs_b, g.slot(0), g.b)
        g.flush()
        for h4 in range(2):
            g = Group(k, a, k.GT[h4 * 512:(h4 + 1) * 512, ts], k.GT_b, 4)
            for j4 in range(4):
                job_fm(k, a, wa, wa_b, 8, C["gate"] + (h4 * 4 + j4) * 128, 128, rhs_h, [hT_b], g.slot(j4), g.b, act=AF.Silu)
            g.flush()
        for which, chunks, nfeat, gsb, gsb_b in ((0, (0, 1), 256, qg, qg_b), (1, (2,), 128, kvg, kvg_b)):
            pm, pm_b = mm_psum(k, a)
            for jj, ci in enumerate(chunks):
                P.op("pe", lambda e, ci=ci, jj=jj, pm=pm, chunks=chunks: e.matmul(pm[:, :], lhsT=k.onesf[:], rhs=sq[:, ci, :], start=(jj == 0), stop=(jj == len(chunks) - 1)),
                     R=[k.onesf_b, sq_b], W=[pm_b], inc=(jj == len(chunks) - 1))
            r_, r_b = rs[which], rs_b[which]
            P.op("act", lambda e, r_=r_, pm=pm, nfeat=nfeat: e.activation(out=r_, in_=pm[:, :], func=AF.Sqrt, scale=1.0 / nfeat, bias=EPS), R=[pm_b], W=[r_b])
            P.op("dve", lambda e, r_=r_: e.reciprocal(out=r_, in_=r_), R=[r_b], W=[r_b])
            for jj, ci in enumerate(chunks):
                P.op("dve", lambda e, ci=ci, jj=jj, r_=r_, gsb=gsb: e.scalar_tensor_tensor(out=latn[:, ci, :], in0=lat[:, ci, :], scalar=gsb[:, jj:jj + 1], in1=r_, op0=ALU.mult, op1=ALU.mult),
                     R=[lat_b, r_b, gsb_b], WP=[latn_b])
        nxt = hT_step_mid(k, a, li, st, xsrc, k.xr_b)
        rhs_q = lambda c: latn[:, c, :]
        rhs_kv = lambda c: latn[:, 2, :]
        for h4 in range(2):
            g = Group(k, a, k.QT[h4 * 512:(h4 + 1) * 512, ts], k.QT_b, 4)
            for j4 in range(4):
                job_fm(k, a, wqn, wqn_b, 2, (h4 * 4 + j4) * 128, 128, rhs_q, [latn_b], g.slot(j4), g.b, eng="dve")
            g.flush()
        g = Group(k, a, k.QT[1024:1536, ts], k.QT_b, 4)
        for hp in range(4):
            job_rope(k, a, wqr, wqr_b, 2, hp * 128, wqrs, wqrs_b, hp * 128, 128, rhs_q, [latn_b], rcs, rcs_b, g.slot(hp), g.b)
        g.flush()
        for h4 in range(2):
            g = Group(k, a, k.KT[h4 * 512:(h4 + 1) * 512, ts], k.KT_b, 4)
            for j4 in range(4):
                job_fm(k, a, wkn, wkn_b, 1, (h4 * 4 + j4) * 128, 128, rhs_kv, [latn_b], g.slot(j4), g.b, eng="dve")
            g.flush()
        for sub in range(4):
            for half in range(2):
                job_tm(k, a, wv, wv_b, 1, half * 512, 512, lambda c, sub=sub: latn[:, 2, sub * 128:(sub + 1) * 128], [latn_b],
                       a.vst[:, half * 4:(half + 1) * 4, sub, :], a.vst_b)
        P.dma("sp", k.VT[:, :, st * 4:(st + 1) * 4, :].rearrange("h p t d -> p h t d"), a.vst, R=[a.vst_b], WP=[k.VT_b])
    P.barrier()
    A.reset()
    kpe = A.alloc([S], BF16); kpe_b = Buf()
    P.dma("sp", kpe[0:64, :], k.KT[1024:1088, :], R=[k.KT_b], W=[kpe_b])
    hb = []
    for i in range(2):
        hb.append(dict(qn=A.alloc([S], BF16), qr=A.alloc([S], BF16), kn=A.alloc([S], BF16), v=A.alloc([NT, 128], BF16),
                       g=A.alloc([S], BF16), b=Buf()))
    pt_bufs = [(A.alloc([512], BF16), Buf()) for _ in range(5)]
    rd = [A.alloc([512], F32) for _ in range(2)]
    rd_b = [Buf() for _ in range(2)]
    og = [A.alloc([512], F32) for _ in range(2)]
    og_b = [Buf() for _ in range(2)]
    ost = [A.alloc([512], BF16) for _ in range(2)]
    ost_b = [Buf() for _ in range(2)]
    dacc = [A.alloc([512], F32) for _ in range(2)]
    dacc_b = [Buf() for _ in range(2)]
    scale = 192.0 ** -0.5
    ep_n = [0]
    for h in range(8):
        hbuf = hb[h % 2]
        b_ = hbuf["b"]
        P.dma("sp", hbuf["qn"], k.QT[h * 128:(h + 1) * 128, :], R=[k.QT_b], WP=[b_])
        P.dma("sp", hbuf["qr"][0:64, :], k.QT[1024 + h * 64:1024 + (h + 1) * 64, :], R=[k.QT_b], WP=[b_])
        P.dma("sp", hbuf["kn"], k.KT[h * 128:(h + 1) * 128, :], R=[k.KT_b], WP=[b_])
        P.dma("sp", hbuf["v"], k.VT[h], R=[k.VT_b], WP=[b_])
        P.dma("sp", hbuf["g"], k.GT[h * 128:(h + 1) * 128, :], R=[k.GT_b], WP=[b_])
        rows = []
        for qi in range(NST):
            qs = slice(qi * 512, (qi + 1) * 512)
            ksteps = []
            for ki in range(4 * qi + 4):
                ks = slice(ki * 128, (ki + 1) * 128)
                s_list = [(hbuf["kn"][:, ks], hbuf["qn"][:, qs], [b_]),
                          (kpe[0:64, ks], hbuf["qr"][0:64, qs], [b_, kpe_b])]
                kk = ki - 4 * qi
                if kk >= 0:
                    s_list.append((k.identb[:], k.masks[:, MASK_CAUSAL[kk], :], [k.identb_b, k.masks_b]))
                pv = [("O", hbuf["v"][:, ki, :], [b_], 128)]
                ksteps.append(dict(s=s_list, pv=pv, cols=band_cols(max(kk, 0), None)))

            ri_ = len(rows)

            def epilogue(O, O_b, Dn, Dn_b, h=h, qs=qs, hbuf=hbuf, b_=b_, ri_=ri_):
                i = ri_ % 2
                Dp, Dp_b = k.ps[6], k.ps_b[6]

                def rest():
                    P.op("pe", lambda e: e.matmul(Dp[:, :], lhsT=k.onesf[:], rhs=dacc[i], start=True, stop=True), R=[k.onesf_b, dacc_b[i]], W=[Dp_b])
                    P.op("act", lambda e: e.activation(out=rd[i], in_=Dp[:, :], func=AF.Ln), R=[Dp_b], W=[rd_b[i]])
                    P.op("act", lambda e: e.activation(out=rd[i], in_=rd[i], func=AF.Exp, scale=-1.0), R=[rd_b[i]], W=[rd_b[i]])
                    P.op("dve", lambda e: e.tensor_tensor(out=og[i], in0=O[:, :], in1=rd[i], op=ALU.mult), R=[O_b, rd_b[i]], W=[og_b[i]])
                    P.op("pool", lambda e: e.tensor_tensor(out=ost[i], in0=og[i], in1=hbuf["g"][:, qs], op=ALU.mult), R=[og_b[i], b_], W=[ost_b[i]])
                    P.dma("pool", k.OT[h * 128:(h + 1) * 128, qs], ost[i], R=[ost_b[i]], WP=[k.OT_b])
                return [None, None, rest]
            rows.append(dict(ksteps=ksteps, o_banks=3, dacc=(dacc[ri_ % 2], dacc_b[ri_ % 2]), epilogue=epilogue))
        attention_rows(k, rows, pt_bufs, scale, s_banks=(0, 1, 2, 7), look=3)
    phase_c(k, li, pre + "wout", last)


def mla_rope_q(k, a, wqr, wqr_b, wqrs, wqrs_b, hp, rhs_q, latn_b, tok0):
    P = k.P
    pm, pm_b = mm_psum(k, a)
    fm_group(k, pm, pm_b, wqr, wqr_b, 2, hp * 128, 128, rhs_q, [latn_b])
    pm2, pm2_b = mm_psum(k, a)
    fm_group(k, pm2, pm2_b, wqrs, wqrs_b, 2, hp * 128, 128, rhs_q, [latn_b])
    r0, r0_b, r1, r1_b = a.rt[0], a.rt_b[0], a.rt[1], a.rt_b[1]
    rcs, rcs_b = a.cur_rcs, a.cur_rcs_b
    P.op("dve", lambda e: e.tensor_tensor(out=r0, in0=pm[:, :], in1=rcs[:, 0, :], op=ALU.mult), R=[pm_b, rcs_b], W=[r0_b])
    P.op("dve", lambda e: e.tensor_tensor(out=r1, in0=pm2[:, :], in1=rcs[:, 1, :], op=ALU.mult), R=[pm2_b, rcs_b], W=[r1_b])
    st, st_b = fst_next(a)
    P.op("pool", lambda e: e.tensor_tensor(out=st, in0=r0, in1=r1, op=ALU.add), R=[r0_b, r1_b], W=[st_b])
    P.dma("pool", k.QT[1024 + hp * 128:1024 + (hp + 1) * 128, tok0:tok0 + 512], st, R=[st_b], WP=[k.QT_b])


def swa_layer(k, li, last):
    P, nc, A, din = k.P, k.nc, k.arena, k.din
    C = SWA_COLS
    xsrc = din["x_in"] if li == 0 else k.xr
    A.reset()
    a = phase_a_common(k, li, (2, 64))
    wa = A.alloc([8, C["n"]], BF16); wa_b = Buf()
    load_cast_weights(k, wa, din["swa_wa"], C["n"], 8, a.stg, a.stg_b, wa_b)
    nxt = hT_prime(k, a, li, xsrc, k.xr_b)
    for st in range(NST):
        tok0 = st * 512
        ts = slice(tok0, tok0 + 512)
        hT, hT_b, rcs, rcs_b = nxt
        rhs_h = lambda c, hT=hT: hT[:, c, :]
        hT_step_begin(k, a, st, xsrc, k.xr_b)
        g = Group(k, a, k.KT[0:128, ts], k.KT_b, 1)
        job_rope(k, a, wa, wa_b, 8, C["k"], wa, wa_b, C["ks"], 128, rhs_h, [hT_b], rcs, rcs_b, g.slot(0), g.b)
        g.flush()
        for m4 in range(2):
            g = Group(k, a, k.QT[m4 * 512:(m4 + 1) * 512, ts], k.QT_b, 4)
            for j4 in range(4):
                m = m4 * 4 + j4
                job_rope(k, a, wa, wa_b, 8, C["q"] + m * 128, wa, wa_b, C["qs"] + m * 128, 128, rhs_h, [hT_b], rcs, rcs_b, g.slot(j4), g.b)
            g.flush()
        for sub in range(4):
            job_tm(k, a, wa, wa_b, 8, C["v"], 128, lambda c, sub=sub, hT=hT: hT[:, c, sub * 128:(sub + 1) * 128], [hT_b],
                   a.vst[:, :, sub, :], a.vst_b)
        P.dma("sp", k.VS[0:2, :, st * 4:(st + 1) * 4, :].rearrange("h p t d -> p h t d"), a.vst, R=[a.vst_b], WP=[k.VS_b])
        nxt = hT_step_mid(k, a, li, st, xsrc, k.xr_b)
        for m4 in range(2):
            g = Group(k, a, k.GT[m4 * 512:(m4 + 1) * 512, ts], k.GT_b, 4)
            for j4 in range(4):
                job_fm(k, a, wa, wa_b, 8, C["gate"] + (m4 * 4 + j4) * 128, 128, rhs_h, [hT_b], g.slot(j4), g.b, act=AF.Silu)
            g.flush()
    P.barrier()
    A.reset()
    kT = [A.alloc([S], BF16) for _ in range(2)]
    kT_b = Buf()
    vv = [A.alloc([NT, 64], BF16) for _ in range(2)]
    vv_b = Buf()
    for kv in range(2):
        P.dma("sp", kT[kv][0:64, :], k.KT[kv * 64:(kv + 1) * 64, :], R=[k.KT_b], WP=[kT_b])
        P.dma("sp", vv[kv], k.VS[kv], R=[k.VS_b], WP=[vv_b])
    snk = A.alloc([16], F32); snk_b = Buf()
    P.dma("sp", snk, din["swa_sinks"].partition_broadcast(128), W=[snk_b])
    P.op("act", lambda e: e.activation(out=snk, in_=snk, func=AF.Exp), R=[snk_b], W=[snk_b])
    hb = [dict(q=A.alloc([S], BF16), g=A.alloc([S], BF16), b=Buf()) for _ in range(2)]
    pt_bufs = [(A.alloc([512], BF16), Buf()) for _ in range(5)]
    rd = [A.alloc([512], F32) for _ in range(2)]; rd_b = [Buf() for _ in range(2)]
    og = [A.alloc([512], F32) for _ in range(2)]; og_b = [Buf() for _ in range(2)]
    ost = [A.alloc([512], BF16) for _ in range(2)]; ost_b = [Buf() for _ in range(2)]
    ep_n = [0]
    for hh in range(16):
        kv = hh // 8
        hbuf = hb[hh % 2]
        b_ = hbuf["b"]
        P.dma("sp", hbuf["q"][0:64, :], k.QT[hh * 64:(hh + 1) * 64, :], R=[k.QT_b], WP=[b_])
        P.dma("sp", hbuf["g"][0:64, :], k.GT[hh * 64:(hh + 1) * 64, :], R=[k.GT_b], WP=[b_])
        rows = []
        for qi in range(NST):
            qs = slice(qi * 512, (qi + 1) * 512)
            ksteps = []
            for kk in range(-1, 4):
                ki = 4 * qi + kk
                if ki < 0:
                    continue
                ks = slice(ki * 128, (ki + 1) * 128)
                s_list = [(kT[kv][0:64, ks], hbuf["q"][0:64, qs], [b_, kT_b]),
                          (k.identb[:], k.masks[:, MASK_SWA[kk], :], [k.identb_b, k.masks_b])]
                pv = [("O", vv[kv][:, ki, :], [vv_b], 64), ("D", k.onesb[:, 0:64], [k.onesb_b], 64)]
                ksteps.append(dict(s=s_list, pv=pv, cols=band_cols(kk, 128)))

            def epilogue(O, O_b, Dn, Dn_b, hh=hh, qs=qs, hbuf=hbuf, b_=b_):
                i = ep_n[0] % 2
                ep_n[0] += 1
                P.op("dve", lambda e: e.tensor_scalar(out=rd[i][0:64, :], in0=Dn[0:64, :], scalar1=snk[0:64, hh:hh + 1], scalar2=None, op0=ALU.add),
                     R=[Dn_b, snk_b], W=[rd_b[i]])
                P.op("dve", lambda e: e.reciprocal(out=rd[i][0:64, :], in_=rd[i][0:64, :]), R=[rd_b[i]], W=[rd_b[i]])
                P.op("dve", lambda e: e.tensor_tensor(out=og[i][0:64, :], in0=O[0:64, :], in1=rd[i][0:64, :], op=ALU.mult), R=[O_b, rd_b[i]], W=[og_b[i]])
                P.op("pool", lambda e: e.tensor_tensor(out=ost[i][0:64, :], in0=og[i][0:64, :], in1=hbuf["g"][0:64, qs], op=ALU.mult), R=[og_b[i], b_], W=[ost_b[i]])
                P.dma("pool", k.OT[hh * 64:(hh + 1) * 64, qs], ost[i][0:64, :], R=[ost_b[i]], WP=[k.OT_b])
            rows.append(dict(ksteps=ksteps, epilogue=epilogue))
        attention_rows(k, rows, pt_bufs, 0.125, s_banks=(0, 1, 2, 7), look=3)
    phase_c(k, li, "swa_wout", last)


def nsa_layer(k, li, last):
    P, nc, A, din = k.P, k.nc, k.arena, k.din
    C = NSA_COLS
    xsrc = din["x_in"] if li == 0 else k.xr
    A.reset()
    a = phase_a_common(k, li, (8, 64))
    wa = A.alloc([8, C["n"]], BF16); wa_b = Buf()
    load_cast_weights(k, wa, din["nsa_wa"], C["n"], 8, a.stg, a.stg_b, wa_b)
    nxt = hT_prime(k, a, li, xsrc, k.xr_b)
    for st in range(NST):
        tok0 = st * 512
        ts = slice(tok0, tok0 + 512)
        hT, hT_b, rcs, rcs_b = nxt
        rhs_h = lambda c, hT=hT: hT[:, c, :]
        hT_step_begin(k, a, st, xsrc, k.xr_b)
        g = Group(k, a, k.KT[0:512, ts], k.KT_b, 4)
        for m in range(2):
            job_rope(k, a, wa, wa_b, 8, C["kslc"] + m * 128, wa, wa_b, C["kslcs"] + m * 128, 128, rhs_h, [hT_b], rcs, rcs_b, g.slot(m), g.b)
        for m in range(2):
            job_rope(k, a, wa, wa_b, 8, C["kwin"] + m * 128, wa, wa_b, C["kwins"] + m * 128, 128, rhs_h, [hT_b], rcs, rcs_b, g.slot(2 + m), g.b)
        g.flush()
        g = Group(k, a, k.KT[512:1024, ts], k.KT_b, 4)
        for m in range(2):
            job_fm(k, a, wa, wa_b, 8, C["kcmp"] + m * 128, 128, rhs_h, [hT_b], g.slot(m), g.b, eng="dve")
        for m in range(2):
            job_fm(k, a, wa, wa_b, 8, C["vcmp"] + m * 128, 128, rhs_h, [hT_b], g.slot(2 + m), g.b, eng="dve")
        g.flush()
        for m4 in range(2):
            g = Group(k, a, k.QT[m4 * 512:(m4 + 1) * 512, ts], k.QT_b, 4)
            for j4 in range(4):
                m = m4 * 4 + j4
                job_rope(k, a, wa, wa_b, 8, C["q"] + m * 128, wa, wa_b, C["qs"] + m * 128, 128, rhs_h, [hT_b], rcs, rcs_b, g.slot(j4), g.b)
            g.flush()
        g = Group(k, a, k.GT[1024:1072, ts], k.GT_b, 1, rows_last=48)
        job_fm(k, a, wa, wa_b, 8, C["gbr"], 48, rhs_h, [hT_b], g.slot(0), g.b, act=AF.Sigmoid)
        g.flush()
        for sub in range(4):
            lf = lambda c, sub=sub, hT=hT: hT[:, c, sub * 128:(sub + 1) * 128]
            job_tm(k, a, wa, wa_b, 8, C["vslc"], 256, lf, [hT_b], a.vst[:, 0:4, sub, :], a.vst_b)
            job_tm(k, a, wa, wa_b, 8, C["vwin"], 256, lf, [hT_b], a.vst[:, 4:8, sub, :], a.vst_b)
        P.dma("sp", k.VS[:, :, st * 4:(st + 1) * 4, :].rearrange("h p t d -> p h t d"), a.vst, R=[a.vst_b], WP=[k.VS_b])
        nxt = hT_step_mid(k, a, li, st, xsrc, k.xr_b)
        for m4 in range(2):
            g = Group(k, a, k.GT[m4 * 512:(m4 + 1) * 512, ts], k.GT_b, 4)
            for j4 in range(4):
                job_fm(k, a, wa, wa_b, 8, C["gate"] + (m4 * 4 + j4) * 128, 128, rhs_h, [hT_b], g.slot(j4), g.b, act=AF.Silu)
            g.flush()
    import os
    STOP = os.environ.get("NSA_STOP", "")
    if STOP == "A":
        return phase_c(k, li, "nsa_wout", last)
    P.barrier()
    A.reset()
    kc = [A.alloc([256], BF16) for _ in range(4)]; kc_b = Buf()
    vc = [A.alloc([2, 64], BF16) for _ in range(4)]; vc_b = Buf()
    keep = A.off
    stg = [A.alloc([8, 256], F32) for _ in range(2)]
    stg_b = [Buf() for _ in range(2)]
    k.stg_n = 0
    w1 = [A.alloc([16, 128], BF16) for _ in range(2)]; w1_b = [Buf(), Buf()]
    w2f = A.alloc([2, 64], F32); w2f_b = Buf()
    w2 = A.alloc([2, 64], BF16); w2_b = Buf()
    pef = A.alloc([16], F32); pef_b = Buf()
    peb = A.alloc([16], BF16); peb_b = Buf()
    b1 = A.alloc([2], F32); b1_b = Buf()
    x2 = [A.alloc([S], BF16) for _ in range(2)]; x2_b = [Buf(), Buf()]
    hid = [A.alloc([256], BF16) for _ in range(2)]; hid_b = [Buf(), Buf()]
    for wi, nm in enumerate(("nsa_wk1", "nsa_wv1")):
        for half in range(2):
            b = k.stg_n % 2
            k.stg_n += 1
            P.dma("sp", stg[b][:, :, 0:128], din[nm][half * 1024:(half + 1) * 1024, :].rearrange("(m p) j -> p m j", p=128), W=[stg_b[b]])
            P.op("dve", lambda e, b=b, wi=wi, half=half: e.tensor_copy(out=w1[wi][:, half * 8:(half + 1) * 8, :], in_=stg[b][:, :, 0:128]),
                 R=[stg_b[b]], WP=[w1_b[wi]])
    P.dma("sp", w2f[:, 0, :], din["nsa_wk2"], WP=[w2f_b])
    P.dma("sp", w2f[:, 1, :], din["nsa_wv2"], WP=[w2f_b])
    P.op("dve", lambda e: e.tensor_copy(out=w2, in_=w2f), R=[w2f_b], W=[w2_b])
    P.dma("sp", pef, din["nsa_pe"], W=[pef_b])
    P.op("dve", lambda e: e.tensor_copy(out=peb, in_=pef), R=[pef_b], W=[peb_b])
    for wi in range(2):
        P.op("pool", lambda e, wi=wi: e.memset(hid[wi][:, 255:256], 0.0), WP=[hid_b[wi]])
        pm, pm_b = k.ps[wi], k.ps_b[wi]
        for m in range(16):
            P.op("pe", lambda e, wi=wi, m=m, pm=pm: e.matmul(pm[:, 0:1], lhsT=w1[wi][:, m, :], rhs=peb[:, m:m + 1], start=(m == 0), stop=(m == 15)),
                 R=[w1_b[wi], peb_b], W=[pm_b], inc=(m == 15))
        P.op("act", lambda e, wi=wi, pm=pm: e.activation(out=b1[:, wi:wi + 1], in_=pm[:, 0:1], func=AF.Copy), R=[pm_b], WP=[b1_b])
    n = 0
    for kv in range(4):
        for wi in range(2):
            xb = n % 2
            n += 1
            row0 = (512 if wi == 0 else 768) + kv * 64
            P.dma("sp", x2[xb][0:64, :], k.KT[row0:row0 + 64, :], R=[k.KT_b], WP=[x2_b[xb]])
            P.dma("sp", x2[xb][64:128, 0:S - 1], k.KT[row0:row0 + 64, 1:S], R=[k.KT_b], WP=[x2_b[xb]])
            x2v = x2[xb].rearrange("p (i l) -> p i l", l=16)
            pm, pm_b = k.ps[2 + xb], k.ps_b[2 + xb]
            for m in range(16):
                l2 = 2 * m
                rhs = x2v[:, 0:255, l2] if l2 < 16 else x2v[:, 1:256, l2 - 16]
                P.op("pe", lambda e, wi=wi, m=m, pm=pm, rhs=rhs: e.matmul(pm[:, 0:255], lhsT=w1[wi][:, m, :], rhs=rhs, start=(m == 0), stop=(m == 15)),
                     R=[w1_b[wi], x2_b[xb]], W=[pm_b], inc=(m == 15))
            P.op("act", lambda e, wi=wi, pm=pm: e.activation(out=hid[wi][:, 0:255], in_=pm[:, 0:255], func=AF.Silu, bias=b1[:, wi:wi + 1]),
                 R=[pm_b, b1_b], WP=[hid_b[wi]])
            if wi == 0:
                pm2, pm2_b = k.ps[4], k.ps_b[4]
                P.op("pe", lambda e, pm2=pm2: e.matmul(pm2[0:64, 0:256], lhsT=w2[:, 0, :], rhs=hid[0][:, :], start=True, stop=True),
                     R=[w2_b, hid_b[0]], W=[pm2_b])
                P.op("dve", lambda e, kv=kv, pm2=pm2: e.tensor_copy(out=kc[kv][0:64, :], in_=pm2[0:64, 0:256]), R=[pm2_b], WP=[kc_b])
            else:
                pm2, pm2_b = k.ps[5], k.ps_b[5]
                for nt in range(2):
                    P.op("pe", lambda e, nt=nt, pm2=pm2: e.matmul(pm2[:, nt * 64:(nt + 1) * 64], lhsT=hid[1][:, nt * 128:(nt + 1) * 128], rhs=w2[:, 1, :], start=True, stop=True),
                         R=[w2_b, hid_b[1]], W=[pm2_b], inc=(nt == 1))
                P.op("dve", lambda e, kv=kv, pm2=pm2: e.tensor_copy(out=vc[kv], in_=pm2[:, 0:128].rearrange("p (a b) -> p a b", a=2)), R=[pm2_b], WP=[vc_b])
    if STOP == "B0":
        return phase_c(k, li, "nsa_wout", last)
    P.barrier()
    A.reset(keep)
    bf = BF16
    cmask = A.alloc([4608], bf); cmask_b = Buf()
    ovl = A.alloc([2, 64], bf); ovl_b = Buf()
    sel3 = A.alloc([48, 64], bf); sel3_b = Buf()
    cmul = A.alloc([24, 64], bf); cadd = A.alloc([24, 64], bf); ctab_b = Buf()
    gsig = A.alloc([S], bf); gsig_b = Buf()
    P.dma("sp", cmask, din["c_cmask"], W=[cmask_b])
    P.dma("sp", ovl, din["c_ovl"], W=[ovl_b])
    P.dma("sp", sel3[0:48], din["c_sel3"][0:48], W=[sel3_b])
    P.dma("sp", cmul, din["c_cmul"][:, 8:32, :], WP=[ctab_b])
    P.dma("sp", cadd, din["c_cadd"][:, 8:32, :], WP=[ctab_b])
    P.dma("sp", gsig[0:48, :], k.GT[1024:1072, :], R=[k.GT_b], W=[gsig_b])
    QS = [A.alloc([S], bf) for _ in range(4)]
    QS_b = [Buf() for _ in range(4)]
    KE = A.alloc([S], bf); KE_b = Buf()
    kwin = A.alloc([S], bf); kwin_b = Buf()
    vs = A.alloc([NT, 128], bf); vs_b = Buf()
    vw = A.alloc([NT, 128], bf); vw_b = Buf()
    P.op("pool", lambda e: e.memset(vs[:, :, 64:128], 1.0), WP=[vs_b])
    P.op("pool", lambda e: e.memset(vw[:, :, 64:128], 1.0), WP=[vw_b])
    dsb_b = [Buf(), Buf()]
    impT = A.alloc([S], F32); impT_b = Buf()
    G = [A.alloc([512], bf) for _ in range(2)]; G_b = [Buf(), Buf()]
    pt_bufs = [(A.alloc([512], bf), Buf()) for _ in range(3)]
    rd = [A.alloc([512], F32) for _ in range(2)]; rd_b = [Buf() for _ in range(2)]
    tmp = [A.alloc([512], F32) for _ in range(2)]; tmp_b = [Buf() for _ in range(2)]
    acc = [A.alloc([512], F32) for _ in range(2)]; acc_b = [Buf() for _ in range(2)]
    ocl = [A.alloc([512], F32) for _ in range(2)]; ocl_b = [Buf() for _ in range(2)]
    ocs, ocs_b = ocl, ocl_b
    ost = [A.alloc([512], bf) for _ in range(2)]; ost_b = [Buf() for _ in range(2)]
    NW = 2
    impm_l = [A.alloc([64], F32) for _ in range(NW)]; impm_bl = [Buf() for _ in range(NW)]
    imp2_l = [A.alloc([64], F32) for _ in range(NW)]; imp2_bl = [Buf() for _ in range(NW)]
    gq = [A.alloc([512], bf) for _ in range(6)]; gq_b = [Buf() for _ in range(6)]
    m1 = A.alloc([8], F32); m1_b = Buf()
    m2 = A.alloc([8], F32); m2_b = Buf()
    selb_l = [A.alloc([128], F32) for _ in range(NW)]; selb_bl = [Buf() for _ in range(NW)]
    cmp3_l = [A.alloc([4096], BF16) for _ in range(NW)]; cmp3_bl = [Buf() for _ in range(NW)]
    dlo = [cmp3_l[i_].bitcast(F32)[:, 0:512] for i_ in range(2)]; dlo_b = cmp3_bl
    for w_ in range(NW):
        P.op("pool", lambda e, w_=w_: e.memset(selb_l[w_], 0.0), W=[selb_bl[w_]])
    P.dma("sp", KE[64:128, :], din["c_ebig"][64:128, :], WP=[KE_b])
    for g in range(4):
        P.dma("sp", QS[g][64:128, 0:1024], din["c_selst"][64:128, :], WP=[QS_b[g]])
    cn = [0]
    for kv in range(4):
        for g in range(4):
            hh = kv * 4 + g
            P.dma("sp", QS[g][0:64, :], k.QT[hh * 64:(hh + 1) * 64, :], R=[k.QT_b], WP=[QS_b[g]])
        P.dma("sp", KE[0:64, :], k.KT[kv * 64:(kv + 1) * 64, :], R=[k.KT_b], WP=[KE_b])
        P.dma("sp", kwin[0:64, :], k.KT[256 + kv * 64:256 + (kv + 1) * 64, :], R=[k.KT_b], W=[kwin_b])
        P.dma("sp", vs[:, :, 0:64], k.VS[kv], R=[k.VS_b], WP=[vs_b])
        P.dma("sp", vw[:, :, 0:64], k.VS[4 + kv], R=[k.VS_b], WP=[vw_b])
        rows = []
        for g in range(4):
            hh = kv * 4 + g
            for qi in range(NST):
                qs = slice(qi * 512, (qi + 1) * 512)
                ksteps = []
                for nt in range(2):
                    if nt == 1 and qi < 4:
                        continue
                    s_list = [(kc[kv][0:64, nt * 128:(nt + 1) * 128], QS[g][0:64, qs], [kc_b, QS_b[g]])]
                    if not (nt == 0 and qi >= 5):
                        cm0 = qi * 512 if nt == 0 else 2560 + (qi - 4) * 512
                        s_list.append((k.identb[:], cmask[:, cm0:cm0 + 512], [k.identb_b, cmask_b]))
                    pv = [("O", vc[kv][:, nt, :], [vc_b], 64), ("D", k.onesb[:, 0:64], [k.onesb_b], 64), ("A", ovl[:, nt, :], [ovl_b], 64)]
                    ksteps.append(dict(s=s_list, pv=pv))

                def epilogue(O, O_b, Dn, Dn_b, hh=hh, g=g, qs=qs):
                    i = cn[0] % 2
                    cn[0] += 1
                    Aa, Aa_b = k.ps[7], k.ps_b[7]
                    P.op("act", lambda e: e.activation(out=rd[i][0:64, :], in_=Dn[0:64, :], func=AF.Ln, bias=1e-18), R=[Dn_b], W=[rd_b[i]])
                    P.op("act", lambda e: e.activation(out=rd[i][0:64, :], in_=rd[i][0:64, :], func=AF.Exp, scale=-1.0), R=[rd_b[i]], W=[rd_b[i]])
                    P.op("dve", lambda e: e.tensor_tensor(out=ocs[i][0:64, :], in0=O[0:64, :], in1=rd[i][0:64, :], op=ALU.mult), R=[O_b, rd_b[i]], W=[ocs_b[i]])
                    P.dma("pool", k.OC[hh * 64:(hh + 1) * 64, qs], ocs[i][0:64, :], R=[ocs_b[i]], WP=[k.OC_b])
                    if g == 0:
                        P.op("dve", lambda e: e.tensor_tensor(out=impT[0:64, qs], in0=Aa[0:64, :], in1=rd[i][0:64, :], op=ALU.mult), R=[Aa_b, rd_b[i]], WP=[impT_b])
                    else:
                        P.op("dve", lambda e: e.tensor_tensor(out=tmp[i][0:64, :], in0=Aa[0:64, :], in1=rd[i][0:64, :], op=ALU.mult), R=[Aa_b, rd_b[i]], W=[tmp_b[i]])
                        P.op("pool", lambda e: e.tensor_tensor(out=impT[0:64, qs], in0=impT[0:64, qs], in1=tmp[i][0:64, :], op=ALU.add), R=[tmp_b[i]], WP=[impT_b])
                rows.append(dict(ksteps=ksteps, epilogue=epilogue))
        attention_rows(k, rows, pt_bufs, 0.125)
        if STOP == "B1":
            return phase_c(k, li, "nsa_wout", last)
        for w_ in range(NW):
            P.op("pool", lambda e, w_=w_: e.memset(selb_l[w_][:, 64:128], -30000.0), WP=[selb_bl[w_]])

        def sel_stages(t, w_):
            impm, impm_b, imp2, imp2_b = impm_l[w_], impm_bl[w_], imp2_l[w_], imp2_bl[w_]
            selb, selb_b, cmp3, cmp3_b = selb_l[w_], selb_bl[w_], cmp3_l[w_], cmp3_bl[w_]
            pT, pT_b = k.ps[w_], k.ps_b[w_]
            p2, p2_b = k.ps[2 + w_], k.ps_b[2 + w_]
            tsl = slice(t * 128, (t + 1) * 128)
            ns = 2 * t + 2
            in0 = impm[:, 0:ns].unsqueeze(1).to_broadcast([128, ns, ns])
            in1 = impm[:, 0:ns].unsqueeze(2).to_broadcast([128, ns, ns])
            c3 = cmp3[:, 0:ns * ns].rearrange("p (a b) -> p a b", a=ns)
            st_ = []
            st_.append(lambda: P.op("pe", lambda e: e.transpose(out=pT[:, 0:64], in_=impT[0:64, tsl], identity=k.ident[0:64, 0:64]),
                                    R=[impT_b, k.ident_b], W=[pT_b]))
            st_.append(lambda: P.op("dve", lambda e: e.tensor_tensor(out=impm, in0=pT[:, 0:64], in1=cmul[:, t - 8, :], op=ALU.mult), R=[pT_b, ctab_b], W=[impm_b]))
            st_.append(lambda: P.op("dve", lambda e: e.tensor_tensor(out=impm, in0=impm, in1=cadd[:, t - 8, :], op=ALU.add), R=[ctab_b, impm_b], W=[impm_b]))
            st_.append(lambda: P.op("dve", lambda e: e.tensor_tensor(out=c3, in0=in0, in1=in1, op=ALU.is_gt), R=[impm_b], W=[cmp3_b]))
            st_.append(lambda: P.op("dve", lambda e: e.tensor_reduce(out=imp2[:, 0:ns], in_=c3, axis=mybir.AxisListType.X, op=ALU.add), R=[cmp3_b], W=[imp2_b]))
            st_.append(lambda: P.op("dve", lambda e: e.tensor_scalar(out=selb[:, 64:64 + ns], in0=imp2[:, 0:ns], scalar1=15.5, scalar2=30000.0, op0=ALU.is_lt, op1=ALU.mult),
                                    R=[imp2_b], WP=[selb_b]))
            st_.append(lambda: P.op("dve", lambda e: e.tensor_scalar(out=selb[:, 64:64 + ns], in0=selb[:, 64:64 + ns], scalar1=-30000.0, scalar2=None, op0=ALU.add),
                                    R=[selb_b], WP=[selb_b]))
            st_.append(lambda: P.op("pe", lambda e: e.transpose(out=p2[:, 0:128], in_=selb, identity=k.ident[:]), R=[selb_b, k.ident_b], W=[p2_b]))
            for g in range(4):
                st_.append(lambda g=g: P.op("act", lambda e: e.activation(out=QS[g][64:128, tsl], in_=p2[64:128, 0:128], func=AF.Copy), R=[p2_b], WP=[QS_b[g]]))
            return st_

        for t0 in range(8, NT, NW):
            chains = [sel_stages(t0 + w_, w_) for w_ in range(NW)]
            for si_ in range(len(chains[0])):
                for ch in chains:
                    ch[si_]()
        if STOP == "SEL":
            return phase_c(k, li, "nsa_wout", last)
        for g in range(4):
            hh = kv * 4 + g
            rows = []
            for qi in range(NST):
                qs = slice(qi * 512, (qi + 1) * 512)
                ksteps = []
                for ki in range(4 * qi + 4):
                    ks = slice(ki * 128, (ki + 1) * 128)
                    s_list = [(KE[:, ks], QS[g][:, qs], [KE_b, QS_b[g]])]
                    kk = ki - 4 * qi
                    if kk >= 0:
                        s_list.append((k.identb[:], k.masks[:, MASK_CAUSAL[kk], :], [k.identb_b, k.masks_b]))
                    pv = [("O", vs[:, ki, :], [vs_b], 128)]
                    ksteps.append(dict(s=s_list, pv=pv, cols=band_cols(max(kk, 0), None)))

                def gate_pre(r, gi, qs=qs):
                    def f():
                        gb, gb_b = k.ps[7], k.ps_b[7]
                        P.op("pe", lambda e: e.matmul(gb[0:64, :], lhsT=sel3[0:48, r, :], rhs=gsig[0:48, qs], start=True, stop=True), R=[sel3_b, gsig_b], W=[gb_b])
                        P.op("act", lambda e: e.activation(out=gq[gi][0:64, :], in_=gb[0:64, :], func=AF.Copy), R=[gb_b], W=[gq_b[gi]])
                    return f

                def den_shift(O, O_b, i):
                    Dsb = rd[i][64:128, :]
                    P.op("act", lambda e: e.activation(out=Dsb, in_=O[64:128, :], func=AF.Copy), R=[O_b], W=[dsb_b[i]])
                    P.dma("sp", dlo[i][0:64, :], Dsb, R=[dsb_b[i]], W=[dlo_b[i]])

                    def f():
                        P.op("dve", lambda e: e.reciprocal(out=rd[i][0:64, :], in_=dlo[i][0:64, :]), R=[dlo_b[i]], W=[rd_b[i]])
                    return f

                def ep_sel(O, O_b, Dn, Dn_b, hh=hh, qs=qs, qi=qi):
                    i = qi % 2
                    f0 = den_shift(O, O_b, i)

                    def rest():
                        f0()
                        P.op("dve", lambda e: e.tensor_tensor(out=tmp[i][0:64, :], in0=O[0:64, :], in1=rd[i][0:64, :], op=ALU.mult), R=[O_b, rd_b[i]], W=[tmp_b[i]])
                        P.op("dve", lambda e: e.tensor_tensor(out=acc[i][0:64, :], in0=tmp[i][0:64, :], in1=gq[i][0:64, :], op=ALU.mult), R=[gq_b[i], tmp_b[i]], W=[acc_b[i]])
                    return [rest]
                rows.append(dict(ksteps=ksteps, o_banks=4, epilogue=ep_sel, pre=[gate_pre(hh * 3 + 1, qi % 2)]))
                ksteps = []
                for kk in range(-4, 4):
                    ki = 4 * qi + kk
                    if ki < 0:
                        continue
                    ks = slice(ki * 128, (ki + 1) * 128)
                    s_list = [(kwin[0:64, ks], QS[g][0:64, qs], [kwin_b, QS_b[g]]),
                              (k.identb[:], k.masks[:, MASK_WIN[kk], :], [k.identb_b, k.masks_b])]
                    pv = [("O", vw[:, ki, :], [vw_b], 128)]
                    ksteps.append(dict(s=s_list, pv=pv, cols=band_cols(kk, 512)))

                def ep_win(O, O_b, Dn, Dn_b, hh=hh, qs=qs, qi=qi):
                    i = qi % 2
                    Gh, Gh_b = G[i], G_b[i]
                    P.dma("sp", ocl[i][0:64, :], k.OC[hh * 64:(hh + 1) * 64, qs], R=[k.OC_b], W=[ocl_b[i]])
                    P.dma("sp", Gh[0:64, :], k.GT[hh * 64:(hh + 1) * 64, qs], R=[k.GT_b], W=[Gh_b])
                    f0 = den_shift(O, O_b, i)

                    def rest():
                        f0()
                        P.op("dve", lambda e: e.tensor_tensor(out=tmp[i][0:64, :], in0=O[0:64, :], in1=rd[i][0:64, :], op=ALU.mult), R=[O_b, rd_b[i]], W=[tmp_b[i]])
                        P.op("dve", lambda e: e.tensor_tensor(out=tmp[i][0:64, :], in0=tmp[i][0:64, :], in1=gq[2 + i][0:64, :], op=ALU.mult), R=[gq_b[2 + i], tmp_b[i]], W=[tmp_b[i]])
                        P.op("pool", lambda e: e.tensor_tensor(out=acc[i][0:64, :], in0=acc[i][0:64, :], in1=tmp[i][0:64, :], op=ALU.add), R=[tmp_b[i], acc_b[i]], W=[acc_b[i]])
                        P.op("dve", lambda e: e.tensor_tensor(out=tmp[i][0:64, :], in0=ocl[i][0:64, :], in1=gq[4 + i][0:64, :], op=ALU.mult), R=[gq_b[4 + i], ocl_b[i]], W=[tmp_b[i]])
                        P.op("pool", lambda e: e.tensor_tensor(out=acc[i][0:64, :], in0=acc[i][0:64, :], in1=tmp[i][0:64, :], op=ALU.add), R=[tmp_b[i], acc_b[i]], W=[acc_b[i]])
                        P.op("pool", lambda e: e.tensor_tensor(out=ost[i][0:64, :], in0=acc[i][0:64, :], in1=Gh[0:64, :], op=ALU.mult), R=[acc_b[i], Gh_b], W=[ost_b[i]])
                        P.dma("pool", k.OT[hh * 64:(hh + 1) * 64, qs], ost[i][0:64, :], R=[ost_b[i]], WP=[k.OT_b])
                    return [rest]
                rows.append(dict(ksteps=ksteps, o_banks=4, epilogue=ep_win, pre=[gate_pre(hh * 3 + 2, 2 + qi % 2), gate_pre(hh * 3 + 0, 4 + qi % 2)]))
            attention_rows(k, rows, pt_bufs, 0.125)
    phase_c(k, li, "nsa_wout", last)


def _np_dt(a):
    if a.dtype == np.float32:
        return F32
    if a.dtype == np.int32:
        return I32
    return BF16


def make_in_maps(inputs, cores):
    w = _prep_weights({kk: np.asarray(v) for kk, v in inputs.items()})
    cst = _consts()
    shared = {}
    shared.update(w)
    shared.update(cst)
    x = np.asarray(inputs["x"], np.float32)
    c = np.asarray(inputs["c"], np.float32)
    pos = np.asarray(inputs["positions"], np.int32)
    maps = []
    for b in cores:
        m = dict(shared)
        m["x_in"] = np.ascontiguousarray(x[b])
        m["c_fm"] = _fm(c[b], 8)
        m["pos"] = np.ascontiguousarray(pos[b].reshape(1, S))
        maps.append(m)
    return maps


def kernel(**inputs):
    maps = make_in_maps(inputs, list(range(8)))
    shapes = {kk: (v.shape, _np_dt(v)) for kk, v in maps[0].items()}
    nc, P = build(DEPTH, shapes)
    res = run_bass_kernel_spmd(nc, maps, core_ids=list(range(8)))
    out = np.stack([np.asarray(r["out"], np.float32) for r in res.results], axis=0)
    return out
```

```python
import numpy as np
import ml_dtypes
import concourse.bass as bass
import concourse.mybir as mybir
from concourse.bass_utils import run_bass_kernel_spmd

F32 = mybir.dt.float32
BF16 = mybir.dt.bfloat16
I32 = mybir.dt.int32
U8 = mybir.dt.uint8
AF = mybir.ActivationFunctionType
ALU = mybir.AluOpType

S = 4096
D = 1024
NT = S // 128
NST = S // 512
DEPTH = 4
NEG = -30000.0
EPS = 1e-6


class Buf:
    __slots__ = ("w", "r")

    def __init__(self):
        self.w = {}
        self.r = {}


class Prog:
    CE = ("pe", "act", "dve", "pool")
    ALLQ = ("pe", "act", "dve", "pool", "sp")

    def __init__(self, nc, n_dma_sems=32):
        self.nc = nc
        self.ops = {e: [] for e in self.ALLQ}
        self.count = {e: 0 for e in self.CE}
        self.sems = {}
        self.known = {e: {} for e in self.ALLQ}
        self.n_dma = n_dma_sems
        self.dma_cnt = [0] * n_dma_sems
        self.dma_rr = 0
        self.dma_rr_pool = 0
        self._stack = []
        for e in self.CE:
            self.sems[e] = self._sem("s_" + e)
        for i in range(n_dma_sems):
            self.sems[("dma", i)] = self._sem("s_dma%d" % i)
        self.n_ops = 0

    def _sem(self, name):
        cm = self.nc.semaphore(name)
        h = cm.__enter__()
        self._stack.append(cm)
        return h

    def _gather(self, eng, R, W, WP):
        need = {}
        for b in R:
            for k, v in b.w.items():
                if need.get(k, 0) < v:
                    need[k] = v
        for b in list(W) + list(WP):
            for k, v in b.w.items():
                if need.get(k, 0) < v:
                    need[k] = v
            for k, v in b.r.items():
                if need.get(k, 0) < v:
                    need[k] = v
        out = []
        kn = self.known[eng]
        for k, v in need.items():
            if eng == "pe" and k == "pe":
                continue
            if kn.get(k, 0) >= v:
                continue
            kn[k] = v
            out.append((k, v))
        return out

    def _record(self, tok, R, W, WP):
        k, v = tok
        for b in W:
            b.w = {k: v}
            b.r = {}
        for b in WP:
            if b.w.get(k, 0) < v:
                b.w[k] = v
        for b in R:
            if b.r.get(k, 0) < v:
                b.r[k] = v

    def op(self, eng, fn, R=(), W=(), WP=(), inc=True):
        waits = self._gather(eng, R, W, WP)
        if inc:
            self.count[eng] += 1
            ms = self.count[eng]
        else:
            ms = self.count[eng] + 1
        self.ops[eng].append((fn, waits, (eng, 1) if inc else None))
        self._record((eng, ms), R, W, WP)
        self.n_ops += 1

    def dma(self, q, out, in_, R=(), W=(), WP=()):
        half = self.n_dma // 2
        if q == "pool":
            s = half + self.dma_rr_pool
            self.dma_rr_pool = (self.dma_rr_pool + 1) % half
        else:
            s = self.dma_rr
            self.dma_rr = (self.dma_rr + 1) % half
        key = ("dma", s)
        waits = self._gather(q, R, W, WP)
        prev = 16 * self.dma_cnt[s]
        if prev > 0 and self.known[q].get(key, 0) < prev:
            self.known[q][key] = prev
            waits.append((key, prev))
        self.dma_cnt[s] += 1
        val = 16 * self.dma_cnt[s]

        def fn(e, out=out, in_=in_):
            return e.dma_start(out=out, in_=in_)
        self.ops[q].append((fn, waits, (key, 16)))
        self._record((key, val), R, W, WP)
        self.n_ops += 1

    def barrier(self):
        for e in self.ALLQ:
            waits = []
            kn = self.known[e]
            for c in self.CE:
                if c == e:
                    continue
                v = self.count[c]
                if v > 0 and kn.get(c, 0) < v:
                    kn[c] = v
                    waits.append((c, v))
            for i in range(self.n_dma):
                v = 16 * self.dma_cnt[i]
                k = ("dma", i)
                if v > 0 and kn.get(k, 0) < v:
                    kn[k] = v
                    waits.append((k, v))
            if waits:
                self.ops[e].append((None, waits, None))

    def emit(self):
        nc = self.nc
        sems = self.sems
        ops = self.ops

        def run(e, lst):
            for fn, waits, inc in lst:
                for k, v in waits:
                    e.wait_ge(sems[k], v)
                if fn is None:
                    continue
                ins = fn(e)
                if inc is not None:
                    ins.then_inc(sems[inc[0]], inc[1])

        with nc.Block() as blk:
            @blk.sync
            def _(e):
                run(e, ops["sp"])

            @blk.tensor
            def _(e):
                run(e, ops["pe"])

            @blk.scalar
            def _(e):
                run(e, ops["act"])

            @blk.vector
            def _(e):
                run(e, ops["dve"])

            @blk.gpsimd
            def _(e):
                run(e, ops["pool"])


class Arena:
    def __init__(self, nc, nbytes):
        self.t = nc.alloc_sbuf_tensor("arena", [128, nbytes], U8)
        self.n = nbytes
        self.off = 0

    def reset(self, off=0):
        self.off = off

    def alloc(self, free_shape, dtype):
        n = int(np.prod(free_shape))
        nb = n * mybir.dt.size(dtype)
        nb = (nb + 63) // 64 * 64
        assert self.off + nb <= self.n, ("arena overflow", self.off, nb, self.n)
        ap = self.t[:, self.off:self.off + nb].bitcast(dtype)[:, 0:n]
        self.off += nb
        if len(free_shape) == 2:
            ap = ap.rearrange("p (a b) -> p a b", a=free_shape[0])
        elif len(free_shape) == 3:
            ap = ap.rearrange("p (a b c) -> p a b c", a=free_shape[0], b=free_shape[1])
        return ap


def _mask(kk, W):
    kp = np.arange(128)[:, None]
    qf = np.arange(512)[None, :]
    dlt = qf - 128 * kk - kp
    ok = dlt >= 0
    if W is not None:
        ok &= dlt < W
    return np.where(ok, 0.0, NEG).astype(np.float32)


MASK_CAUSAL = {kk: kk for kk in range(4)}
MASK_SWA = {kk: 4 + (kk + 1) for kk in range(-1, 4)}
MASK_WIN = {kk: 9 + (kk + 4) for kk in range(-4, 0)}
for _kk in range(4):
    MASK_WIN[_kk] = MASK_CAUSAL[_kk]
NMASK = 13


def _consts():
    bf = ml_dtypes.bfloat16
    masks = np.zeros((128, NMASK, 512), np.float32)
    for kk in range(4):
        masks[:, MASK_CAUSAL[kk]] = _mask(kk, None)
    for kk in range(-1, 4):
        masks[:, MASK_SWA[kk]] = _mask(kk, 128)
    for kk in range(-4, 0):
        masks[:, MASK_WIN[kk]] = _mask(kk, 512)
    n = np.arange(256)[:, None]
    q = np.arange(S)[None, :]
    cm = np.where(16 * n + 31 <= q, 0.0, NEG).astype(np.float32)
    cmask = np.concatenate([cm[0:128, 0:2560], cm[128:256, 2048:4096]], axis=1)
    ov = np.zeros((256, 64), np.float32)
    for nn in range(255):
        a0, a1 = 16 * nn, 16 * nn + 32
        for s in range(64):
            o = min(a1, 64 * s + 64) - max(a0, 64 * s)
            if o > 0:
                ov[nn, s] = o / 32.0
    ovl = np.stack([ov[0:128], ov[128:256]], axis=1)
    ebig = np.zeros((128, S), np.float32)
    ebig[64 + (np.arange(S) // 64), np.arange(S)] = 1.0
    qq = np.arange(S)
    qb = qq // 64
    s = np.arange(64)[None, :]
    causal = s <= qb[:, None]
    forced = (s == 0) | (s == qb[:, None]) | (s == qb[:, None] - 1)
    cmul = causal.astype(np.float32)
    cadd = np.where(causal, np.where(forced, 1e4, 0.0), -1.0).astype(np.float32)
    cmul = cmul.reshape(32, 128, 64).transpose(1, 0, 2)
    cadd = cadd.reshape(32, 128, 64).transpose(1, 0, 2)
    sel3 = np.zeros((128, 48, 64), np.float32)
    for r in range(48):
        sel3[r, r, :] = 1.0
    half = 32
    inv = (10000.0 ** (-np.arange(half, dtype=np.float32) / half)).astype(np.float32)
    invf = np.tile(inv, 4)[:, None].astype(np.float32)
    sgn = np.where((np.arange(128) % 64) < 32, -1.0, 1.0).astype(np.float32)[:, None]
    return {
        "c_ident": np.eye(128, dtype=np.float32),
        "c_masks": masks.astype(bf),
        "c_cmask": cmask.astype(bf),
        "c_ovl": ovl.astype(bf),
        "c_ebig": ebig.astype(bf),
        "c_cmul": np.ascontiguousarray(cmul).astype(bf),
        "c_cadd": np.ascontiguousarray(cadd).astype(bf),
        "c_sel3": sel3.astype(bf),
        "c_selst": np.where((np.arange(128)[:, None] - 64) <= (np.arange(1024)[None, :] // 64), 0.0, NEG).astype(np.float32).astype(bf),
        "c_invf": invf,
        "c_sgn": sgn,
    }


def _swap64(w):
    k, n = w.shape
    return np.ascontiguousarray(w.reshape(k, n // 64, 2, 32)[:, :, ::-1, :].reshape(k, n))


def _fm(v, nchunk):
    return np.ascontiguousarray(np.asarray(v, np.float32).reshape(nchunk, 128).T)


MLA_COLS = dict(qa=0, kva=256, kpe=384, kpes=448, gate=512, n=1536)
SWA_COLS = dict(q=0, qs=1024, k=2048, ks=2176, v=2304, gate=2432, n=3456)
NSA_COLS = dict(q=0, qs=1024, kslc=2048, kslcs=2304, kwin=2560, kwins=2816, kcmp=3072, vcmp=3328,
                vslc=3584, vwin=3840, gbr=4096, gate=4144, n=5168)


def _prep_weights(inp):
    out = {}
    for j in range(2):
        w = inp["mla_w_in"][j]
        kpe = w[:, 384:448]
        out["mla%d_wa" % j] = np.ascontiguousarray(np.concatenate(
            [w[:, 0:256], w[:, 256:384], kpe, _swap64(kpe), w[:, 448:1472]], axis=1))
        qb = inp["mla_w_q_b"][j].reshape(256, 8, 192)
        out["mla%d_wqn" % j] = np.ascontiguousarray(qb[:, :, 0:128].reshape(256, 1024))
        qr = np.ascontiguousarray(qb[:, :, 128:192].reshape(256, 512))
        out["mla%d_wqr" % j] = qr
        out["mla%d_wqrs" % j] = _swap64(qr)
        kvb = inp["mla_w_kv_b"][j].reshape(128, 8, 256)
        out["mla%d_wkn" % j] = np.ascontiguousarray(kvb[:, :, 0:128].reshape(128, 1024))
        out["mla%d_wv" % j] = np.ascontiguousarray(kvb[:, :, 128:256].reshape(128, 1024))
        out["mla%d_wout" % j] = np.ascontiguousarray(inp["mla_w_out"][j])
        out["mla%d_qg" % j] = _fm(inp["mla_q_norm_g"][j], 2)
        out["mla%d_kvg" % j] = _fm(inp["mla_kv_norm_g"][j], 1)
    w = inp["swa_w_in"][0]
    q, k, v, g = w[:, 0:1024], w[:, 1024:1152], w[:, 1152:1280], w[:, 1280:2304]
    out["swa_wa"] = np.ascontiguousarray(np.concatenate([q, _swap64(q), k, _swap64(k), v, g], axis=1))
    out["swa_wout"] = np.ascontiguousarray(inp["swa_w_out"][0])
    out["swa_sinks"] = np.ascontiguousarray(inp["swa_sinks"][0].reshape(1, 16).astype(np.float32))
    w = inp["nsa_w_in"][0]
    q = w[:, 0:1024]
    kcmp, vcmp, kslc, vslc, kwin, vwin = [w[:, 1024 + 256 * i:1280 + 256 * i] for i in range(6)]
    gbr = w[:, 2560:2608]
    g = w[:, 2608:3632]
    out["nsa_wa"] = np.ascontiguousarray(np.concatenate(
        [q, _swap64(q), kslc, _swap64(kslc), kwin, _swap64(kwin), kcmp, vcmp, vslc, vwin, gbr, g], axis=1))
    out["nsa_wout"] = np.ascontiguousarray(inp["nsa_w_out"][0])
    out["nsa_pe"] = np.ascontiguousarray(inp["nsa_cmp_pos"][0].reshape(16, 128).T.astype(np.float32))
    out["nsa_wk1"] = np.ascontiguousarray(inp["nsa_w_cmp_k1"][0])
    out["nsa_wk2"] = np.ascontiguousarray(inp["nsa_w_cmp_k2"][0])
    out["nsa_wv1"] = np.ascontiguousarray(inp["nsa_w_cmp_v1"][0])
    out["nsa_wv2"] = np.ascontiguousarray(inp["nsa_w_cmp_v2"][0])
    out["ada_w"] = np.ascontiguousarray(inp["ada_w"])
    out["ada_b"] = np.ascontiguousarray(inp["ada_b"].reshape(4, 24, 128).transpose(2, 0, 1))
    out["ada_b_row"] = np.ascontiguousarray(inp["ada_b"].astype(np.float32))
    out["norm_g"] = np.ascontiguousarray(inp["norm_g"].reshape(4, 8, 128).transpose(2, 0, 1))
    out["final_g"] = np.ascontiguousarray(inp["final_norm_g"].reshape(1, 1024))
    return out


class K:
    pass


def build(n_layers=DEPTH, shapes=None):
    nc = bass.Bass("TRN2", target_bir_lowering=False)
    P = Prog(nc)
    k = K()
    k.nc, k.P = nc, P
    dram_in = {}

    def din(name, shape, dt):
        dram_in[name] = nc.dram_tensor(name, list(shape), dt, kind="ExternalInput").ap()
        return dram_in[name]

    for name, (shape, dt) in shapes.items():
        din(name, shape, dt)
    k.din = dram_in
    out = nc.dram_tensor("out", [S, D], F32, kind="ExternalOutput").ap()
    k.out = out
    k.xr = nc.dram_tensor("xr", [S, D], F32).ap()
    k.QT = nc.dram_tensor("QT", [1536, S], BF16).ap()
    k.KT = nc.dram_tensor("KT", [1280, S], BF16).ap()
    k.VT = nc.dram_tensor("VT", [8, 128, NT, 128], BF16).ap()
    k.VS = nc.dram_tensor("VS", [8, 128, NT, 64], BF16).ap()
    k.GT = nc.dram_tensor("GT", [1024 + 128, S], BF16).ap()
    k.OT = nc.dram_tensor("OT", [1024, S], BF16).ap()
    k.OC = nc.dram_tensor("OC", [1024, S], F32).ap()
    k.modrow = nc.dram_tensor("modrow", [4, 3072], F32).ap()
    k.xr_b = [Buf() for _ in range(NT)]
    k.QT_b, k.KT_b, k.VT_b, k.GT_b, k.OT_b, k.OC_b, k.modrow_b, k.out_b, k.VS_b = [Buf() for _ in range(9)]

    sb = nc.alloc_sbuf_tensor
    k.ident = sb("ident", [128, 128], F32); k.ident_b = Buf()
    k.identb = sb("identb", [128, 128], BF16); k.identb_b = Buf()
    k.onesb = sb("onesb", [128, 128], BF16); k.onesb_b = Buf()
    k.onesf = sb("onesf", [128, 128], F32); k.onesf_b = Buf()
    k.masks = sb("masks", [128, NMASK, 512], BF16); k.masks_b = Buf()
    k.ropeD = nc.dram_tensor("ropeD", [2, 128, S], F32).ap()
    k.rope_b = Buf()
    k.mod = sb("mod", [128, 4, 24], F32); k.mod_b = Buf()
    k.gmod = sb("gmod", [128, 4, 8], F32); k.gmod_b = Buf()
    k.ps = [nc.alloc_psum_tensor("ps%d" % i, [128, 512], F32) for i in range(8)]
    k.ps_b = [Buf() for _ in range(8)]
    k.arena = Arena(nc, 160 * 1024)

    prologue(k)
    import os
    kinds = os.environ.get("K_KINDS", "mla,swa,nsa,mla").split(",")
    for i in range(n_layers):
        kind = kinds[i]
        j = i // 3
        last = (i == n_layers - 1)
        if kind == "mla":
            mla_layer(k, i, j, last)
        elif kind == "swa":
            swa_layer(k, i, last)
        else:
            nsa_layer(k, i, last)
    P.barrier()
    P.emit()
    return nc, P


def prologue(k):
    P, nc, A = k.P, k.nc, k.arena
    din = k.din
    A.reset()
    P.dma("sp", k.ident[:], din["c_ident"], W=[k.ident_b])
    P.dma("sp", k.masks[:], din["c_masks"], W=[k.masks_b])
    P.op("pool", lambda e: e.tensor_copy(out=k.identb[:], in_=k.ident[:]), R=[k.ident_b], W=[k.identb_b])
    P.op("pool", lambda e: e.memset(k.onesb[:], 1.0), W=[k.onesb_b])
    P.op("pool", lambda e: e.memset(k.onesf[:], 1.0), W=[k.onesf_b])
    posi = A.alloc([S], I32); posi_b = Buf()
    ang = A.alloc([S], F32); ang_b = Buf()
    kf = A.alloc([S], F32); kf_b = Buf()
    ki = A.alloc([S], I32); ki_b = Buf()
    rr = A.alloc([S], F32); rr_b = Buf()
    ivf = A.alloc([1], F32); ivf_b = Buf()
    sgn = A.alloc([1], F32); sgn_b = Buf()
    P.dma("sp", posi, din["pos"].partition_broadcast(128), W=[posi_b])
    P.dma("sp", ivf, din["c_invf"], W=[ivf_b])
    P.dma("sp", sgn, din["c_sgn"], W=[sgn_b])
    P.op("dve", lambda e: e.tensor_copy(out=kf, in_=posi), R=[posi_b], W=[kf_b])
    P.op("dve", lambda e: e.tensor_scalar(out=ang, in0=kf, scalar1=ivf[:, 0:1], scalar2=None, op0=ALU.mult),
         R=[kf_b, ivf_b], W=[ang_b])
    TWO_PI = 2 * np.pi
    c1 = float(np.float32(6.28125))
    c2 = float(TWO_PI - 6.28125)
    for dst, shift, use_sgn in ((0, np.pi / 2, False), (1, 0.0, True)):
        P.op("dve", lambda e, shift=shift: e.tensor_scalar(out=kf, in0=ang, scalar1=float(shift), scalar2=float(1.0 / TWO_PI),
                                                          op0=ALU.add, op1=ALU.mult), R=[ang_b], W=[kf_b])
        P.op("dve", lambda e: e.tensor_copy(out=ki, in_=kf), R=[kf_b], W=[ki_b])
        P.op("dve", lambda e: e.tensor_copy(out=kf, in_=ki), R=[ki_b], W=[kf_b])
        P.op("dve", lambda e: e.scalar_tensor_tensor(out=rr, in0=kf, scalar=-c1, in1=ang, op0=ALU.mult, op1=ALU.add),
             R=[kf_b, ang_b], W=[rr_b])
        P.op("dve", lambda e: e.scalar_tensor_tensor(out=rr, in0=kf, scalar=-c2, in1=rr, op0=ALU.mult, op1=ALU.add),
             R=[kf_b, rr_b], W=[rr_b])
        P.op("dve", lambda e, shift=shift: e.tensor_scalar(out=kf, in0=rr, scalar1=float(shift), scalar2=float(np.pi),
                                                          op0=ALU.add, op1=ALU.is_gt), R=[rr_b], W=[kf_b])
        P.op("dve", lambda e, shift=shift: e.tensor_scalar(out=rr, in0=rr, scalar1=float(shift), scalar2=None, op0=ALU.add),
             R=[rr_b], W=[rr_b])
        P.op("dve", lambda e: e.scalar_tensor_tensor(out=rr, in0=kf, scalar=-TWO_PI, in1=rr, op0=ALU.mult, op1=ALU.add),
             R=[kf_b, rr_b], W=[rr_b])
        P.op("dve", lambda e: e.tensor_scalar(out=kf, in0=rr, scalar1=float(-np.pi), scalar2=None, op0=ALU.is_lt),
             R=[rr_b], W=[kf_b])
        P.op("dve", lambda e: e.scalar_tensor_tensor(out=rr, in0=kf, scalar=TWO_PI, in1=rr, op0=ALU.mult, op1=ALU.add),
             R=[kf_b, rr_b], W=[rr_b])
        P.op("dve", lambda e: e.tensor_scalar(out=rr, in0=rr, scalar1=3.141592, scalar2=-3.141592, op0=ALU.min, op1=ALU.max),
             R=[rr_b], W=[rr_b])
        if use_sgn:
            P.op("act", lambda e: e.activation(out=rr, in_=rr, func=AF.Sin, scale=sgn[:, 0:1]),
                 R=[rr_b, sgn_b], W=[rr_b])
        else:
            P.op("act", lambda e: e.activation(out=rr, in_=rr, func=AF.Sin), R=[rr_b], W=[rr_b])
        P.dma("sp", k.ropeD[dst], rr, R=[rr_b], WP=[k.rope_b])
    P.barrier()
    A.reset()
    cfm = A.alloc([8], F32); cfm_b = Buf()
    cond = A.alloc([8], F32); cond_b = Buf()
    adab = A.alloc([4, 24], F32); adab_b = Buf()
    ng = A.alloc([4, 8], F32); ng_b = Buf()
    P.dma("sp", cfm, din["c_fm"], W=[cfm_b])
    P.dma("sp", adab, din["ada_b"], W=[adab_b])
    P.dma("sp", ng, din["norm_g"], W=[ng_b])
    P.op("act", lambda e: e.activation(out=cond, in_=cfm, func=AF.Silu), R=[cfm_b], W=[cond_b])
    wst = [A.alloc([8, 512], F32) for _ in range(2)]
    wst_b = [Buf() for _ in range(2)]
    modr = A.alloc([3072], F32); modr_b = Buf()
    abr = A.alloc([3072], F32); abr_b = Buf()
    one1 = A.alloc([1], F32); one1_b = Buf()
    P.op("pool", lambda e: e.memset(one1, 1.0), W=[one1_b])
    n = 0
    for i in range(DEPTH):
        P.dma("sp", abr[0:1, :], din["ada_b_row"][i:i + 1, :], W=[abr_b])
        for mg in range(6):
            b = n % 2
            n += 1
            P.dma("sp", wst[b], din["ada_w"][i, :, mg * 512:(mg + 1) * 512].rearrange("(c p) n -> p c n", p=128), W=[wst_b[b]])
            pr, pr_b = k.ps[n % 4], k.ps_b[n % 4]
            for c in range(8):
                P.op("pe", lambda e, b=b, c=c, pr=pr: e.matmul(pr[0:1, :], lhsT=cond[:, c:c + 1], rhs=wst[b][:, c, :], start=(c == 0), stop=(c == 7)),
                     R=[wst_b[b], cond_b], W=[pr_b], inc=(c == 7))
            P.op("act", lambda e, mg=mg, pr=pr: e.activation(out=modr[0:1, mg * 512:(mg + 1) * 512], in_=pr[0:1, :], func=AF.Copy), R=[pr_b], WP=[modr_b])
        P.op("dve", lambda e: e.tensor_tensor(out=modr[0:1, :], in0=modr[0:1, :], in1=abr[0:1, :], op=ALU.add), R=[abr_b], WP=[modr_b])
        P.dma("sp", k.modrow[i:i + 1, :], modr[0:1, :], R=[modr_b], WP=[k.modrow_b])
        pm, pm_b = k.ps[4 + i % 2], k.ps_b[4 + i % 2]
        for m in range(24):
            P.op("pe", lambda e, m=m, pm=pm: e.matmul(pm[:, m:m + 1], lhsT=modr[0:1, m * 128:(m + 1) * 128], rhs=one1[0:1, 0:1], start=True, stop=True),
                 R=[modr_b, one1_b], WP=[pm_b], inc=(m == 23))
        P.op("dve", lambda e, i=i, pm=pm: e.tensor_copy(out=k.mod[:, i, :], in_=pm[:, 0:24]), R=[pm_b], WP=[k.mod_b])
        P.op("dve", lambda e, i=i: e.scalar_tensor_tensor(out=k.gmod[:, i, :], in0=k.mod[:, i, 8:16], scalar=1.0, in1=ng[:, i, :],
                                                         op0=ALU.add, op1=ALU.mult), R=[k.mod_b, ng_b], WP=[k.gmod_b])
    P.barrier()


def load_cast_weights(k, dst, src, ncols, kchunks, stg, stg_b, dst_b, cw=256):
    P = k.P
    n = 0
    for c0 in range(0, ncols, cw):
        w = min(cw, ncols - c0)
        b = k.stg_n % 2
        k.stg_n += 1
        P.dma("sp", stg[b][:, 0:kchunks, 0:w], src[:, c0:c0 + w].rearrange("(c p) n -> p c n", p=128), W=[stg_b[b]])
        if n % 2 == 0:
            P.op("act", lambda e, b=b, c0=c0, w=w: e.activation(out=dst[:, :, c0:c0 + w], in_=stg[b][:, 0:kchunks, 0:w], func=AF.Copy),
                 R=[stg_b[b]], WP=[dst_b])
        else:
            P.op("dve", lambda e, b=b, c0=c0, w=w: e.tensor_copy(out=dst[:, :, c0:c0 + w], in_=stg[b][:, 0:kchunks, 0:w]),
                 R=[stg_b[b]], WP=[dst_b])
        n += 1


class Group:
    def __init__(self, k, a, dst_rows, dst_b, nch, rows_last=128):
        i = a.grp_n % 2
        a.grp_n += 1
        self.k, self.buf, self.b = k, a.gst[i], a.gst_b[i]
        self.dst_rows, self.dst_b, self.nch, self.rows_last = dst_rows, dst_b, nch, rows_last

    def slot(self, j):
        return self.buf[:, j, :]

    def flush(self):
        P = self.k.P
        if self.rows_last == 128:
            P.dma("sp", self.dst_rows.rearrange("(c p) t -> p c t", p=128), self.buf[:, 0:self.nch, :], R=[self.b], WP=[self.dst_b])
        else:
            assert self.nch == 1
            P.dma("sp", self.dst_rows, self.buf[0:self.rows_last, 0, :], R=[self.b], WP=[self.dst_b])


def phase_a_common(k, li, vshape):
    A, P = k.arena, k.P
    a = K()
    a.xt4s = [A.alloc([4, D], F32) for _ in range(2)]
    a.xt4_bss = [[Buf(), Buf()], [Buf(), Buf()]]
    xflat = a.xt4s[1].rearrange("p s d -> p (s d)")
    a.stg = [xflat[:, i * 2048:(i + 1) * 2048].rearrange("p (c n) -> p c n", c=8) for i in range(2)]
    a.stg_b = a.xt4_bss[1]
    a.junk = A.alloc([D], F32); a.junk_b = Buf()
    a.ss = A.alloc([16], F32); a.ss_b = Buf()
    a.hT = [A.alloc([8, 512], BF16) for _ in range(2)]
    a.hT_b = [Buf() for _ in range(2)]
    a.gst = [A.alloc([4, 512], BF16) for _ in range(2)]
    a.gst_b = [Buf() for _ in range(2)]
    a.grp_n = 0
    a.vst = A.alloc([vshape[0], 4, vshape[1]], BF16); a.vst_b = Buf()
    a.rt = [A.alloc([512], F32) for _ in range(2)]
    a.rt_b = [Buf() for _ in range(2)]
    a.rcs = [A.alloc([2, 512], F32) for _ in range(2)]
    a.rcs_b = [Buf() for _ in range(2)]
    k.stg_n = 0
    a.mm_n = 0
    a.tr_n = 0
    return a


def hT_load(k, a, st, xsrc, xsrc_b):
    P = k.P
    hb = st % 2
    P.dma("sp", a.xt4s[hb], xsrc[st * 512:(st + 1) * 512, :].rearrange("(s p) d -> p s d", p=128),
          R=[xsrc_b[st * 4 + s_] for s_ in range(4)], W=list(a.xt4_bss[hb]))


def hT_front(k, a, st):
    P = k.P
    hb = st % 2
    xt4, xt4_bs, ss, ss_b = a.xt4s[hb], a.xt4_bss[hb], a.ss, a.ss_b
    for sub in range(4):
        P.op("act", lambda e, sub=sub: e.activation(out=a.junk, in_=xt4[:, sub, :], func=AF.Square, accum_out=ss[:, sub:sub + 1]),
             R=list(xt4_bs), W=[a.junk_b], WP=[ss_b])
    P.op("dve", lambda e: e.tensor_scalar(out=ss[:, 4:8], in0=ss[:, 0:4], scalar1=1.0 / D, scalar2=EPS, op0=ALU.mult, op1=ALU.add),
         R=[ss_b], WP=[ss_b])
    P.op("act", lambda e: e.activation(out=ss[:, 8:12], in_=ss[:, 4:8], func=AF.Sqrt), R=[ss_b], WP=[ss_b])
    P.op("dve", lambda e: e.reciprocal(out=ss[:, 12:16], in_=ss[:, 8:12]), R=[ss_b], WP=[ss_b])
    for sub in range(4):
        P.op("dve", lambda e, sub=sub: e.tensor_scalar(out=xt4[:, sub, :], in0=xt4[:, sub, :], scalar1=ss[:, 12 + sub:13 + sub], scalar2=None, op0=ALU.mult),
             R=[ss_b], WP=list(xt4_bs))


def hT_back(k, a, li, st):
    P = k.P
    hb = st % 2
    hT, hT_b = a.hT[hb], a.hT_b[hb]
    xt4, xt4_bs = a.xt4s[hb], a.xt4_bss[hb]
    for sub in range(4):
        for half in range(2):
            pi = 4 + a.tr_n % 4
            a.tr_n += 1
            pt, pt_b = k.ps[pi], k.ps_b[pi]
            for c4 in range(4):
                c = half * 4 + c4
                P.op("pe", lambda e, sub=sub, c=c, c4=c4, pt=pt: e.transpose(out=pt[:, c4 * 128:(c4 + 1) * 128], in_=xt4[:, sub, c * 128:(c + 1) * 128], identity=k.ident[:]),
                     R=list(xt4_bs) + [k.ident_b], W=[pt_b], inc=(c4 == 3))
            for c4 in range(4):
                c = half * 4 + c4
                o = hT[:, c, sub * 128:(sub + 1) * 128]
                i_ = pt[:, c4 * 128:(c4 + 1) * 128]
                if c % 2 == 0:
                    P.op("act", lambda e, o=o, i_=i_, c=c: e.activation(out=o, in_=i_, func=AF.Identity, scale=k.gmod[:, li, c:c + 1], bias=k.mod[:, li, c:c + 1]),
                         R=[pt_b, k.gmod_b, k.mod_b], WP=[hT_b])
                else:
                    P.op("dve", lambda e, o=o, i_=i_, c=c: e.tensor_scalar(out=o, in0=i_, scalar1=k.gmod[:, li, c:c + 1], scalar2=k.mod[:, li, c:c + 1], op0=ALU.mult, op1=ALU.add),
                         R=[pt_b, k.gmod_b, k.mod_b], WP=[hT_b])
    return hT, hT_b, a.rcs[hb], a.rcs_b[hb]


def rcs_load(k, a, st):
    hb = st % 2
    k.P.dma("sp", a.rcs[hb], k.ropeD[:, :, st * 512:(st + 1) * 512].rearrange("j p t -> p j t"), R=[k.rope_b], W=[a.rcs_b[hb]])


def hT_prime(k, a, li, xsrc, xsrc_b):
    rcs_load(k, a, 0)
    hT_load(k, a, 0, xsrc, xsrc_b)
    hT_load(k, a, 1, xsrc, xsrc_b)
    hT_front(k, a, 0)
    return hT_back(k, a, li, 0)


def hT_step_begin(k, a, st, xsrc, xsrc_b):
    if st + 1 < NST:
        rcs_load(k, a, st + 1)
        hT_front(k, a, st + 1)


def hT_step_mid(k, a, li, st, xsrc, xsrc_b):
    nxt = None
    if st + 1 < NST:
        nxt = hT_back(k, a, li, st + 1)
    if st + 2 < NST:
        hT_load(k, a, st + 2, xsrc, xsrc_b)
    return nxt


def mm_psum(k, a):
    i = a.mm_n % 4
    a.mm_n += 1
    return k.ps[i], k.ps_b[i]


def fm_group(k, pm, pm_b, w, w_b, kch, col0, ncols, rhs_fn, rhs_b):
    P = k.P
    for c in range(kch):
        P.op("pe", lambda e, c=c: e.matmul(pm[0:ncols, :], lhsT=w[:, c, col0:col0 + ncols], rhs=rhs_fn(c), start=(c == 0), stop=(c == kch - 1)),
             R=[w_b] + list(rhs_b), W=[pm_b], inc=(c == kch - 1))


def job_fm(k, a, w, w_b, kch, col0, ncols, rhs_fn, rhs_b, out, out_b, act=None, eng="act"):
    P = k.P
    pm, pm_b = mm_psum(k, a)
    fm_group(k, pm, pm_b, w, w_b, kch, col0, ncols, rhs_fn, rhs_b)
    o = out[0:ncols, :]
    if act is not None:
        P.op("act", lambda e: e.activation(out=o, in_=pm[0:ncols, :], func=act), R=[pm_b], WP=[out_b])
    elif eng == "act":
        P.op("act", lambda e: e.activation(out=o, in_=pm[0:ncols, :], func=AF.Copy), R=[pm_b], WP=[out_b])
    else:
        P.op("dve", lambda e: e.tensor_copy(out=o, in_=pm[0:ncols, :]), R=[pm_b], WP=[out_b])


def job_rope(k, a, w, w_b, kch, col0, w2, w2_b, cols0, ncols, rhs_fn, rhs_b, rcs, rcs_b, out, out_b):
    P = k.P
    pm, pm_b = mm_psum(k, a)
    fm_group(k, pm, pm_b, w, w_b, kch, col0, ncols, rhs_fn, rhs_b)
    pm2, pm2_b = mm_psum(k, a)
    fm_group(k, pm2, pm2_b, w2, w2_b, kch, cols0, ncols, rhs_fn, rhs_b)
    r0, r0_b, r1, r1_b = a.rt[0], a.rt_b[0], a.rt[1], a.rt_b[1]
    P.op("dve", lambda e: e.tensor_tensor(out=r0[0:ncols, :], in0=pm[0:ncols, :], in1=rcs[0:ncols, 0, :], op=ALU.mult),
         R=[pm_b, rcs_b], W=[r0_b])
    P.op("dve", lambda e: e.tensor_tensor(out=r1[0:ncols, :], in0=pm2[0:ncols, :], in1=rcs[0:ncols, 1, :], op=ALU.mult),
         R=[pm2_b, rcs_b], W=[r1_b])
    P.op("pool", lambda e: e.tensor_tensor(out=out[0:ncols, :], in0=r0[0:ncols, :], in1=r1[0:ncols, :], op=ALU.add),
         R=[r0_b, r1_b], WP=[out_b])


def job_tm(k, a, w, w_b, kch, col0, ncols, lhs_fn, lhs_b, out, out_b):
    P = k.P
    pm, pm_b = mm_psum(k, a)
    for c in range(kch):
        P.op("pe", lambda e, c=c: e.matmul(pm[:, 0:ncols], lhsT=lhs_fn(c), rhs=w[:, c, col0:col0 + ncols], start=(c == 0), stop=(c == kch - 1)),
             R=[w_b] + list(lhs_b), W=[pm_b], inc=(c == kch - 1))
    dv = out.shape[-1]
    P.op("act", lambda e: e.activation(out=out, in_=pm[:, 0:ncols].rearrange("p (h d) -> p h d", d=dv), func=AF.Copy), R=[pm_b], WP=[out_b])


def band_cols(kk, W):
    lo = max(0, 128 * kk)
    hi = 512 if W is None else min(512, 128 * kk + 127 + W)
    return (lo, hi)


def attention_rows(k, rows, pt_bufs, scale, s_banks=(0, 1, 2), look=2):
    P = k.P
    steps = []
    pending = []
    for ri, row in enumerate(rows):
        n = len(row["ksteps"])
        for si, stp in enumerate(row["ksteps"]):
            steps.append((ri, si, n, stp))

    def issue_s(idx):
        ri, si, n, stp = steps[idx]
        sb_i = s_banks[idx % len(s_banks)]
        ps, ps_b = k.ps[sb_i], k.ps_b[sb_i]
        kp = stp.get("kp", 128)
        c0, c1 = stp.get("cols", (0, 512))
        ns = len(stp["s"])
        for j, (lhsT, rhs, bufs) in enumerate(stp["s"]):
            P.op("pe", lambda e, lhsT=lhsT, rhs=rhs, j=j, ps=ps, kp=kp, c0=c0, c1=c1: e.matmul(ps[0:kp, c0:c1], lhsT=lhsT, rhs=rhs[:, c0:c1], start=(j == 0), stop=(j == ns - 1)),
                 R=list(bufs), W=[ps_b], inc=(j == ns - 1))
        pt, pt_b = pt_bufs[idx % len(pt_bufs)]
        P.op("act", lambda e, pt=pt, ps=ps, kp=kp, c0=c0, c1=c1: e.activation(out=pt[0:kp, c0:c1], in_=ps[0:kp, c0:c1], func=AF.Exp, scale=float(scale)),
             R=[ps_b], W=[pt_b])

    def issue_pv(idx):
        ri, si, n, stp = steps[idx]
        pt, pt_b = pt_bufs[idx % len(pt_bufs)]
        kp = stp.get("kp", 128)
        c0, c1 = stp.get("cols", (0, 512))
        nob = rows[ri].get("o_banks", 2)
        par = ri % nob
        dacc = rows[ri].get("dacc")
        if dacc is not None:
            acc_t, acc_b_ = dacc
            if si == 0:
                P.op("dve", lambda e, pt=pt, kp=kp, c0=c0, c1=c1: e.tensor_copy(out=acc_t[0:kp, c0:c1], in_=pt[0:kp, c0:c1]), R=[pt_b], W=[acc_b_])
            else:
                P.op("dve", lambda e, pt=pt, kp=kp, c0=c0, c1=c1: e.tensor_tensor(out=acc_t[0:kp, c0:c1], in0=acc_t[0:kp, c0:c1], in1=pt[0:kp, c0:c1], op=ALU.add),
                     R=[pt_b], WP=[acc_b_])
        for (which, lhsT, bufs, mrows) in stp["pv"]:
            pi = 7 if which == "A" else (3 + par if which == "O" else 5 + ri % 2)
            po, po_b = k.ps[pi], k.ps_b[pi]
            P.op("pe", lambda e, lhsT=lhsT, po=po, pt=pt, kp=kp, mrows=mrows, si=si, n=n, c0=c0, c1=c1: e.matmul(po[0:mrows, c0:c1], lhsT=lhsT, rhs=pt[0:kp, c0:c1], start=(si == 0), stop=(si == n - 1), skip_group_check=True),
                 R=list(bufs) + [pt_b], W=[po_b], inc=True)
        if si == 0:
            pending.extend(rows[ri].get("pre", ()))
        if si == n - 1:
            pending[:] = [p_ for p_ in pending if p_ is not None]
            while pending:
                pending.pop(0)()
            ret = rows[ri]["epilogue"](k.ps[3 + par], k.ps_b[3 + par], k.ps[5 + ri % 2], k.ps_b[5 + ri % 2])
            if ret:
                pending.extend(ret)
        elif pending:
            p_ = pending.pop(0)
            if p_ is not None:
                p_()

    N = len(steps)
    if N == 0:
        return
    assert len(pt_bufs) >= look + 1 and len(s_banks) >= look + 1
    for j in range(min(look, N)):
        issue_s(j)
    for idx in range(N):
        if idx + look < N:
            issue_s(idx + look)
        issue_pv(idx)
    while pending:
        p_ = pending.pop(0)
        if p_ is not None:
            p_()


def phase_c(k, li, wout_name, last):
    P, nc, A, din = k.P, k.nc, k.arena, k.din
    P.barrier()
    A.reset()
    wo = A.alloc([8, D], BF16); wo_b = Buf()
    stg = [A.alloc([8, 256], F32) for _ in range(2)]
    stg_b = [Buf() for _ in range(2)]
    k.stg_n = 0
    load_cast_weights(k, wo, din[wout_name], D, 8, stg, stg_b, wo_b)
    gbc = A.alloc([D], F32); gbc_b = Buf()
    P.dma("sp", gbc, k.modrow[li:li + 1, 2048:3072].partition_broadcast(128), R=[k.modrow_b], W=[gbc_b])
    if last:
        fg = A.alloc([D], F32); fg_b = Buf()
        P.dma("sp", fg, din["final_g"].partition_broadcast(128), W=[fg_b])
    ots = [A.alloc([8, 512], BF16) for _ in range(2)]
    ots_b = [Buf() for _ in range(2)]
    xt = [A.alloc([D], F32) for _ in range(2)]
    xt_b = [Buf() for _ in range(2)]
    xo = [A.alloc([D], F32) for _ in range(2)]
    xo_b = [Buf() for _ in range(2)]
    junk = A.alloc([D], F32); junk_b = Buf()
    ss = [A.alloc([4], F32) for _ in range(2)]
    ss_b = [Buf() for _ in range(2)]
    xsrc = din["x_in"] if li == 0 else k.xr
    n = 0
    for st in range(NST):
        ob = st % 2
        P.dma("sp", ots[ob], k.OT[:, st * 512:(st + 1) * 512].rearrange("(c p) t -> p c t", p=128), R=[k.OT_b], W=[ots_b[ob]])
        for sub in range(4):
            t = st * 4 + sub
            b = t % 2
            P.dma("sp", xt[b], xsrc[t * 128:(t + 1) * 128, :], R=[k.xr_b[t]], W=[xt_b[b]])
            for half in range(2):
                pm, pm_b = k.ps[n % 4], k.ps_b[n % 4]
                n += 1
                for c in range(8):
                    P.op("pe", lambda e, c=c, ob=ob, sub=sub, half=half, pm=pm: e.matmul(pm[:, :], lhsT=ots[ob][:, c, sub * 128:(sub + 1) * 128], rhs=wo[:, c, half * 512:(half + 1) * 512], start=(c == 0), stop=(c == 7)),
                         R=[ots_b[ob], wo_b], W=[pm_b], inc=(c == 7))
                hs = slice(half * 512, (half + 1) * 512)
                P.op("dve", lambda e, b=b, hs=hs, pm=pm: e.tensor_tensor(out=xo[b][:, hs], in0=pm[:, :], in1=gbc[:, hs], op=ALU.mult),
                     R=[pm_b, gbc_b], WP=[xo_b[b]])
                P.op("pool", lambda e, b=b, hs=hs: e.tensor_tensor(out=xo[b][:, hs], in0=xo[b][:, hs], in1=xt[b][:, hs], op=ALU.add),
                     R=[xt_b[b]], WP=[xo_b[b]])
            if not last:
                P.dma("act", k.xr[t * 128:(t + 1) * 128, :], xo[b], R=[xo_b[b]], W=[k.xr_b[t]])
            else:
                s_, s_b = ss[b], ss_b[b]
                P.op("act", lambda e, b=b, s_=s_: e.activation(out=junk, in_=xo[b], func=AF.Square, accum_out=s_[:, 0:1]),
                     R=[xo_b[b]], W=[junk_b], WP=[s_b])
                P.op("dve", lambda e, s_=s_: e.tensor_scalar(out=s_[:, 1:2], in0=s_[:, 0:1], scalar1=1.0 / D, scalar2=EPS, op0=ALU.mult, op1=ALU.add),
                     R=[s_b], WP=[s_b])
                P.op("act", lambda e, s_=s_: e.activation(out=s_[:, 2:3], in_=s_[:, 1:2], func=AF.Sqrt), R=[s_b], WP=[s_b])
                P.op("dve", lambda e, s_=s_: e.reciprocal(out=s_[:, 3:4], in_=s_[:, 2:3]), R=[s_b], WP=[s_b])
                P.op("dve", lambda e, b=b, s_=s_: e.scalar_tensor_tensor(out=xo[b], in0=xo[b], scalar=s_[:, 3:4], in1=fg, op0=ALU.mult, op1=ALU.mult),
                     R=[s_b, fg_b], WP=[xo_b[b]])
                P.dma("act", k.out[t * 128:(t + 1) * 128, :], xo[b], R=[xo_b[b]], WP=[k.out_b])
    P.barrier()


def mla_layer(k, li, j, last):
    P, nc, A, din = k.P, k.nc, k.arena, k.din
    pre = "mla%d_" % j
    C = MLA_COLS
    xsrc = din["x_in"] if li == 0 else k.xr
    A.reset()
    a = phase_a_common(k, li, (8, 128))
    wa = A.alloc([8, C["n"]], BF16); wa_b = Buf()
    load_cast_weights(k, wa, din[pre + "wa"], C["n"], 8, a.stg, a.stg_b, wa_b)
    wqn = A.alloc([2, 1024], BF16); wqn_b = Buf()
    load_cast_weights(k, wqn, din[pre + "wqn"], 1024, 2, a.stg, a.stg_b, wqn_b)
    wqr = A.alloc([2, 512], BF16); wqr_b = Buf()
    load_cast_weights(k, wqr, din[pre + "wqr"], 512, 2, a.stg, a.stg_b, wqr_b)
    wqrs = A.alloc([2, 512], BF16); wqrs_b = Buf()
    load_cast_weights(k, wqrs, din[pre + "wqrs"], 512, 2, a.stg, a.stg_b, wqrs_b)
    wkn = A.alloc([1, 1024], BF16); wkn_b = Buf()
    load_cast_weights(k, wkn, din[pre + "wkn"], 1024, 1, a.stg, a.stg_b, wkn_b)
    wv = A.alloc([1, 1024], BF16); wv_b = Buf()
    load_cast_weights(k, wv, din[pre + "wv"], 1024, 1, a.stg, a.stg_b, wv_b)
    qg = A.alloc([2], F32); qg_b = Buf()
    kvg = A.alloc([1], F32); kvg_b = Buf()
    P.dma("sp", qg, din[pre + "qg"], W=[qg_b])
    P.dma("sp", kvg, din[pre + "kvg"], W=[kvg_b])
    lat = A.alloc([3, 512], F32); lat_b = Buf()
    sq = A.alloc([3, 512], F32); sq_b = Buf()
    rs = [A.alloc([512], F32) for _ in range(2)]
    rs_b = [Buf() for _ in range(2)]
    latn = A.alloc([3, 512], BF16); latn_b = Buf()
    nxt = hT_prime(k, a, li, xsrc, k.xr_b)
    for st in range(NST):
        tok0 = st * 512
        ts = slice(tok0, tok0 + 512)
        hT, hT_b, rcs, rcs_b = nxt
        rhs_h = lambda c, hT=hT: hT[:, c, :]
        hT_step_begin(k, a, st, xsrc, k.xr_b)
        for ci in range(3):
            pm, pm_b = mm_psum(k, a)
            fm_group(k, pm, pm_b, wa, wa_b, 8, ci * 128, 128, rhs_h, [hT_b])
            P.op("act", lambda e, ci=ci, pm=pm: e.activation(out=lat[:, ci, :], in_=pm[:, :], func=AF.Copy), R=[pm_b], WP=[lat_b])
            P.op("act", lambda e, ci=ci, pm=pm: e.activation(out=sq[:, ci, :], in_=pm[:, :], func=AF.Square), R=[pm_b], WP=[sq_b])
        g = Group(k, a, k.KT[1024:1088, ts], k.KT_b, 1, rows_last=64)
        job_rope(k, a, wa, wa_b, 8, C["kpe"], wa, wa_b, C["kpes"], 64, rhs_h, [hT_b], rcs, rcs_b, g.slot(0), g.b)
        g.flush()
        for h4 in range(2):
            g = Group(k, a, k.GT[h4 * 512:(h4 + 1) * 512, ts], k.GT_b, 4)
            for j4 in range(4):
                job_fm(k, a, wa, wa_b, 8, C["gate"] + (h4 * 4 + j4) * 128, 128, rhs_h, [hT_b], g.slot(j4), g.b, act=AF.Silu)
            g.flush()
        for which, chunks, nfeat, gsb, gsb_b in ((0, (0, 1), 256, qg, qg_b), (1, (2,), 128, kvg, kvg_b)):
            pm, pm_b = mm_psum(k, a)
            for jj, ci in enumerate(chunks):
                P.op("pe", lambda e, ci=ci, jj=jj, pm=pm, chunks=chunks: e.matmul(pm[:, :], lhsT=k.onesf[:], rhs=sq[:, ci, :], start=(jj == 0), stop=(jj == len(chunks) - 1)),
                     R=[k.onesf_b, sq_b], W=[pm_b], inc=(jj == len(chunks) - 1))
            r_, r_b = rs[which], rs_b[which]
            P.op("act", lambda e, r_=r_, pm=pm, nfeat=nfeat: e.activation(out=r_, in_=pm[:, :], func=AF.Ln, scale=1.0 / nfeat, bias=EPS), R=[pm_b], W=[r_b])
            P.op("act", lambda e, r_=r_: e.activation(out=r_, in_=r_, func=AF.Exp, scale=-0.5), R=[r_b], W=[r_b])
            for jj, ci in enumerate(chunks):
                P.op("dve", lambda e, ci=ci, jj=jj, r_=r_, gsb=gsb: e.scalar_tensor_tensor(out=latn[:, ci, :], in0=lat[:, ci, :], scalar=gsb[:, jj:jj + 1], in1=r_, op0=ALU.mult, op1=ALU.mult),
                     R=[lat_b, r_b, gsb_b], WP=[latn_b])
        nxt = hT_step_mid(k, a, li, st, xsrc, k.xr_b)
        rhs_q = lambda c: latn[:, c, :]
        rhs_kv = lambda c: latn[:, 2, :]
        for h4 in range(2):
            g = Group(k, a, k.QT[h4 * 512:(h4 + 1) * 512, ts], k.QT_b, 4)
            for j4 in range(4):
                job_fm(k, a, wqn, wqn_b, 2, (h4 * 4 + j4) * 128, 128, rhs_q, [latn_b], g.slot(j4), g.b, eng="dve")
            g.flush()
        g = Group(k, a, k.QT[1024:1536, ts], k.QT_b, 4)
        for hp in range(4):
            job_rope(k, a, wqr, wqr_b, 2, hp * 128, wqrs, wqrs_b, hp * 128, 128, rhs_q, [latn_b], rcs, rcs_b, g.slot(hp), g.b)
        g.flush()
        for h4 in range(2):
            g = Group(k, a, k.KT[h4 * 512:(h4 + 1) * 512, ts], k.KT_b, 4)
            for j4 in range(4):
                job_fm(k, a, wkn, wkn_b, 1, (h4 * 4 + j4) * 128, 128, rhs_kv, [latn_b], g.slot(j4), g.b, eng="dve")
            g.flush()
        for sub in range(4):
            for half in range(2):
                job_tm(k, a, wv, wv_b, 1, half * 512, 512, lambda c, sub=sub: latn[:, 2, sub * 128:(sub + 1) * 128], [latn_b],
                       a.vst[:, half * 4:(half + 1) * 4, sub, :], a.vst_b)
        P.dma("sp", k.VT[:, :, st * 4:(st + 1) * 4, :].rearrange("h p t d -> p h t d"), a.vst, R=[a.vst_b], WP=[k.VT_b])
    P.barrier()
    A.reset()
    kpe = A.alloc([S], BF16); kpe_b = Buf()
    P.dma("sp", kpe[0:64, :], k.KT[1024:1088, :], R=[k.KT_b], W=[kpe_b])
    hb = []
    for i in range(2):
        hb.append(dict(qn=A.alloc([S], BF16), qr=A.alloc([S], BF16), kn=A.alloc([S], BF16), v=A.alloc([NT, 128], BF16),
                       g=A.alloc([S], BF16), b=Buf()))
    pt_bufs = [(A.alloc([512], BF16), Buf()) for _ in range(5)]
    rd = [A.alloc([512], F32) for _ in range(2)]
    rd_b = [Buf() for _ in range(2)]
    og = [A.alloc([512], F32) for _ in range(2)]
    og_b = [Buf() for _ in range(2)]
    ost = [A.alloc([512], BF16) for _ in range(2)]
    ost_b = [Buf() for _ in range(2)]
    dacc = [A.alloc([512], F32) for _ in range(2)]
    dacc_b = [Buf() for _ in range(2)]
    scale = 192.0 ** -0.5
    ep_n = [0]
    for h in range(8):
        hbuf = hb[h % 2]
        b_ = hbuf["b"]
        P.dma("sp", hbuf["qn"], k.QT[h * 128:(h + 1) * 128, :], R=[k.QT_b], WP=[b_])
        P.dma("sp", hbuf["qr"][0:64, :], k.QT[1024 + h * 64:1024 + (h + 1) * 64, :], R=[k.QT_b], WP=[b_])
        P.dma("sp", hbuf["kn"], k.KT[h * 128:(h + 1) * 128, :], R=[k.KT_b], WP=[b_])
        P.dma("sp", hbuf["v"], k.VT[h], R=[k.VT_b], WP=[b_])
        P.dma("sp", hbuf["g"], k.GT[h * 128:(h + 1) * 128, :], R=[k.GT_b], WP=[b_])
        rows = []
        for qi in range(NST):
            qs = slice(qi * 512, (qi + 1) * 512)
            ksteps = []
            for ki in range(4 * qi + 4):
                ks = slice(ki * 128, (ki + 1) * 128)
                s_list = [(hbuf["kn"][:, ks], hbuf["qn"][:, qs], [b_]),
                          (kpe[0:64, ks], hbuf["qr"][0:64, qs], [b_, kpe_b])]
                kk = ki - 4 * qi
                if kk >= 0:
                    s_list.append((k.identb[:], k.masks[:, MASK_CAUSAL[kk], :], [k.identb_b, k.masks_b]))
                pv = [("O", hbuf["v"][:, ki, :], [b_], 128)]
                ksteps.append(dict(s=s_list, pv=pv, cols=band_cols(max(kk, 0), None)))

            ri_ = len(rows)

            def epilogue(O, O_b, Dn, Dn_b, h=h, qs=qs, hbuf=hbuf, b_=b_, ri_=ri_):
                i = ri_ % 2
                Dp, Dp_b = k.ps[6], k.ps_b[6]

                def rest():
                    P.op("pe", lambda e: e.matmul(Dp[:, :], lhsT=k.onesf[:], rhs=dacc[i], start=True, stop=True), R=[k.onesf_b, dacc_b[i]], W=[Dp_b])
                    P.op("act", lambda e: e.activation(out=rd[i], in_=Dp[:, :], func=AF.Ln), R=[Dp_b], W=[rd_b[i]])
                    P.op("act", lambda e: e.activation(out=rd[i], in_=rd[i], func=AF.Exp, scale=-1.0), R=[rd_b[i]], W=[rd_b[i]])
                    P.op("dve", lambda e: e.tensor_tensor(out=og[i], in0=O[:, :], in1=rd[i], op=ALU.mult), R=[O_b, rd_b[i]], W=[og_b[i]])
                    P.op("pool", lambda e: e.tensor_tensor(out=ost[i], in0=og[i], in1=hbuf["g"][:, qs], op=ALU.mult), R=[og_b[i], b_], W=[ost_b[i]])
                    P.dma("pool", k.OT[h * 128:(h + 1) * 128, qs], ost[i], R=[ost_b[i]], WP=[k.OT_b])
                return [None, None, rest]
            rows.append(dict(ksteps=ksteps, o_banks=3, dacc=(dacc[ri_ % 2], dacc_b[ri_ % 2]), epilogue=epilogue))
        attention_rows(k, rows, pt_bufs, scale, s_banks=(0, 1, 2, 7), look=3)
    phase_c(k, li, pre + "wout", last)


def mla_rope_q(k, a, wqr, wqr_b, wqrs, wqrs_b, hp, rhs_q, latn_b, tok0):
    P = k.P
    pm, pm_b = mm_psum(k, a)
    fm_group(k, pm, pm_b, wqr, wqr_b, 2, hp * 128, 128, rhs_q, [latn_b])
    pm2, pm2_b = mm_psum(k, a)
    fm_group(k, pm2, pm2_b, wqrs, wqrs_b, 2, hp * 128, 128, rhs_q, [latn_b])
    r0, r0_b, r1, r1_b = a.rt[0], a.rt_b[0], a.rt[1], a.rt_b[1]
    rcs, rcs_b = a.cur_rcs, a.cur_rcs_b
    P.op("dve", lambda e: e.tensor_tensor(out=r0, in0=pm[:, :], in1=rcs[:, 0, :], op=ALU.mult), R=[pm_b, rcs_b], W=[r0_b])
    P.op("dve", lambda e: e.tensor_tensor(out=r1, in0=pm2[:, :], in1=rcs[:, 1, :], op=ALU.mult), R=[pm2_b, rcs_b], W=[r1_b])
    st, st_b = fst_next(a)
    P.op("pool", lambda e: e.tensor_tensor(out=st, in0=r0, in1=r1, op=ALU.add), R=[r0_b, r1_b], W=[st_b])
    P.dma("pool", k.QT[1024 + hp * 128:1024 + (hp + 1) * 128, tok0:tok0 + 512], st, R=[st_b], WP=[k.QT_b])


def swa_layer(k, li, last):
    P, nc, A, din = k.P, k.nc, k.arena, k.din
    C = SWA_COLS
    xsrc = din["x_in"] if li == 0 else k.xr
    A.reset()
    a = phase_a_common(k, li, (2, 64))
    wa = A.alloc([8, C["n"]], BF16); wa_b = Buf()
    load_cast_weights(k, wa, din["swa_wa"], C["n"], 8, a.stg, a.stg_b, wa_b)
    nxt = hT_prime(k, a, li, xsrc, k.xr_b)
    for st in range(NST):
        tok0 = st * 512
        ts = slice(tok0, tok0 + 512)
        hT, hT_b, rcs, rcs_b = nxt
        rhs_h = lambda c, hT=hT: hT[:, c, :]
        hT_step_begin(k, a, st, xsrc, k.xr_b)
        g = Group(k, a, k.KT[0:128, ts], k.KT_b, 1)
        job_rope(k, a, wa, wa_b, 8, C["k"], wa, wa_b, C["ks"], 128, rhs_h, [hT_b], rcs, rcs_b, g.slot(0), g.b)
        g.flush()
        for m4 in range(2):
            g = Group(k, a, k.QT[m4 * 512:(m4 + 1) * 512, ts], k.QT_b, 4)
            for j4 in range(4):
                m = m4 * 4 + j4
                job_rope(k, a, wa, wa_b, 8, C["q"] + m * 128, wa, wa_b, C["qs"] + m * 128, 128, rhs_h, [hT_b], rcs, rcs_b, g.slot(j4), g.b)
            g.flush()
        for sub in range(4):
            job_tm(k, a, wa, wa_b, 8, C["v"], 128, lambda c, sub=sub, hT=hT: hT[:, c, sub * 128:(sub + 1) * 128], [hT_b],
                   a.vst[:, :, sub, :], a.vst_b)
        P.dma("sp", k.VS[0:2, :, st * 4:(st + 1) * 4, :].rearrange("h p t d -> p h t d"), a.vst, R=[a.vst_b], WP=[k.VS_b])
        nxt = hT_step_mid(k, a, li, st, xsrc, k.xr_b)
        for m4 in range(2):
            g = Group(k, a, k.GT[m4 * 512:(m4 + 1) * 512, ts], k.GT_b, 4)
            for j4 in range(4):
                job_fm(k, a, wa, wa_b, 8, C["gate"] + (m4 * 4 + j4) * 128, 128, rhs_h, [hT_b], g.slot(j4), g.b, act=AF.Silu)
            g.flush()
    P.barrier()
    A.reset()
    kT = [A.alloc([S], BF16) for _ in range(2)]
    kT_b = Buf()
    vv = [A.alloc([NT, 64], BF16) for _ in range(2)]
    vv_b = Buf()
    for kv in range(2):
        P.dma("sp", kT[kv][0:64, :], k.KT[kv * 64:(kv + 1) * 64, :], R=[k.KT_b], WP=[kT_b])
        P.dma("sp", vv[kv], k.VS[kv], R=[k.VS_b], WP=[vv_b])
    snk = A.alloc([16], F32); snk_b = Buf()
    P.dma("sp", snk, din["swa_sinks"].partition_broadcast(128), W=[snk_b])
    P.op("act", lambda e: e.activation(out=snk, in_=snk, func=AF.Exp), R=[snk_b], W=[snk_b])
    hb = [dict(q=A.alloc([S], BF16), g=A.alloc([S], BF16), b=Buf()) for _ in range(2)]
    pt_bufs = [(A.alloc([512], BF16), Buf()) for _ in range(5)]
    rd = [A.alloc([512], F32) for _ in range(2)]; rd_b = [Buf() for _ in range(2)]
    og = [A.alloc([512], F32) for _ in range(2)]; og_b = [Buf() for _ in range(2)]
    ost = [A.alloc([512], BF16) for _ in range(2)]; ost_b = [Buf() for _ in range(2)]
    ep_n = [0]
    for hh in range(16):
        kv = hh // 8
        hbuf = hb[hh % 2]
        b_ = hbuf["b"]
        P.dma("sp", hbuf["q"][0:64, :], k.QT[hh * 64:(hh + 1) * 64, :], R=[k.QT_b], WP=[b_])
        P.dma("sp", hbuf["g"][0:64, :], k.GT[hh * 64:(hh + 1) * 64, :], R=[k.GT_b], WP=[b_])
        rows = []
        for qi in range(NST):
            qs = slice(qi * 512, (qi + 1) * 512)
            ksteps = []
            for kk in range(-1, 4):
                ki = 4 * qi + kk
                if ki < 0:
                    continue
                ks = slice(ki * 128, (ki + 1) * 128)
                s_list = [(kT[kv][0:64, ks], hbuf["q"][0:64, qs], [b_, kT_b]),
                          (k.identb[:], k.masks[:, MASK_SWA[kk], :], [k.identb_b, k.masks_b])]
                pv = [("O", vv[kv][:, ki, :], [vv_b], 64), ("D", k.onesb[:, 0:64], [k.onesb_b], 64)]
                ksteps.append(dict(s=s_list, pv=pv, cols=band_cols(kk, 128)))

            def epilogue(O, O_b, Dn, Dn_b, hh=hh, qs=qs, hbuf=hbuf, b_=b_):
                i = ep_n[0] % 2
                ep_n[0] += 1
                P.op("act", lambda e: e.activation(out=rd[i][0:64, :], in_=Dn[0:64, :], func=AF.Ln, bias=snk[0:64, hh:hh + 1]),
                     R=[Dn_b, snk_b], W=[rd_b[i]])
                P.op("act", lambda e: e.activation(out=rd[i][0:64, :], in_=rd[i][0:64, :], func=AF.Exp, scale=-1.0), R=[rd_b[i]], W=[rd_b[i]])
                P.op("dve", lambda e: e.tensor_tensor(out=og[i][0:64, :], in0=O[0:64, :], in1=rd[i][0:64, :], op=ALU.mult), R=[O_b, rd_b[i]], W=[og_b[i]])
                P.op("pool", lambda e: e.tensor_tensor(out=ost[i][0:64, :], in0=og[i][0:64, :], in1=hbuf["g"][0:64, qs], op=ALU.mult), R=[og_b[i], b_], W=[ost_b[i]])
                P.dma("pool", k.OT[hh * 64:(hh + 1) * 64, qs], ost[i][0:64, :], R=[ost_b[i]], WP=[k.OT_b])
            rows.append(dict(ksteps=ksteps, epilogue=epilogue))
        attention_rows(k, rows, pt_bufs, 0.125, s_banks=(0, 1, 2, 7), look=3)
    phase_c(k, li, "swa_wout", last)


def nsa_layer(k, li, last):
    P, nc, A, din = k.P, k.nc, k.arena, k.din
    C = NSA_COLS
    xsrc = din["x_in"] if li == 0 else k.xr
    A.reset()
    a = phase_a_common(k, li, (8, 64))
    wa = A.alloc([8, C["n"]], BF16); wa_b = Buf()
    load_cast_weights(k, wa, din["nsa_wa"], C["n"], 8, a.stg, a.stg_b, wa_b)
    nxt = hT_prime(k, a, li, xsrc, k.xr_b)
    for st in range(NST):
        tok0 = st * 512
        ts = slice(tok0, tok0 + 512)
        hT, hT_b, rcs, rcs_b = nxt
        rhs_h = lambda c, hT=hT: hT[:, c, :]
        hT_step_begin(k, a, st, xsrc, k.xr_b)
        g = Group(k, a, k.KT[0:512, ts], k.KT_b, 4)
        for m in range(2):
            job_rope(k, a, wa, wa_b, 8, C["kslc"] + m * 128, wa, wa_b, C["kslcs"] + m * 128, 128, rhs_h, [hT_b], rcs, rcs_b, g.slot(m), g.b)
        for m in range(2):
            job_rope(k, a, wa, wa_b, 8, C["kwin"] + m * 128, wa, wa_b, C["kwins"] + m * 128, 128, rhs_h, [hT_b], rcs, rcs_b, g.slot(2 + m), g.b)
        g.flush()
        g = Group(k, a, k.KT[512:1024, ts], k.KT_b, 4)
        for m in range(2):
            job_fm(k, a, wa, wa_b, 8, C["kcmp"] + m * 128, 128, rhs_h, [hT_b], g.slot(m), g.b, eng="dve")
        for m in range(2):
            job_fm(k, a, wa, wa_b, 8, C["vcmp"] + m * 128, 128, rhs_h, [hT_b], g.slot(2 + m), g.b, eng="dve")
        g.flush()
        for m4 in range(2):
            g = Group(k, a, k.QT[m4 * 512:(m4 + 1) * 512, ts], k.QT_b, 4)
            for j4 in range(4):
                m = m4 * 4 + j4
                job_rope(k, a, wa, wa_b, 8, C["q"] + m * 128, wa, wa_b, C["qs"] + m * 128, 128, rhs_h, [hT_b], rcs, rcs_b, g.slot(j4), g.b)
            g.flush()
        g = Group(k, a, k.GT[1024:1072, ts], k.GT_b, 1, rows_last=48)
        job_fm(k, a, wa, wa_b, 8, C["gbr"], 48, rhs_h, [hT_b], g.slot(0), g.b, act=AF.Sigmoid)
        g.flush()
        for sub in range(4):
            lf = lambda c, sub=sub, hT=hT: hT[:, c, sub * 128:(sub + 1) * 128]
            job_tm(k, a, wa, wa_b, 8, C["vslc"], 256, lf, [hT_b], a.vst[:, 0:4, sub, :], a.vst_b)
            job_tm(k, a, wa, wa_b, 8, C["vwin"], 256, lf, [hT_b], a.vst[:, 4:8, sub, :], a.vst_b)
        P.dma("sp", k.VS[:, :, st * 4:(st + 1) * 4, :].rearrange("h p t d -> p h t d"), a.vst, R=[a.vst_b], WP=[k.VS_b])
        nxt = hT_step_mid(k, a, li, st, xsrc, k.xr_b)
        for m4 in range(2):
            g = Group(k, a, k.GT[m4 * 512:(m4 + 1) * 512, ts], k.GT_b, 4)
            for j4 in range(4):
                job_fm(k, a, wa, wa_b, 8, C["gate"] + (m4 * 4 + j4) * 128, 128, rhs_h, [hT_b], g.slot(j4), g.b, act=AF.Silu)
            g.flush()
    import os
    STOP = os.environ.get("NSA_STOP", "")
    if STOP == "A":
        return phase_c(k, li, "nsa_wout", last)
    P.barrier()
    A.reset()
    kc = [A.alloc([256], BF16) for _ in range(4)]; kc_b = Buf()
    vc = [A.alloc([2, 64], BF16) for _ in range(4)]; vc_b = Buf()
    keep = A.off
    stg = [A.alloc([8, 256], F32) for _ in range(2)]
    stg_b = [Buf() for _ in range(2)]
    k.stg_n = 0
    w1 = [A.alloc([16, 128], BF16) for _ in range(2)]; w1_b = [Buf(), Buf()]
    w2f = A.alloc([2, 64], F32); w2f_b = Buf()
    w2 = A.alloc([2, 64], BF16); w2_b = Buf()
    pef = A.alloc([16], F32); pef_b = Buf()
    peb = A.alloc([16], BF16); peb_b = Buf()
    b1 = A.alloc([2], F32); b1_b = Buf()
    x2 = [A.alloc([S], BF16) for _ in range(2)]; x2_b = [Buf(), Buf()]
    hid = [A.alloc([256], BF16) for _ in range(2)]; hid_b = [Buf(), Buf()]
    for wi, nm in enumerate(("nsa_wk1", "nsa_wv1")):
        for half in range(2):
            b = k.stg_n % 2
            k.stg_n += 1
            P.dma("sp", stg[b][:, :, 0:128], din[nm][half * 1024:(half + 1) * 1024, :].rearrange("(m p) j -> p m j", p=128), W=[stg_b[b]])
            P.op("dve", lambda e, b=b, wi=wi, half=half: e.tensor_copy(out=w1[wi][:, half * 8:(half + 1) * 8, :], in_=stg[b][:, :, 0:128]),
                 R=[stg_b[b]], WP=[w1_b[wi]])
    P.dma("sp", w2f[:, 0, :], din["nsa_wk2"], WP=[w2f_b])
    P.dma("sp", w2f[:, 1, :], din["nsa_wv2"], WP=[w2f_b])
    P.op("dve", lambda e: e.tensor_copy(out=w2, in_=w2f), R=[w2f_b], W=[w2_b])
    P.dma("sp", pef, din["nsa_pe"], W=[pef_b])
    P.op("dve", lambda e: e.tensor_copy(out=peb, in_=pef), R=[pef_b], W=[peb_b])
    for wi in range(2):
        P.op("pool", lambda e, wi=wi: e.memset(hid[wi][:, 255:256], 0.0), WP=[hid_b[wi]])
        pm, pm_b = k.ps[wi], k.ps_b[wi]
        for m in range(16):
            P.op("pe", lambda e, wi=wi, m=m, pm=pm: e.matmul(pm[:, 0:1], lhsT=w1[wi][:, m, :], rhs=peb[:, m:m + 1], start=(m == 0), stop=(m == 15)),
                 R=[w1_b[wi], peb_b], W=[pm_b], inc=(m == 15))
        P.op("act", lambda e, wi=wi, pm=pm: e.activation(out=b1[:, wi:wi + 1], in_=pm[:, 0:1], func=AF.Copy), R=[pm_b], WP=[b1_b])
    n = 0
    for kv in range(4):
        for wi in range(2):
            xb = n % 2
            n += 1
            row0 = (512 if wi == 0 else 768) + kv * 64
            P.dma("sp", x2[xb][0:64, :], k.KT[row0:row0 + 64, :], R=[k.KT_b], WP=[x2_b[xb]])
            P.dma("sp", x2[xb][64:128, 0:S - 1], k.KT[row0:row0 + 64, 1:S], R=[k.KT_b], WP=[x2_b[xb]])
            x2v = x2[xb].rearrange("p (i l) -> p i l", l=16)
            pm, pm_b = k.ps[2 + xb], k.ps_b[2 + xb]
            for m in range(16):
                l2 = 2 * m
                rhs = x2v[:, 0:255, l2] if l2 < 16 else x2v[:, 1:256, l2 - 16]
                P.op("pe", lambda e, wi=wi, m=m, pm=pm, rhs=rhs: e.matmul(pm[:, 0:255], lhsT=w1[wi][:, m, :], rhs=rhs, start=(m == 0), stop=(m == 15)),
                     R=[w1_b[wi], x2_b[xb]], W=[pm_b], inc=(m == 15))
            P.op("act", lambda e, wi=wi, pm=pm: e.activation(out=hid[wi][:, 0:255], in_=pm[:, 0:255], func=AF.Silu, bias=b1[:, wi:wi + 1]),
                 R=[pm_b, b1_b], WP=[hid_b[wi]])
            if wi == 0:
                pm2, pm2_b = k.ps[4], k.ps_b[4]
                P.op("pe", lambda e, pm2=pm2: e.matmul(pm2[0:64, 0:256], lhsT=w2[:, 0, :], rhs=hid[0][:, :], start=True, stop=True),
                     R=[w2_b, hid_b[0]], W=[pm2_b])
                P.op("dve", lambda e, kv=kv, pm2=pm2: e.tensor_copy(out=kc[kv][0:64, :], in_=pm2[0:64, 0:256]), R=[pm2_b], WP=[kc_b])
            else:
                pm2, pm2_b = k.ps[5], k.ps_b[5]
                for nt in range(2):
                    P.op("pe", lambda e, nt=nt, pm2=pm2: e.matmul(pm2[:, nt * 64:(nt + 1) * 64], lhsT=hid[1][:, nt * 128:(nt + 1) * 128], rhs=w2[:, 1, :], start=True, stop=True),
                         R=[w2_b, hid_b[1]], W=[pm2_b], inc=(nt == 1))
                P.op("dve", lambda e, kv=kv, pm2=pm2: e.tensor_copy(out=vc[kv], in_=pm2[:, 0:128].rearrange("p (a b) -> p a b", a=2)), R=[pm2_b], WP=[vc_b])
    if STOP == "B0":
        return phase_c(k, li, "nsa_wout", last)
    P.barrier()
    A.reset(keep)
    bf = BF16
    cmask = A.alloc([4608], bf); cmask_b = Buf()
    ovl = A.alloc([2, 64], bf); ovl_b = Buf()
    sel3 = A.alloc([48, 64], bf); sel3_b = Buf()
    cmul = A.alloc([24, 64], bf); cadd = A.alloc([24, 64], bf); ctab_b = Buf()
    gsig = A.alloc([S], bf); gsig_b = Buf()
    P.dma("sp", cmask, din["c_cmask"], W=[cmask_b])
    P.dma("sp", ovl, din["c_ovl"], W=[ovl_b])
    P.dma("sp", sel3[0:48], din["c_sel3"][0:48], W=[sel3_b])
    P.dma("sp", cmul, din["c_cmul"][:, 8:32, :], WP=[ctab_b])
    P.dma("sp", cadd, din["c_cadd"][:, 8:32, :], WP=[ctab_b])
    P.dma("sp", gsig[0:48, :], k.GT[1024:1072, :], R=[k.GT_b], W=[gsig_b])
    QS = [A.alloc([S], bf) for _ in range(4)]
    QS_b = [Buf() for _ in range(4)]
    KE = A.alloc([S], bf); KE_b = Buf()
    kwin = A.alloc([S], bf); kwin_b = Buf()
    vs = A.alloc([NT, 128], bf); vs_b = Buf()
    vw = A.alloc([NT, 128], bf); vw_b = Buf()
    P.op("pool", lambda e: e.memset(vs[:, :, 64:128], 1.0), WP=[vs_b])
    P.op("pool", lambda e: e.memset(vw[:, :, 64:128], 1.0), WP=[vw_b])
    dsb_b = [Buf(), Buf()]
    impT = A.alloc([S], F32); impT_b = Buf()
    G = [A.alloc([512], bf) for _ in range(2)]; G_b = [Buf(), Buf()]
    pt_bufs = [(A.alloc([512], bf), Buf()) for _ in range(3)]
    rd = [A.alloc([512], F32) for _ in range(2)]; rd_b = [Buf() for _ in range(2)]
    tmp = [A.alloc([512], F32) for _ in range(2)]; tmp_b = [Buf() for _ in range(2)]
    acc = [A.alloc([512], F32) for _ in range(2)]; acc_b = [Buf() for _ in range(2)]
    ocl = [A.alloc([512], F32) for _ in range(2)]; ocl_b = [Buf() for _ in range(2)]
    ocs, ocs_b = ocl, ocl_b
    ost = [A.alloc([512], bf) for _ in range(2)]; ost_b = [Buf() for _ in range(2)]
    NW = 2
    impm_l = [A.alloc([64], F32) for _ in range(NW)]; impm_bl = [Buf() for _ in range(NW)]
    imp2_l = [A.alloc([64], F32) for _ in range(NW)]; imp2_bl = [Buf() for _ in range(NW)]
    gq = [A.alloc([512], bf) for _ in range(6)]; gq_b = [Buf() for _ in range(6)]
    m1 = A.alloc([8], F32); m1_b = Buf()
    m2 = A.alloc([8], F32); m2_b = Buf()
    selb_l = [A.alloc([128], F32) for _ in range(NW)]; selb_bl = [Buf() for _ in range(NW)]
    cmp3_l = [A.alloc([4096], BF16) for _ in range(NW)]; cmp3_bl = [Buf() for _ in range(NW)]
    dlo = [cmp3_l[i_].bitcast(F32)[:, 0:512] for i_ in range(2)]; dlo_b = cmp3_bl
    for w_ in range(NW):
        P.op("pool", lambda e, w_=w_: e.memset(selb_l[w_], 0.0), W=[selb_bl[w_]])
    P.dma("sp", KE[64:128, :], din["c_ebig"][64:128, :], WP=[KE_b])
    for g in range(4):
        P.dma("sp", QS[g][64:128, 0:1024], din["c_selst"][64:128, :], WP=[QS_b[g]])
    cn = [0]
    for kv in range(4):
        for g in range(4):
            hh = kv * 4 + g
            P.dma("sp", QS[g][0:64, :], k.QT[hh * 64:(hh + 1) * 64, :], R=[k.QT_b], WP=[QS_b[g]])
        P.dma("sp", KE[0:64, :], k.KT[kv * 64:(kv + 1) * 64, :], R=[k.KT_b], WP=[KE_b])
        P.dma("sp", kwin[0:64, :], k.KT[256 + kv * 64:256 + (kv + 1) * 64, :], R=[k.KT_b], W=[kwin_b])
        P.dma("sp", vs[:, :, 0:64], k.VS[kv], R=[k.VS_b], WP=[vs_b])
        P.dma("sp", vw[:, :, 0:64], k.VS[4 + kv], R=[k.VS_b], WP=[vw_b])
        rows = []
        for g in range(4):
            hh = kv * 4 + g
            for qi in range(NST):
                qs = slice(qi * 512, (qi + 1) * 512)
                ksteps = []
                for nt in range(2):
                    if nt == 1 and qi < 4:
                        continue
                    s_list = [(kc[kv][0:64, nt * 128:(nt + 1) * 128], QS[g][0:64, qs], [kc_b, QS_b[g]])]
                    if not (nt == 0 and qi >= 5):
                        cm0 = qi * 512 if nt == 0 else 2560 + (qi - 4) * 512
                        s_list.append((k.identb[:], cmask[:, cm0:cm0 + 512], [k.identb_b, cmask_b]))
                    pv = [("O", vc[kv][:, nt, :], [vc_b], 64), ("D", k.onesb[:, 0:64], [k.onesb_b], 64), ("A", ovl[:, nt, :], [ovl_b], 64)]
                    ksteps.append(dict(s=s_list, pv=pv))

                def epilogue(O, O_b, Dn, Dn_b, hh=hh, g=g, qs=qs):
                    i = cn[0] % 2
                    cn[0] += 1
                    Aa, Aa_b = k.ps[7], k.ps_b[7]
                    P.op("act", lambda e: e.activation(out=rd[i][0:64, :], in_=Dn[0:64, :], func=AF.Ln, bias=1e-18), R=[Dn_b], W=[rd_b[i]])
                    P.op("act", lambda e: e.activation(out=rd[i][0:64, :], in_=rd[i][0:64, :], func=AF.Exp, scale=-1.0), R=[rd_b[i]], W=[rd_b[i]])
                    P.op("dve", lambda e: e.tensor_tensor(out=ocs[i][0:64, :], in0=O[0:64, :], in1=rd[i][0:64, :], op=ALU.mult), R=[O_b, rd_b[i]], W=[ocs_b[i]])
                    P.dma("pool", k.OC[hh * 64:(hh + 1) * 64, qs], ocs[i][0:64, :], R=[ocs_b[i]], WP=[k.OC_b])
                    if g == 0:
                        P.op("dve", lambda e: e.tensor_tensor(out=impT[0:64, qs], in0=Aa[0:64, :], in1=rd[i][0:64, :], op=ALU.mult), R=[Aa_b, rd_b[i]], WP=[impT_b])
                    else:
                        P.op("dve", lambda e: e.tensor_tensor(out=tmp[i][0:64, :], in0=Aa[0:64, :], in1=rd[i][0:64, :], op=ALU.mult), R=[Aa_b, rd_b[i]], W=[tmp_b[i]])
                        P.op("pool", lambda e: e.tensor_tensor(out=impT[0:64, qs], in0=impT[0:64, qs], in1=tmp[i][0:64, :], op=ALU.add), R=[tmp_b[i]], WP=[impT_b])
                rows.append(dict(ksteps=ksteps, epilogue=epilogue))
        attention_rows(k, rows, pt_bufs, 0.125)
        if STOP == "B1":
            return phase_c(k, li, "nsa_wout", last)
        for w_ in range(NW):
            P.op("pool", lambda e, w_=w_: e.memset(selb_l[w_][:, 64:128], -30000.0), WP=[selb_bl[w_]])

        def sel_stages(t, w_):
            impm, impm_b, imp2, imp2_b = impm_l[w_], impm_bl[w_], imp2_l[w_], imp2_bl[w_]
            selb, selb_b, cmp3, cmp3_b = selb_l[w_], selb_bl[w_], cmp3_l[w_], cmp3_bl[w_]
            pT, pT_b = k.ps[w_], k.ps_b[w_]
            p2, p2_b = k.ps[2 + w_], k.ps_b[2 + w_]
            tsl = slice(t * 128, (t + 1) * 128)
            ns = 2 * t + 2
            in0 = impm[:, 0:ns].unsqueeze(1).to_broadcast([128, ns, ns])
            in1 = impm[:, 0:ns].unsqueeze(2).to_broadcast([128, ns, ns])
            c3 = cmp3[:, 0:ns * ns].rearrange("p (a b) -> p a b", a=ns)
            st_ = []
            st_.append(lambda: P.op("pe", lambda e: e.transpose(out=pT[:, 0:64], in_=impT[0:64, tsl], identity=k.ident[0:64, 0:64]),
                                    R=[impT_b, k.ident_b], W=[pT_b]))
            st_.append(lambda: P.op("dve", lambda e: e.tensor_tensor(out=impm, in0=pT[:, 0:64], in1=cmul[:, t - 8, :], op=ALU.mult), R=[pT_b, ctab_b], W=[impm_b]))
            st_.append(lambda: P.op("dve", lambda e: e.tensor_tensor(out=impm, in0=impm, in1=cadd[:, t - 8, :], op=ALU.add), R=[ctab_b, impm_b], W=[impm_b]))
            st_.append(lambda: P.op("dve", lambda e: e.tensor_tensor(out=c3, in0=in0, in1=in1, op=ALU.is_gt), R=[impm_b], W=[cmp3_b]))
            st_.append(lambda: P.op("dve", lambda e: e.tensor_reduce(out=imp2[:, 0:ns], in_=c3, axis=mybir.AxisListType.X, op=ALU.add), R=[cmp3_b], W=[imp2_b]))
            st_.append(lambda: P.op("dve", lambda e: e.tensor_scalar(out=selb[:, 64:64 + ns], in0=imp2[:, 0:ns], scalar1=15.5, scalar2=30000.0, op0=ALU.is_lt, op1=ALU.mult),
                                    R=[imp2_b], WP=[selb_b]))
            st_.append(lambda: P.op("dve", lambda e: e.tensor_scalar(out=selb[:, 64:64 + ns], in0=selb[:, 64:64 + ns], scalar1=-30000.0, scalar2=None, op0=ALU.add),
                                    R=[selb_b], WP=[selb_b]))
            st_.append(lambda: P.op("pe", lambda e: e.transpose(out=p2[:, 0:128], in_=selb, identity=k.ident[:]), R=[selb_b, k.ident_b], W=[p2_b]))
            for g in range(4):
                st_.append(lambda g=g: P.op("act", lambda e: e.activation(out=QS[g][64:128, tsl], in_=p2[64:128, 0:128], func=AF.Copy), R=[p2_b], WP=[QS_b[g]]))
            return st_

        for t0 in range(8, NT, NW):
            chains = [sel_stages(t0 + w_, w_) for w_ in range(NW)]
            for si_ in range(len(chains[0])):
                for ch in chains:
                    ch[si_]()
        if STOP == "SEL":
            return phase_c(k, li, "nsa_wout", last)
        for g in range(4):
            hh = kv * 4 + g
            rows = []
            for qi in range(NST):
                qs = slice(qi * 512, (qi + 1) * 512)
                ksteps = []
                for ki in range(4 * qi + 4):
                    ks = slice(ki * 128, (ki + 1) * 128)
                    s_list = [(KE[:, ks], QS[g][:, qs], [KE_b, QS_b[g]])]
                    kk = ki - 4 * qi
                    if kk >= 0:
                        s_list.append((k.identb[:], k.masks[:, MASK_CAUSAL[kk], :], [k.identb_b, k.masks_b]))
                    pv = [("O", vs[:, ki, :], [vs_b], 128)]
                    ksteps.append(dict(s=s_list, pv=pv, cols=band_cols(max(kk, 0), None)))

                def gate_pre(r, gi, qs=qs):
                    def f():
                        gb, gb_b = k.ps[7], k.ps_b[7]
                        P.op("pe", lambda e: e.matmul(gb[0:64, :], lhsT=sel3[0:48, r, :], rhs=gsig[0:48, qs], start=True, stop=True), R=[sel3_b, gsig_b], W=[gb_b])
                        P.op("act", lambda e: e.activation(out=gq[gi][0:64, :], in_=gb[0:64, :], func=AF.Copy), R=[gb_b], W=[gq_b[gi]])
                    return f

                def den_shift(O, O_b, i):
                    Dsb = rd[i][64:128, :]
                    P.op("act", lambda e: e.activation(out=Dsb, in_=O[64:128, :], func=AF.Copy), R=[O_b], W=[dsb_b[i]])
                    P.dma("sp", dlo[i][0:64, :], Dsb, R=[dsb_b[i]], W=[dlo_b[i]])

                    def f():
                        P.op("dve", lambda e: e.reciprocal(out=rd[i][0:64, :], in_=dlo[i][0:64, :]), R=[dlo_b[i]], W=[rd_b[i]])
                    return f

                def ep_sel(O, O_b, Dn, Dn_b, hh=hh, qs=qs, qi=qi):
                    i = qi % 2
                    f0 = den_shift(O, O_b, i)

                    def rest():
                        f0()
                        P.op("dve", lambda e: e.tensor_tensor(out=tmp[i][0:64, :], in0=O[0:64, :], in1=rd[i][0:64, :], op=ALU.mult), R=[O_b, rd_b[i]], W=[tmp_b[i]])
                        P.op("dve", lambda e: e.tensor_tensor(out=acc[i][0:64, :], in0=tmp[i][0:64, :], in1=gq[i][0:64, :], op=ALU.mult), R=[gq_b[i], tmp_b[i]], W=[acc_b[i]])
                    return [rest]
                rows.append(dict(ksteps=ksteps, o_banks=4, epilogue=ep_sel, pre=[gate_pre(hh * 3 + 1, qi % 2)]))
                ksteps = []
                for kk in range(-4, 4):
                    ki = 4 * qi + kk
                    if ki < 0:
                        continue
                    ks = slice(ki * 128, (ki + 1) * 128)
                    s_list = [(kwin[0:64, ks], QS[g][0:64, qs], [kwin_b, QS_b[g]]),
                              (k.identb[:], k.masks[:, MASK_WIN[kk], :], [k.identb_b, k.masks_b])]
                    pv = [("O", vw[:, ki, :], [vw_b], 128)]
                    ksteps.append(dict(s=s_list, pv=pv, cols=band_cols(kk, 512)))

                def ep_win(O, O_b, Dn, Dn_b, hh=hh, qs=qs, qi=qi):
                    i = qi % 2
                    Gh, Gh_b = G[i], G_b[i]
                    P.dma("sp", ocl[i][0:64, :], k.OC[hh * 64:(hh + 1) * 64, qs], R=[k.OC_b], W=[ocl_b[i]])
                    P.dma("sp", Gh[0:64, :], k.GT[hh * 64:(hh + 1) * 64, qs], R=[k.GT_b], W=[Gh_b])
                    f0 = den_shift(O, O_b, i)

                    def rest():
                        f0()
                        P.op("dve", lambda e: e.tensor_tensor(out=tmp[i][0:64, :], in0=O[0:64, :], in1=rd[i][0:64, :], op=ALU.mult), R=[O_b, rd_b[i]], W=[tmp_b[i]])
                        P.op("dve", lambda e: e.tensor_tensor(out=tmp[i][0:64, :], in0=tmp[i][0:64, :], in1=gq[2 + i][0:64, :], op=ALU.mult), R=[gq_b[2 + i], tmp_b[i]], W=[tmp_b[i]])
                        P.op("pool", lambda e: e.tensor_tensor(out=acc[i][0:64, :], in0=acc[i][0:64, :], in1=tmp[i][0:64, :], op=ALU.add), R=[tmp_b[i], acc_b[i]], W=[acc_b[i]])
                        P.op("dve", lambda e: e.tensor_tensor(out=tmp[i][0:64, :], in0=ocl[i][0:64, :], in1=gq[4 + i][0:64, :], op=ALU.mult), R=[gq_b[4 + i], ocl_b[i]], W=[tmp_b[i]])
                        P.op("pool", lambda e: e.tensor_tensor(out=acc[i][0:64, :], in0=acc[i][0:64, :], in1=tmp[i][0:64, :], op=ALU.add), R=[tmp_b[i], acc_b[i]], W=[acc_b[i]])
                        P.op("pool", lambda e: e.tensor_tensor(out=ost[i][0:64, :], in0=acc[i][0:64, :], in1=Gh[0:64, :], op=ALU.mult), R=[acc_b[i], Gh_b], W=[ost_b[i]])
                        P.dma("pool", k.OT[hh * 64:(hh + 1) * 64, qs], ost[i][0:64, :], R=[ost_b[i]], WP=[k.OT_b])
                    return [rest]
                rows.append(dict(ksteps=ksteps, o_banks=4, epilogue=ep_win, pre=[gate_pre(hh * 3 + 2, 2 + qi % 2), gate_pre(hh * 3 + 0, 4 + qi % 2)]))
            attention_rows(k, rows, pt_bufs, 0.125)
    phase_c(k, li, "nsa_wout", last)


def _np_dt(a):
    if a.dtype == np.float32:
        return F32
    if a.dtype == np.int32:
        return I32
    return BF16


def make_in_maps(inputs, cores):
    w = _prep_weights({kk: np.asarray(v) for kk, v in inputs.items()})
    cst = _consts()
    shared = {}
    shared.update(w)
    shared.update(cst)
    x = np.asarray(inputs["x"], np.float32)
    c = np.asarray(inputs["c"], np.float32)
    pos = np.asarray(inputs["positions"], np.int32)
    maps = []
    for b in cores:
        m = dict(shared)
        m["x_in"] = np.ascontiguousarray(x[b])
        m["c_fm"] = _fm(c[b], 8)
        m["pos"] = np.ascontiguousarray(pos[b].reshape(1, S))
        maps.append(m)
    return maps


def kernel(**inputs):
    maps = make_in_maps(inputs, list(range(8)))
    shapes = {kk: (v.shape, _np_dt(v)) for kk, v in maps[0].items()}
    nc, P = build(DEPTH, shapes)
    res = run_bass_kernel_spmd(nc, maps, core_ids=list(range(8)))
    out = np.stack([np.asarray(r["out"], np.float32) for r in res.results], axis=0)
    return out
```
